# Optimizing a Trainium2 kernel written in Bass

```python
import jax, jax.numpy as jnp
from jax import lax
import numpy as np

D_MODEL = 1024
BATCH = 8
SEQ = 4096
DEPTH = 4

CHUNK = 64
N_META = 16
EPS = 1e-6
GLA_HEADS = 4
GLA_DV = D_MODEL // 2 // GLA_HEADS
GLA_DK = GLA_DV // 2
GLA_LOWRANK = 16
GLA_TAU = 16.0
FOX_DH = 64
FOX_HEADS = D_MODEL // 2 // FOX_DH
Q_BLOCK = 128
FORGET_BIAS_INIT = 4.0
D_FF = 128 * (-(-8 * D_MODEL // (3 * 128)))
CONV_W = 3
GLA_WIDTH = GLA_HEADS * GLA_DV
FOX_WIDTH = FOX_HEADS * FOX_DH
MIX_WIDTH = GLA_WIDTH + FOX_WIDTH
IN_SIZES = (GLA_HEADS * GLA_DK, GLA_HEADS * GLA_DK, GLA_WIDTH, GLA_WIDTH, GLA_LOWRANK,
            FOX_WIDTH, FOX_WIDTH, FOX_WIDTH, FOX_HEADS)
N_IN = sum(IN_SIZES)

kernel_name = "hymba_style_gla_fox_convffn_streaming"


def rms_norm(x, g):
    xf = x.astype(jnp.float32)
    y = xf * lax.rsqrt(jnp.mean(xf * xf, axis=-1, keepdims=True) + EPS)
    return (y * g.astype(jnp.float32)).astype(x.dtype)


def chunk_kv_summary(k, v, log_a):
    log_b = jnp.cumsum(log_a, axis=3)
    log_end = log_b[:, :, :, -1:, :]
    u = jnp.einsum('bhntk,bhntv->bhnkv', k * jnp.exp(log_end - log_b), v)
    return u, jnp.exp(log_end[:, :, :, 0, :])


def gla_chunk_causal(q, k, v, log_a):
    B, L, H, _ = q.shape
    dv = v.shape[-1]
    n_chunks = (L - N_META) // CHUNK

    def split(t):
        t = jnp.transpose(t.astype(jnp.float32), (0, 2, 1, 3))
        d = t.shape[-1]
        return (t[:, :, :N_META].reshape(B, H, 1, N_META, d),
                t[:, :, N_META:].reshape(B, H, n_chunks, CHUNK, d))

    q_m, q_r = split(q)
    k_m, k_r = split(k)
    v_m, v_r = split(v)
    a_m, a_r = split(log_a)
    u_m, _ = chunk_kv_summary(k_m, v_m, a_m)
    s_meta = u_m[:, :, 0]
    u_r, decay_r = chunk_kv_summary(k_r, v_r, a_r)

    def step(s, inp):
        u_c, d_c = inp
        s = d_c[..., None] * s + u_c
        return s, s

    _, states = lax.scan(step, s_meta, (jnp.moveaxis(u_r, 2, 0), jnp.moveaxis(decay_r, 2, 0)))
    states = jnp.moveaxis(states, 0, 2)
    o_m = jnp.einsum('bhtk,bhkv->bhtv', q_m[:, :, 0], s_meta)
    o_r = jnp.einsum('bhntk,bhnkv->bhntv', q_r, states).reshape(B, H, n_chunks * CHUNK, dv)
    o = jnp.concatenate([o_m, o_r], axis=2)
    return jnp.transpose(o, (0, 2, 1, 3))


def forgetting_attention(q, k, v, log_f):
    B, L, H, dh = q.shape
    n_blk = -(-L // Q_BLOCK)
    pad = n_blk * Q_BLOCK - L
    c = jnp.cumsum(log_f.astype(jnp.float32), axis=1)
    c_k = jnp.transpose(c, (0, 2, 1))
    c_q = jnp.pad(c, ((0, 0), (0, pad), (0, 0)), mode='edge')
    q_pad = jnp.pad(q.astype(jnp.float32) * dh ** -0.5, ((0, 0), (0, pad), (0, 0), (0, 0)))
    q_blocks = jnp.moveaxis(q_pad.reshape(B, n_blk, Q_BLOCK, H, dh), 1, 0)
    c_blocks = jnp.moveaxis(jnp.transpose(c_q.reshape(B, n_blk, Q_BLOCK, H), (0, 1, 3, 2)), 1, 0)
    kf = k.astype(jnp.float32)
    vf = v.astype(jnp.float32)
    key_pos = jnp.arange(L)

    def block(inp):
        q_b, c_b, start = inp
        logits = (jnp.einsum('bqhd,bkhd->bhqk', q_b, kf)
                  + c_b[..., None] - c_k[:, :, None, :])
        q_pos = start + jnp.arange(Q_BLOCK)
        logits = jnp.where(key_pos[None, :] <= q_pos[:, None], logits, -jnp.inf)
        return jnp.einsum('bhqk,bkhd->bqhd', jax.nn.softmax(logits, axis=-1), vf)

    out = lax.map(block, (q_blocks, c_blocks, jnp.arange(n_blk) * Q_BLOCK))
    out = jnp.moveaxis(out, 0, 1).reshape(B, n_blk * Q_BLOCK, H, dh)
    return out[:, :L]


def conv_ffn(x, w_up, conv_w, conv_b, w_down):
    h = x @ w_up
    L = h.shape[1]
    hp = jnp.pad(h, ((0, 0), (CONV_W - 1, 0), (0, 0)))
    hc = conv_b
    for j in range(CONV_W):
        hc = hc + hp[:, j:j + L] * conv_w[j]
    u, g = jnp.split(hc, 2, axis=-1)
    return (jax.nn.silu(g) * u) @ w_down


def setup_inputs(seed: int = 0) -> dict:
    key = jax.random.key(seed)
    ks = jax.random.split(key, 16)

    def nrm(k, shape, scale):
        return jax.random.normal(k, shape, jnp.float32) * scale

    return {
        'x': nrm(ks[0], (BATCH, SEQ, D_MODEL), 1.0),
        'meta_tokens': nrm(ks[1], (N_META, D_MODEL), 1.0),
        'attn_norm': 1.0 + nrm(ks[2], (DEPTH, D_MODEL), 0.02),
        'w_in': nrm(ks[3], (DEPTH, D_MODEL, N_IN), D_MODEL ** -0.5),
        'w_alpha_up': nrm(ks[4], (DEPTH, GLA_LOWRANK, GLA_HEADS * GLA_DK), GLA_LOWRANK ** -0.5),
        'b_alpha': nrm(ks[5], (DEPTH, GLA_HEADS * GLA_DK), 0.1),
        'b_forget': FORGET_BIAS_INIT + nrm(ks[6], (DEPTH, FOX_HEADS), 0.5),
        'gla_norm': 1.0 + nrm(ks[7], (DEPTH, GLA_WIDTH), 0.02),
        'fox_norm': 1.0 + nrm(ks[8], (DEPTH, FOX_WIDTH), 0.02),
        'w_out': nrm(ks[9], (DEPTH, MIX_WIDTH, D_MODEL), MIX_WIDTH ** -0.5),
        'ffn_norm': 1.0 + nrm(ks[10], (DEPTH, D_MODEL), 0.02),
        'w_up': nrm(ks[11], (DEPTH, D_MODEL, 2 * D_FF), D_MODEL ** -0.5),
        'conv_w': nrm(ks[12], (DEPTH, CONV_W, 2 * D_FF), CONV_W ** -0.5),
        'conv_b': nrm(ks[13], (DEPTH, 2 * D_FF), 0.02),
        'w_down': nrm(ks[14], (DEPTH, D_FF, D_MODEL), D_FF ** -0.5),
        'final_norm': 1.0 + nrm(ks[15], (D_MODEL,), 0.02),
    }


def reference(x, meta_tokens, attn_norm, w_in, w_alpha_up, b_alpha, b_forget, gla_norm,
              fox_norm, w_out, ffn_norm, w_up, conv_w, conv_b, w_down, final_norm):
    B = x.shape[0]
    meta = jnp.broadcast_to(meta_tokens[None].astype(x.dtype), (B, N_META, D_MODEL))
    h = jnp.concatenate([meta, x], axis=1)
    L = h.shape[1]
    cuts = np.cumsum(IN_SIZES)[:-1].tolist()
    for i in range(DEPTH):
        xn = rms_norm(h, attn_norm[i])
        gq, gk, gv, gr, glr, fq, fk, fv, ff = jnp.split(xn @ w_in[i], cuts, axis=-1)
        log_a = jax.nn.log_sigmoid((glr @ w_alpha_up[i] + b_alpha[i]).astype(jnp.float32)) / GLA_TAU
        o_gla = gla_chunk_causal(gq.reshape(B, L, GLA_HEADS, GLA_DK) * GLA_DK ** -0.5,
                                 gk.reshape(B, L, GLA_HEADS, GLA_DK),
                                 gv.reshape(B, L, GLA_HEADS, GLA_DV),
                                 log_a.reshape(B, L, GLA_HEADS, GLA_DK)).astype(h.dtype)
        o_gla = rms_norm(o_gla, gla_norm[i].reshape(GLA_HEADS, GLA_DV)).reshape(B, L, GLA_WIDTH)
        o_gla = o_gla * jax.nn.silu(gr)
        log_f = jax.nn.log_sigmoid((ff + b_forget[i]).astype(jnp.float32))
        o_fox = forgetting_attention(fq.reshape(B, L, FOX_HEADS, FOX_DH),
                                     fk.reshape(B, L, FOX_HEADS, FOX_DH),
                                     fv.reshape(B, L, FOX_HEADS, FOX_DH), log_f).astype(h.dtype)
        o_fox = rms_norm(o_fox, fox_norm[i].reshape(FOX_HEADS, FOX_DH)).reshape(B, L, FOX_WIDTH)
        h = h + jnp.concatenate([o_gla, o_fox], axis=-1) @ w_out[i]
        h = h + conv_ffn(rms_norm(h, ffn_norm[i]), w_up[i], conv_w[i], conv_b[i], w_down[i])
    return rms_norm(h, final_norm)[:, N_META:]
```

```python
import contextlib
import numpy as np
import ml_dtypes
import concourse.bass as bass
import concourse.mybir as mybir
from concourse.bass_utils import run_bass_kernel_spmd

F32 = mybir.dt.float32
BF16 = mybir.dt.bfloat16
ALU = mybir.AluOpType
AF = mybir.ActivationFunctionType

D = 1024
SEQ = 4096
NMETA = 16
L = SEQ + NMETA
DEPTH = 4
NIN = 3096
DFF = 2816
NJ = DFF // 128
EPS = 1e-6
C_GQ, C_GK, C_GV, C_GR, C_LR, C_FQ, C_FK, C_FV, C_FF = 0, 256, 512, 1024, 1536, 1552, 2064, 2576, 3088

TILES = [(0, NMETA)] + [(NMETA + 512 * i, 512) for i in range(8)]
NT = len(TILES)
NKB = 33


def blocks_of(t0, T):
    return [(t0 + o, o, min(128, T - o)) for o in range(0, T, 128)]


def gblk(b0):
    return 0 if b0 == 0 else 1 + (b0 - NMETA) // 128


class Buf:
    __slots__ = ("name", "w", "r", "excl")

    def __init__(self, name, excl=False):
        self.name = name
        self.w = {}
        self.r = {}
        self.excl = excl


class _Eng:
    def __init__(self, name, h, sem):
        self.name, self.h, self.sem = name, h, sem
        self.count = 0
        self.seen = {}


class _Slot:
    def __init__(self, sem):
        self.sem = sem
        self.cnt = 0


class Tracker:
    def __init__(self, nc, es, n_work=24, n_wq=46):
        self.nc = nc
        self.e = {}
        for name, h in (("pe", nc.tensor), ("act", nc.scalar), ("dve", nc.vector),
                        ("pool", nc.gpsimd), ("sp", nc.sync)):
            self.e[name] = _Eng(name, h, es.enter_context(nc.semaphore("s_" + name)))
        self.work = [_Slot(es.enter_context(nc.semaphore(f"dw{i}"))) for i in range(n_work)]
        self.wq = [_Slot(es.enter_context(nc.semaphore(f"dq{i}"))) for i in range(n_wq)]
        self.pwork = [_Slot(es.enter_context(nc.semaphore(f"dp{i}"))) for i in range(10)]
        self.wi = 0
        self.qi = 0
        self.pi = 0
        import os, sys
        self.limit = int(os.environ.get("K_LIMIT", "0")) or None
        self.nops = 0
        self.log = []

    def _skip(self, engname):
        self.nops += 1
        if self.limit is not None:
            import sys
            self.log.append((self.nops, engname, sys._getframe(2).f_lineno))
            return self.nops > self.limit
        return False

    def _wait(self, eng, sem, val):
        k = id(sem)
        if eng.seen.get(k, 0) >= val:
            return
        eng.h.wait_ge(sem, val)
        eng.seen[k] = val

    def _deps(self, eng, reads, writes):
        need = {}

        def add(ev, raw):
            sem, val = ev
            if sem is eng.sem:
                if eng.name == "pe":
                    return
            k = id(sem)
            if k not in need or need[k][1] < val:
                need[k] = ev

        for b in reads:
            for ev in b.w.values():
                add(ev, True)
        for b in writes:
            for ev in b.w.values():
                add(ev, False)
            for ev in b.r.values():
                add(ev, False)
        for sem, val in need.values():
            self._wait(eng, sem, val)

    def op(self, engname, fn, reads=(), writes=(), partial=False):
        if self._skip(engname):
            return None
        eng = self.e[engname]
        xs = [b for b in reads if b.excl]
        if xs:
            writes = list(writes) + xs
        self._deps(eng, reads, writes)
        ins = fn(eng.h)
        eng.count += 1
        ins.then_inc(eng.sem, 1)
        ev = (eng.sem, eng.count)
        k = id(eng.sem)
        for b in reads:
            b.r[k] = ev
        for b in writes:
            if partial:
                b.w[k] = ev
            else:
                b.w = {k: ev}
                b.r = {}
        return ins

    def dma(self, qname, out, in_, reads=(), writes=(), weights=False):
        if self._skip("dma_" + qname):
            return None
        eng = self.e[qname]
        if weights:
            slot = self.wq[self.qi % len(self.wq)]
            self.qi += 1
        elif qname == "pool":
            slot = self.pwork[self.pi % len(self.pwork)]
            self.pi += 1
        else:
            slot = self.work[self.wi % len(self.work)]
            self.wi += 1
        self._deps(eng, reads, writes)
        if slot.cnt:
            self._wait(eng, slot.sem, slot.cnt)
        ins = eng.h.dma_start(out=out, in_=in_)
        slot.cnt += 16
        ins.then_inc(slot.sem, 16)
        ev = (slot.sem, slot.cnt)
        k = id(slot.sem)
        for b in reads:
            b.r[k] = ev
        for b in writes:
            b.w = {k: ev}
            b.r = {}

    def barrier(self):
        engs = list(self.e.values())
        for e in engs:
            for o in engs:
                if o is not e and o.count:
                    self._wait(e, o.sem, o.count)
            for s in self.work + self.pwork:
                if s.cnt:
                    self._wait(e, s.sem, s.cnt)

    def final_wait(self):
        sp = self.e["sp"]
        for s in self.work + self.wq + self.pwork:
            if s.cnt:
                self._wait(sp, s.sem, s.cnt)
        for o in self.e.values():
            if o is not sp and o.count:
                self._wait(sp, o.sem, o.count)


def build_nc(n_layers=DEPTH, dbg=False, stop_after=None):
    nc = bass.Bass("TRN2", target_bir_lowering=False)
    es = contextlib.ExitStack()
    with es:
        _build(nc, es, n_layers, dbg, stop_after)
    return nc


def _build(nc, es, n_layers, dbg, stop_after):
    def dram_in(name, shape, dt=F32):
        return nc.dram_tensor(name, list(shape), dt, kind="ExternalInput").ap()

    skind = "ExternalOutput" if dbg else "Internal"

    def dram_s(name, shape, dt):
        return nc.dram_tensor(name, list(shape), dt, kind=skind).ap()

    x_d = dram_in("x", [SEQ, D])
    meta_d = dram_in("meta", [NMETA, D])
    win_d = dram_in("w_in", [DEPTH, D, NIN])
    wout_d = dram_in("w_out", [DEPTH, D, D])
    wup_d = dram_in("w_up", [DEPTH, D, 2 * DFF])
    wdn_d = dram_in("w_down", [DEPTH, DFF, D])
    anorm_d = dram_in("anorm_bc", [DEPTH, 128, D])
    fnorm_d = dram_in("fnorm_bc", [DEPTH, 128, D])
    final_d = dram_in("final_bc", [128, D])
    wau_d = dram_in("wau", [DEPTH, 16, 256])
    balpha_d = dram_in("balpha_bc", [DEPTH, 128, 256])
    bforget_d = dram_in("bforget_bc", [DEPTH, 128, 8])
    gnorm_d = dram_in("gnorm_fm", [DEPTH, 128, 4])
    xnorm_d = dram_in("xnorm_fm", [DEPTH, 128, 4])
    convw_d = dram_in("convw_fm", [DEPTH, 128, 2 * NJ, 3])
    convb_d = dram_in("convb_fm", [DEPTH, 128, 2 * NJ])
    cf32_d = dram_in("cf32", [128, 4 * 128 + 8])
    cbf_d = dram_in("cbf", [128, 5 * 128], BF16)
    y_d = nc.dram_tensor("y", [SEQ, D], F32, kind="ExternalOutput").ap()

    H_d = dram_s("H", [L, D], F32)
    QT_d = dram_s("QT", [4, 128, L], BF16)
    KT_d = dram_s("KT", [4, 128, L], BF16)
    V_d = dram_s("V", [L, 768], BF16)
    CSB_d = dram_s("CSB", [128, NKB, 8], F32)
    CREF_d = dram_s("CREF", [128, NT, 8], F32)
    MIXT_d = dram_s("MIXT", [8, 128, L], BF16)

    tk = Tracker(nc, es)
    op, dma = tk.op, tk.dma

    uid = [0]

    def sb(name, shape, dt, stack=es):
        uid[0] += 1
        return stack.enter_context(nc.sbuf_tensor(f"sb{uid[0]}_{name}", list(shape), dt))

    R_up = sb("R_up", [128, 8 * 2 * DFF], BF16)
    B_Rup = Buf("R_up")
    win_v = R_up[:, 0:8 * NIN].rearrange("p (k n) -> p k n", k=8)
    wup_v = R_up[:, :].rearrange("p (k n) -> p k n", k=8)
    cbf = sb("cbf", [128, 5 * 128], BF16)
    B_const = Buf("const")
    IDENT = cbf[:, 0:128]
    TRIB = cbf[:, 128:256]
    ONES128 = cbf[:, 256:384]
    WA = cbf[:, 384:512]
    WB = cbf[:, 512:640]

    PT = es.enter_context(nc.psum_tensor("PT", [128, 1024], BF16))
    PS = [es.enter_context(nc.psum_tensor(f"PS{i}", [128, 512], F32)) for i in range(7)]
    B_PT = Buf("PT", True)
    B_PS = [Buf(f"PS{i}", True) for i in range(7)]

    dma("sp", cbf[:, :], cbf_d, writes=[B_const])

    B_H = [Buf(f"H{g}") for g in range(NKB)]
    B_QT = [Buf(f"QT{t}") for t in range(NT)]
    B_KT = [Buf(f"KT{t}") for t in range(NT)]
    B_V = [Buf(f"V{g}") for g in range(NKB)]
    B_CSB = Buf("CSB")
    B_CREF = Buf("CREF")
    B_MIXG = [Buf(f"MIXG{t}") for t in range(NT)]
    B_MIXF = [[Buf(f"MIXF{p}_{t}") for t in range(NT)] for p in range(4)]

    def h_rows(layer, b0, B, first_src):
        if layer == 0 and first_src:
            if b0 == 0:
                return meta_d[0:B, :]
            return x_d[b0 - NMETA:b0 - NMETA + B, :]
        return H_d[b0:b0 + B, :]

    def act(out, in_, func, bias=None, scale=None, accum_out=None):
        kw = {}
        if bias is not None:
            kw["bias"] = bias
        if scale is not None:
            kw["scale"] = scale
        if accum_out is not None:
            kw["accum_out"] = accum_out
        return lambda e: e.activation(out=out, in_=in_, func=func, **kw)

    def mm(out, lhsT, rhs, start=True, stop=True):
        return lambda e: e.matmul(out, lhsT=lhsT, rhs=rhs, start=start, stop=stop)

    def norm_transpose(ws, hb, B_hb, gbc, B_g, xnT, B_xnT, o, B):
        xn, B_xn, st, B_st = ws["xn"], ws["B_xn"], ws["st"], ws["B_st"]
        op("act", act(xn[0:B, :], hb[0:B, :], AF.Square, accum_out=st[0:B, 0:1]),
           reads=[B_hb], writes=[B_xn, B_st])
        op("act", act(st[0:B, 1:2], st[0:B, 0:1], AF.Ln, bias=EPS, scale=1.0 / D),
           reads=[B_st], writes=[B_st])
        op("act", act(st[0:B, 2:3], st[0:B, 1:2], AF.Exp, scale=-0.5),
           reads=[B_st], writes=[B_st])
        op("dve", lambda e: e.scalar_tensor_tensor(out=xn[0:B, :], in0=hb[0:B, :], scalar=st[0:B, 2:3],
                                                   in1=gbc[0:B, :], op0=ALU.mult, op1=ALU.mult),
           reads=[B_hb, B_st, B_g], writes=[B_xn])
        for k in range(8):
            op("pe", lambda e, k=k: e.transpose(out=PT[:, k * 128:k * 128 + B], in_=xn[0:B, k * 128:(k + 1) * 128],
                                                identity=IDENT[0:B, 0:B]),
               reads=[B_xn, B_const], writes=[B_PT])
        src = PT[:, :].rearrange("p (k t) -> p k t", k=8)[:, :, 0:B]
        op("act", lambda e: e.activation(out=xnT[:, :, o:o + B], in_=src, func=AF.Copy),
           reads=[B_PT], writes=[B_xnT], partial=True)

    for layer in range(n_layers):
        last = layer == n_layers - 1
        for k in range(8):
            dma("pool", win_v[:, k, :], win_d[layer, k * 128:(k + 1) * 128, :], writes=[B_Rup], weights=True)

        with contextlib.ExitStack() as pa, nc.named_scope(f"A{layer}"):
            def sba(name, shape, dt):
                return sb(name, shape, dt, pa)
            cf32 = sba("cf32", [128, 4 * 128 + 8], F32)
            B_cf = Buf("cf32")
            dma("sp", cf32[:, :], cf32_d, writes=[B_cf])
            TRI = cf32[:, 0:128]
            SU = cf32[:, 128:256]
            E127 = cf32[:, 256:384]
            E15 = cf32[:, 384:512]
            CH = cf32[:, 512:520]
            gbc = sba("gbc", [128, D], F32); B_gbc = Buf("gbc")
            balpha = sba("balpha", [128, 256], F32)
            bforget = sba("bforget", [128, 8], F32)
            wau32 = sba("wau32", [16, 256], F32)
            wau = sba("wau", [16, 256], BF16)
            gnorm = sba("gnorm", [128, 4], F32)
            B_pp = Buf("layer_params")
            dma("sp", gbc[:, :], anorm_d[layer], writes=[B_gbc])
            dma("sp", balpha[:, :], balpha_d[layer], writes=[B_pp])
            dma("sp", bforget[:, :], bforget_d[layer], writes=[B_pp])
            dma("sp", gnorm[:, :], gnorm_d[layer], writes=[B_pp])
            dma("sp", wau32[:, :], wau_d[layer], writes=[B_pp])
            op("dve", lambda e: e.tensor_copy(out=wau[:, :], in_=wau32[:, :]), reads=[B_pp], writes=[B_pp])

            hbs = [sba(f"hb{i}", [128, D], F32) for i in range(2)]
            B_hbs = [Buf(f"hb{i}") for i in range(2)]
            ws = dict(xn=sba("xn", [128, D], BF16), B_xn=Buf("xn"), st=sba("st", [128, 4], F32), B_st=Buf("st"))
            xnTs = [sba(f"xnT{i}", [128, 8, 512], BF16) for i in range(2)]
            B_xnTs = [Buf(f"xnT{i}") for i in range(2)]
            gqT = sba("gqT", [128, 2, 512], BF16); B_gqT = Buf("gqT")
            sgT = sba("sgT", [128, 4, 512], F32); B_sgT = Buf("sgT")
            glrT = sba("glrT", [16, 512], BF16); B_glrT = Buf("glrT")
            qT = sba("qT", [128, 4, 512], BF16); B_qT = Buf("qT")
            kT = sba("kT", [128, 4, 512], BF16); B_kT = Buf("kT")
            vts = [sba(f"vt{i}", [128, 768], BF16) for i in range(2)]
            B_vts = [Buf(f"vt{i}") for i in range(2)]
            gv = sba("gv", [128, 4, 512], BF16); B_gv = Buf("gv")
            kdec = sba("kdec", [128, 4, 256], BF16); B_kdec = Buf("kdec")
            zb = sba("zb", [128, 256], F32); B_zb = Buf("zb")
            lz = sba("lz", [128, 256], F32); B_lz = Buf("lz")
            wdec = sba("wdec", [128, 256], F32); B_wdec = Buf("wdec")
            f8 = sba("f8", [128, 16], F32); B_f8 = Buf("f8")
            cposs = [sba(f"cpos{i}", [128, 8], F32) for i in range(2)]
            B_cposs = [Buf(f"cpos{i}") for i in range(2)]
            crefs = sba("crefs", [128, NT, 8], F32); B_crefs = Buf("crefs")
            dec = sba("dec", [128, 2, 8], F32); B_dec = Buf("dec")
            S = sba("S", [128, 2, 128], F32); B_S = Buf("S")
            Sbs = [sba(f"Sb{i}", [128, 2, 128], BF16) for i in range(2)]
            B_Sbs = [Buf(f"Sb{i}") for i in range(2)]
            oraw = sba("oraw", [128, 4, 512], F32); B_oraw = Buf("oraw")
            sq = sba("sq", [128, 512], BF16); B_sq = Buf("sq")
            rs = sba("rs", [128, 512], F32); B_rs = Buf("rs")
            tmp = sba("tmp", [128, 512], F32); B_tmp = Buf("tmp")
            mixg = sba("mixg", [128, 4, 512], BF16); B_mixg = Buf("mixg")

            op("pool", lambda e: e.memset(S[:, :, :], 0.0), writes=[B_S])
            for i in range(2):
                op("pool", lambda e, i=i: e.memset(cposs[i][:, :], 0.0), writes=[B_cposs[i]])
            for i in range(2):
                op("pool", lambda e, i=i: e.memset(vts[i][:, :], 1.0), writes=[B_vts[i]])

            pa_i = [0]

            def next_pa():
                i = pa_i[0] % 2
                pa_i[0] += 1
                return PS[i], B_PS[i]
            PM1, B_PM1a, B_PM1b = PS[3], B_PS[3], B_PS[3]
            PM2 = PS[4]
            B_PM2w = B_PM2ff = B_PM2cum = B_PM2cref = B_PM2dec = B_PS[4]
            PU, B_PU = PS[5], [B_PS[5], B_PS[5]]
            PRs, B_PRs = [PS[6], PS[2]], [B_PS[6], B_PS[2]]
            PUv = PU[:, :].rearrange("p (b q c) -> p b q c", b=2, q=2)
            PRvs = [P_[:, :].rearrange("p (h c) -> p h c", h=8) for P_ in PRs]

            nblk_seen = 0
            chunk_g = 0
            for ti, (t0, T) in enumerate(TILES):
                blks = blocks_of(t0, T)
                C = 16 if ti == 0 else 64
                xnT, B_xnT = xnTs[ti % 2], B_xnTs[ti % 2]

                def norm_block(tj, bj):
                    tt0, TT = TILES[tj]
                    b0_, o_, B_ = blocks_of(tt0, TT)[bj]
                    g_ = gblk(b0_)
                    hb, B_hb = hbs[g_ % 2], B_hbs[g_ % 2]
                    dma("sp", hb[0:B_, :], h_rows(layer, b0_, B_, True), reads=[B_H[g_]], writes=[B_hb])
                    norm_transpose(ws, hb, B_hb, gbc, B_gbc, xnTs[tj % 2], B_xnTs[tj % 2], o_, B_)

                if ti == 0:
                    norm_block(0, 0)
                def proj_fm(col0, M, evac):
                    ps, B_ps = next_pa()
                    for k in range(8):
                        op("pe", mm(ps[0:M, 0:T], win_v[:, k, col0:col0 + M], xnT[:, k, 0:T], k == 0, k == 7),
                           reads=[B_Rup, B_xnT], writes=[B_ps])
                    evac(ps, B_ps)
                proj_fm(C_LR, 16, lambda ps, B_ps: op(
                    "dve", lambda e: e.tensor_copy(out=glrT[:, 0:T], in_=ps[0:16, 0:T]), reads=[B_ps], writes=[B_glrT]))
                for bi, (b0, o, B) in enumerate(blks):
                    g = gblk(b0)
                    nch = 1 if ti == 0 else 2

                    def proj_tm(ps_ap, B_ps, col0, N):
                        for k in range(8):
                            op("pe", mm(ps_ap, xnT[:, k, o:o + B], win_v[:, k, col0:col0 + N], k == 0, k == 7),
                               reads=[B_Rup, B_xnT], writes=[B_ps])
                    ps, B_ps = next_pa()
                    proj_tm(ps[0:B, 0:512], B_ps, C_FV, 512)
                    vt, B_vt = vts[g % 2], B_vts[g % 2]
                    vt4 = vt[:, :].rearrange("b (p s c) -> b p s c", p=4, s=3)
                    ps4 = ps[:, :].rearrange("b (p e c) -> b p e c", p=4, e=2)
                    for e_ in range(2):
                        op("dve", lambda e, e_=e_: e.tensor_copy(out=vt4[0:B, :, 2 * e_, :], in_=ps4[0:B, :, e_, :]),
                           reads=[B_ps], writes=[B_vt], partial=True)
                    dma("sp", V_d[b0:b0 + B, :], vt[0:B, :], reads=[B_vt], writes=[B_V[g]])
                    ps, B_ps = next_pa()
                    proj_tm(ps[0:B, 0:512], B_ps, C_GV, 512)
                    op("act", act(gv[0:B, bi, :], ps[0:B, 0:512], AF.Copy), reads=[B_ps], writes=[B_gv], partial=True)
                    proj_tm(PM1[0:B, 0:256], B_PM1a, C_GK, 256)
                    proj_tm(PM2[0:B, 256:264], B_PM2ff, C_FF, 8)
                    op("dve", lambda e: e.tensor_tensor(out=f8[0:B, 0:8], in0=PM2[0:B, 256:264], in1=bforget[0:B, :], op=ALU.add),
                       reads=[B_PM2ff, B_pp], writes=[B_f8])
                    op("act", act(f8[0:B, 8:16], f8[0:B, 0:8], AF.Exp, scale=-1.0), reads=[B_f8], writes=[B_f8])
                    op("act", act(f8[0:B, 0:8], f8[0:B, 8:16], AF.Ln, bias=1.0), reads=[B_f8], writes=[B_f8])
                    cur, B_cur = cposs[g % 2], B_cposs[g % 2]
                    prv, B_prv = cposs[(g + 1) % 2], B_cposs[(g + 1) % 2]
                    op("pe", mm(PM2[0:B, 264:272], TRI[0:B, 0:B], f8[0:B, 0:8], True, g == 0),
                       reads=[B_cf, B_f8], writes=[B_PM2cum])
                    if g > 0:
                        Bp = 16 if g == 1 else 128
                        Es = E15 if g == 1 else E127
                        op("pe", mm(PM2[0:B, 264:272], Es[0:Bp, 0:B], prv[0:Bp, :], False, True),
                           reads=[B_cf, B_prv], writes=[B_PM2cum])
                    op("dve", lambda e: e.tensor_copy(out=cur[0:B, :], in_=PM2[0:B, 264:272]), reads=[B_PM2cum], writes=[B_cur])
                    Bw = 128 if g == 0 else B
                    dma("sp", CSB_d[0:Bw, g, :], cur[0:Bw, :], reads=[B_cur], writes=[B_CSB])
                    if (ti == 0 and bi == 0) or (ti > 0 and bi == 1):
                        Es = E15 if ti == 0 else E127
                        op("pe", mm(PM2[:, 272:280], Es[0:B, :], cur[0:B, :]), reads=[B_cf, B_cur], writes=[B_PM2cref])
                        op("dve", lambda e: e.tensor_copy(out=crefs[:, ti, :], in_=PM2[:, 272:280]),
                           reads=[B_PM2cref], writes=[B_crefs], partial=True)
                    op("pe", mm(PM1[0:B, 256:512], glrT[0:16, o:o + B], wau[:, :]), reads=[B_glrT, B_pp], writes=[B_PM1b])
                    op("dve", lambda e: e.tensor_tensor(out=zb[0:B, :], in0=PM1[0:B, 256:512], in1=balpha[0:B, :], op=ALU.add),
                       reads=[B_PM1b, B_pp], writes=[B_zb])
                    op("act", act(zb[0:B, :], zb[0:B, :], AF.Exp, scale=-1.0), reads=[B_zb], writes=[B_zb])
                    op("act", act(lz[0:B, :], zb[0:B, :], AF.Ln, bias=1.0), reads=[B_zb], writes=[B_lz])
                    op("pe", mm(PM2[0:B, 0:256], SU[0:B, 0:B], lz[0:B, :]), reads=[B_cf, B_lz], writes=[B_PM2w])
                    op("act", act(wdec[0:B, :], PM2[0:B, 0:256], AF.Exp, scale=-1.0 / 16), reads=[B_PM2w], writes=[B_wdec])
                    op("dve", lambda e: e.tensor_tensor(out=kdec[0:B, bi, :], in0=PM1[0:B, 0:256], in1=wdec[0:B, :], op=ALU.mult),
                       reads=[B_PM1a, B_wdec], writes=[B_kdec], partial=True)
                    for p in range(2):
                        op("pe", mm(PM2[:, 280 + 8 * p:280 + 8 * p + 8], lz[0:B, p * 128:(p + 1) * 128], CH[0:B, 0:8]),
                           reads=[B_cf, B_lz], writes=[B_PM2dec])
                    PMd = PM2[:, 280:296].rearrange("p (q c) -> p q c", q=2)
                    op("act", act(dec[:, :, 2 * bi:2 * bi + nch], PMd[:, :, 0:nch], AF.Exp, scale=-1.0 / 16),
                       reads=[B_PM2dec], writes=[B_dec], partial=True)
                for j in range(2):
                    proj_fm(C_GQ + 128 * j, 128, lambda ps, B_ps, j=j: op(
                        "act", act(gqT[:, j, 0:T], ps[:, 0:T], AF.Copy, scale=0.125), reads=[B_ps], writes=[B_gqT], partial=True))
                pq = []
                for j in range(4):
                    pq.append(lambda j=j: proj_fm(C_GR + 128 * j, 128, lambda ps, B_ps, j=j: op(
                        "act", act(sgT[:, j, 0:T], ps[:, 0:T], AF.Silu), reads=[B_ps], writes=[B_sgT], partial=True)))
                for j in range(4):
                    pq.append(lambda j=j: proj_fm(C_FQ + 128 * j, 128, lambda ps, B_ps, j=j: op(
                        "dve", lambda e: e.tensor_copy(out=qT[:, j, 0:T], in_=ps[:, 0:T]), reads=[B_ps], writes=[B_qT], partial=True)))
                pq.append(lambda: dma("sp", QT_d[:, :, t0:t0 + T].rearrange("c p t -> p c t"), qT[:, :, 0:T], reads=[B_qT], writes=[B_QT[ti]]))
                for j in range(4):
                    pq.append(lambda j=j: proj_fm(C_FK + 128 * j, 128, lambda ps, B_ps, j=j: op(
                        "dve", lambda e: e.tensor_copy(out=kT[:, j, 0:T], in_=ps[:, 0:T]), reads=[B_ps], writes=[B_kT], partial=True)))
                pq.append(lambda: dma("sp", KT_d[:, :, t0:t0 + T].rearrange("c p t -> p c t"), kT[:, :, 0:T], reads=[B_kT], writes=[B_KT[ti]]))
                nq = []
                if ti + 1 < NT:
                    for bj in range(len(blocks_of(*TILES[ti + 1]))):
                        nq.append(lambda bj=bj: norm_block(ti + 1, bj))
                nchunks = T // C
                for c in range(nchunks):
                    bi = (c * C) // 128
                    r0 = (c * C) % 128
                    cb = 0
                    for h in range(4):
                        p, e_ = h // 2, h % 2
                        op("pe", mm(PUv[e_ * 64:(e_ + 1) * 64, cb, p, :], kdec[r0:r0 + C, bi, h * 64:(h + 1) * 64],
                                    gv[r0:r0 + C, bi, h * 128:(h + 1) * 128]),
                           reads=[B_kdec, B_gv], writes=[B_PU[cb]])
                    for p in range(2):
                        op("dve", lambda e, p=p: e.scalar_tensor_tensor(out=S[:, p, :], in0=S[:, p, :], scalar=dec[:, p, c:c + 1],
                                                                        in1=PUv[:, cb, p, :], op0=ALU.mult, op1=ALU.add),
                           reads=[B_S, B_dec, B_PU[cb]], writes=[B_S])
                    Sb, B_Sb = Sbs[cb], B_Sbs[cb]
                    op("pool", lambda e: e.tensor_copy(out=Sb[:, :, :], in_=S[:, :, :]), reads=[B_S], writes=[B_Sb])
                    for h in range(4):
                        p, e_ = h // 2, h % 2
                        op("pe", mm(PRvs[e_][:, p, 0:C], Sb[e_ * 64:(e_ + 1) * 64, p, :], gqT[e_ * 64:(e_ + 1) * 64, p, c * C:(c + 1) * C]),
                           reads=[B_Sb, B_gqT], writes=[B_PRs[e_]])
                    orv = oraw[:, :, :].rearrange("p (q e) t -> p q e t", e=2)
                    for e_ in range(2):
                        op("act" if e_ == 0 else "dve",
                           (lambda e, e_=e_: e.activation(out=orv[:, :, e_, c * C:(c + 1) * C], in_=PRvs[e_][:, 0:2, 0:C], func=AF.Copy)) if e_ == 0 else
                           (lambda e, e_=e_: e.tensor_copy(out=orv[:, :, e_, c * C:(c + 1) * C], in_=PRvs[e_][:, 0:2, 0:C])),
                           reads=[B_PRs[e_]], writes=[B_oraw], partial=True)
                    for _ in range(2):
                        if pq:
                            pq.pop(0)()
                    if nq and (c % 2 == 1 or nchunks == 1):
                        nq.pop(0)()
                while pq:
                    pq.pop(0)()
                while nq:
                    nq.pop(0)()
                for h in range(4):
                    op("act", act(sq[:, 0:T], oraw[:, h, 0:T], AF.Square), reads=[B_oraw], writes=[B_sq])
                    ps, B_ps = next_pa()
                    op("pe", mm(ps[:, 0:T], ONES128, sq[:, 0:T]), reads=[B_const, B_sq], writes=[B_ps])
                    op("act", act(rs[:, 0:T], ps[:, 0:T], AF.Ln, bias=EPS), reads=[B_ps], writes=[B_rs])
                    op("act", act(rs[:, 0:T], rs[:, 0:T], AF.Exp, scale=-0.5), reads=[B_rs], writes=[B_rs])
                    op("dve", lambda e, h=h: e.scalar_tensor_tensor(out=tmp[:, 0:T], in0=oraw[:, h, 0:T], scalar=gnorm[:, h:h + 1],
                                                                    in1=rs[:, 0:T], op0=ALU.mult, op1=ALU.mult),
                       reads=[B_oraw, B_pp, B_rs], writes=[B_tmp])
                    op("dve", lambda e, h=h: e.tensor_tensor(out=mixg[:, h, 0:T], in0=tmp[:, 0:T], in1=sgT[:, h, 0:T], op=ALU.mult),
                       reads=[B_tmp, B_sgT], writes=[B_mixg], partial=True)
                dma("sp", MIXT_d[0:4, :, t0:t0 + T].rearrange("c p t -> p c t"), mixg[:, :, 0:T], reads=[B_mixg], writes=[B_MIXG[ti]])
            dma("sp", CREF_d, crefs[:, :, :], reads=[B_crefs], writes=[B_CREF])
            tk.barrier()
        if stop_after == "A":
            break

        lw = contextlib.ExitStack()
        R_dn = sb("R_dn", [128, NJ, D], BF16, lw); B_Rdn = Buf("R_dn")
        lo = contextlib.ExitStack()
        R_out = sb("R_out", [128, 8, D], BF16, lo); B_Rout = Buf("R_out")
        for k in range(8):
            dma("pool", R_out[:, k, :], wout_d[layer, k * 128:(k + 1) * 128, :], writes=[B_Rout], weights=True)
        for k in range(8):
            dma("pool", wup_v[:, k, :], wup_d[layer, k * 128:(k + 1) * 128, :], writes=[B_Rup], weights=True)
        for j in range(NJ):
            dma("pool", R_dn[:, j, :], wdn_d[layer, j * 128:(j + 1) * 128, :], writes=[B_Rdn], weights=True)

        with contextlib.ExitStack() as pb, nc.named_scope(f"B{layer}"):
            def sbb(name, shape, dt):
                return sb(name, shape, dt, pb)
            KTs = [sbb("KTs0", [128, L], BF16)] * 2
            B_KTs = [Buf("KTs0")] * 2
            Vps = [sbb(f"Vp{i}", [128, NKB, 192], BF16) for i in range(2)]
            B_Vps = [Buf(f"Vp{i}") for i in range(2)]
            csb = sbb("csb", [128, NKB, 8], F32); B_csb = Buf("csb")
            cref = sbb("cref", [128, NT, 8], F32); B_cref = Buf("cref")
            xnorm = sbb("xnorm", [128, 4], F32); B_xnorm = Buf("xnorm")
            biases = [sbb(f"bias{i}", [128, NKB, 2], F32) for i in range(2)]
            B_biases = [Buf(f"bias{i}") for i in range(2)]
            qts = [sbb(f"qt{i}", [128, 512], BF16) for i in range(2)]
            B_qts = [Buf(f"qt{i}") for i in range(2)]
            pts = [sbb(f"pt{i}", [128, 512], BF16) for i in range(4)]
            B_pts = [Buf(f"pt{i}") for i in range(4)]
            pcs = [sbb(f"pc{i}", [128, 512], F32) for i in range(2)]
            B_pcs = [Buf(f"pc{i}") for i in range(2)]
            sqs = [sbb("sqb0", [128, 512], BF16)]
            B_sqs = [Buf("sqb0")]
            rd = sbb("rd", [128, 512], F32); B_rd = Buf("rd")
            onorm = sbb("onorm", [128, 512], F32); B_on = Buf("onorm")
            rsb = sbb("rsb", [128, 512], F32); B_rsb = Buf("rsb")
            ots = [sbb(f"ot{i}", [128, 512], BF16) for i in range(2)]
            B_ots = [Buf(f"ot{i}") for i in range(2)]

            dma("sp", csb[:, :, :], CSB_d, reads=[B_CSB], writes=[B_csb])
            dma("sp", cref[:, :, :], CREF_d, reads=[B_CREF], writes=[B_cref])
            dma("sp", xnorm[:, :], xnorm_d[layer], writes=[B_xnorm])
            SBK = [[(PS[0], B_PS[0]), (PS[1], B_PS[1])], [(PS[2], B_PS[2]), (PS[6], B_PS[6])]]
            iters = [(p, ti) for p in range(4) for ti in range(NT)]

            def kbs_of(ti):
                if ti == 0:
                    return [(0, 0, 16)]
                return [(0, 0, 16)] + [(1 + j, NMETA + 128 * j, 128) for j in range(4 * ti)]

            def load_kt(p):
                dma("sp", KTs[p % 2][:, :], KT_d[p], reads=B_KT, writes=[B_KTs[p % 2]])

            def load_pair(p):
                Vp, B_Vp = Vps[p % 2], B_Vps[p % 2]
                dma("sp", Vp[0:16, 0, :], V_d[0:16, 192 * p:192 * p + 192], reads=B_V, writes=[B_Vp])
                dma("sp", Vp[:, 1:NKB, :], V_d[16:L, 192 * p:192 * p + 192].rearrange("(n q) c -> q n c", q=128),
                    reads=B_V, writes=[B_Vp])

            def load_q(n):
                p, ti = iters[n]
                t0, T = TILES[ti]
                nkb = len(kbs_of(ti))
                dma("sp", qts[n % 2][:, 0:T], QT_d[p, :, t0:t0 + T], reads=[B_QT[ti]], writes=[B_qts[n % 2]])
                op("dve", lambda e: e.tensor_tensor(
                    out=biases[n % 2][:, 0:nkb, :], in0=csb[:, 0:nkb, 2 * p:2 * p + 2],
                    in1=cref[:, ti:ti + 1, 2 * p:2 * p + 2].to_broadcast([128, nkb, 2]), op=ALU.subtract),
                   reads=[B_csb, B_cref], writes=[B_biases[n % 2]])

            def epilogue_dve(n):
                p, ti = iters[n]
                t0, T = TILES[ti]
                for e_ in range(2):
                    orow = slice(e_ * 64, (e_ + 1) * 64)
                    drow = slice((1 - e_) * 64, (2 - e_) * 64)
                    op("dve", lambda e: e.reciprocal(out=rd[orow, 0:T], in_=pcs[e_][drow, 0:T]), reads=[B_pcs[e_]], writes=[B_rd], partial=True)
                    op("dve", lambda e: e.tensor_tensor(out=onorm[orow, 0:T], in0=pcs[e_][orow, 0:T], in1=rd[orow, 0:T], op=ALU.mult),
                       reads=[B_pcs[e_], B_rd], writes=[B_on], partial=True)

            def epilogue(n):
                p, ti = iters[n]
                t0, T = TILES[ti]
                ot, B_ot = ots[n % 2], B_ots[n % 2]
                op("act", act(sqs[0][:, 0:T], onorm[:, 0:T], AF.Square), reads=[B_on], writes=[B_sqs[0]])
                pst, B_pst = PS[5], B_PS[5]
                op("pe", mm(pst[:, 0:T], WA, sqs[0][:, 0:T]), reads=[B_const, B_sqs[0]], writes=[B_pst])
                op("act", act(rsb[:, 0:T], pst[:, 0:T], AF.Ln, bias=EPS), reads=[B_pst], writes=[B_rsb])
                op("act", act(rsb[:, 0:T], rsb[:, 0:T], AF.Exp, scale=-0.5), reads=[B_rsb], writes=[B_rsb])
                op("dve", lambda e: e.scalar_tensor_tensor(
                    out=ot[:, 0:T], in0=onorm[:, 0:T], scalar=xnorm[:, p:p + 1],
                    in1=rsb[:, 0:T], op0=ALU.mult, op1=ALU.mult),
                   reads=[B_on, B_xnorm, B_rsb], writes=[B_ot])
                dma("sp", MIXT_d[4 + p, :, t0:t0 + T], ot[:, 0:T], reads=[B_ot], writes=[B_MIXF[p][ti]])

            load_kt(0)
            load_pair(0)
            load_q(0)
            deferred = []
            for n, (p, ti) in enumerate(iters):
                t0, T = TILES[ti]
                KTp, B_KTp = KTs[p % 2], B_KTs[p % 2]
                Vp, B_Vp = Vps[p % 2], B_Vps[p % 2]
                qt, B_qt = qts[n % 2], B_qts[n % 2]
                bias, B_bias = biases[n % 2], B_biases[n % 2]
                kbs = kbs_of(ti)
                nkb = len(kbs)

                def emit_s(idx):
                    g, k0, KB = kbs[idx]
                    qa = max(0, k0 - t0)
                    for e_ in range(2):
                        rows = slice(e_ * 64, (e_ + 1) * 64)
                        ps, B_ps = SBK[e_][idx % 2]
                        op("pe", mm(ps[0:KB, qa:T], KTp[rows, k0:k0 + KB], qt[rows, qa:T]),
                           reads=[B_KTp, B_qt], writes=[B_ps])

                emit_s(0)
                if n + 1 < len(iters):
                    if iters[n + 1][0] != p:
                        load_pair(iters[n + 1][0])
                    load_q(n + 1)
                for idx in range(nkb):
                    g, k0, KB = kbs[idx]
                    qa = max(0, k0 - t0)
                    diag = k0 + KB > t0
                    if idx + 1 < nkb:
                        emit_s(idx + 1)
                    for e_ in range(2):
                        ps, B_ps = SBK[e_][idx % 2]
                        pt, B_pt = pts[2 * e_ + idx % 2], B_pts[2 * e_ + idx % 2]
                        po, B_po = PS[3 + e_], B_PS[3 + e_]
                        op("act", act(pt[0:KB, qa:T], ps[0:KB, qa:T], AF.Exp, bias=bias[0:KB, g, e_:e_ + 1], scale=0.125),
                           reads=[B_ps, B_bias], writes=[B_pt])
                        if diag:
                            op("dve", lambda e: e.tensor_tensor(out=pt[0:KB, qa:qa + KB], in0=pt[0:KB, qa:qa + KB],
                                                                in1=TRIB[0:KB, 0:KB], op=ALU.mult),
                               reads=[B_pt, B_const], writes=[B_pt])
                        op("pe", mm(po[:, qa:T], Vp[0:KB, g, e_ * 64:e_ * 64 + 128], pt[0:KB, qa:T], idx == 0, idx == nkb - 1),
                           reads=[B_Vp, B_pt], writes=[B_po])
                    if deferred and idx == min(7, nkb - 1):
                        epilogue(deferred.pop(0))
                for e_ in range(2):
                    op("dve", lambda e, e_=e_: e.tensor_copy(out=pcs[e_][:, 0:T], in_=PS[3 + e_][:, 0:T]),
                       reads=[B_PS[3 + e_]], writes=[B_pcs[e_]])
                epilogue_dve(n)
                deferred.append(n)
                if n + 1 < len(iters) and iters[n + 1][0] != p:
                    load_kt(iters[n + 1][0])
            while deferred:
                epilogue(deferred.pop(0))
            tk.barrier()
        if stop_after == "B":
            lo.close()
            lw.close()
            break

        with contextlib.ExitStack() as pc, nc.named_scope(f"C{layer}"):
            def sbc(name, shape, dt):
                return sb(name, shape, dt, pc)
            mixts = [sbc(f"mixt{i}", [128, 8, 512], BF16) for i in range(2)]
            B_mixts = [Buf(f"mixt{i}") for i in range(2)]
            hbs = [sbc(f"hbc{i}", [128, D], F32) for i in range(2)]
            B_hbs = [Buf(f"hbc{i}") for i in range(2)]
            for ti, (t0, T) in enumerate(TILES):
                mt, B_mt = mixts[ti % 2], B_mixts[ti % 2]
                dma("sp", mt[:, :, 0:T], MIXT_d[:, :, t0:t0 + T].rearrange("c p t -> p c t"),
                    reads=[B_MIXG[ti]] + [B_MIXF[p][ti] for p in range(4)], writes=[B_mt])
                for bi, (b0, o, B) in enumerate(blocks_of(t0, T)):
                    g = gblk(b0)
                    hb, B_hb = hbs[g % 2], B_hbs[g % 2]
                    dma("sp", hb[0:B, :], h_rows(layer, b0, B, True), reads=[B_H[g]], writes=[B_hb])
                    for half in range(2):
                        ps, B_ps = PS[half], B_PS[half]
                        for c in range(8):
                            op("pe", mm(ps[0:B, :], mt[:, c, o:o + B], R_out[:, c, half * 512:(half + 1) * 512], c == 0, c == 7),
                               reads=[B_mt, B_Rout], writes=[B_ps])
                        op("dve", lambda e, half=half, ps=ps: e.tensor_tensor(
                            out=hb[0:B, half * 512:(half + 1) * 512], in0=ps[0:B, :], in1=hb[0:B, half * 512:(half + 1) * 512], op=ALU.add),
                           reads=[B_ps, B_hb], writes=[B_hb], partial=True)
                    dma("sp", H_d[b0:b0 + B, :], hb[0:B, :], reads=[B_hb], writes=[B_H[g]])
            tk.barrier()
        lo.close()
        if stop_after == "C":
            lw.close()
            break

        with contextlib.ExitStack() as pd, nc.named_scope(f"D{layer}"):
            def sbd(name, shape, dt):
                return sb(name, shape, dt, pd)
            g2bc = sbd("g2bc", [128, D], F32); B_g2 = Buf("g2bc")
            convw = sbd("convw", [128, 2 * NJ, 3], F32)
            convb = sbd("convb", [128, 2 * NJ], F32)
            B_cv = Buf("conv")
            dma("sp", g2bc[:, :], fnorm_d[layer], writes=[B_g2])
            dma("sp", convw[:, :, :], convw_d[layer], writes=[B_cv])
            dma("sp", convb[:, :], convb_d[layer], writes=[B_cv])
            if last:
                fbc = sbd("fbc", [128, D], F32); B_fbc = Buf("fbc")
                dma("sp", fbc[:, :], final_d, writes=[B_fbc])
            halo = sbd("halo", [128, 2 * NJ, 2], F32); B_halo = Buf("halo")
            op("pool", lambda e: e.memset(halo[:, :, :], 0.0), writes=[B_halo])
            hbs = [sbd(f"hbd{i}", [128, D], F32) for i in range(2)]
            B_hbs = [Buf(f"hbd{i}") for i in range(2)]
            if last:
                hbx = sbd("hbx", [128, D], F32)
                B_hbx = Buf("hbx")
                hbn, B_hbn = [hbx, hbx], [B_hbx, B_hbx]
            else:
                hbn = [sbd(f"hbn{i}", [128, D], F32) for i in range(2)]
                B_hbn = [Buf(f"hbn{i}") for i in range(2)]
            ws = dict(xn=sbd("xnd", [128, D], BF16), B_xn=Buf("xnd"), st=sbd("std", [128, 4], F32), B_st=Buf("std"))
            xnT = sbd("xnTd", [128, 8, 512], BF16); B_xnT = Buf("xnTd")
            hbufs = [sbd(f"hbuf{i}", [128, 514], F32) for i in range(4)]
            B_hbufs = [Buf(f"hbuf{i}") for i in range(4)]
            B_hhalo = [Buf(f"hhalo{i}") for i in range(4)]
            t1s = [sbd(f"t1_{i}", [128, 512], F32) for i in range(6)]
            B_t1s = [Buf(f"t1_{i}") for i in range(6)]
            actT = sbd("actT", [128, NJ, 512], BF16); B_actT = Buf("actT")
            hcnt = 0
            for ti, (t0, T) in enumerate(TILES):
                blks = blocks_of(t0, T)

                def norm_block_d(tj, bj):
                    tt0, TT = TILES[tj]
                    b0_, o_, B_ = blocks_of(tt0, TT)[bj]
                    g_ = gblk(b0_)
                    hb, B_hb = hbn[g_ % 2], B_hbn[g_ % 2]
                    dma("sp", hb[0:B_, :], H_d[b0_:b0_ + B_, :], reads=[B_H[g_]], writes=[B_hb])
                    norm_transpose(ws, hb, B_hb, g2bc, B_g2, xnT, B_xnT, o_, B_)

                if ti == 0:
                    norm_block_d(0, 0)
                nq = []
                if ti + 1 < NT:
                    for bj in range(len(blocks_of(*TILES[ti + 1]))):
                        nq.append(lambda bj=bj: norm_block_d(ti + 1, bj))
                pend = []
                silu_done = set()

                def fin(item):
                    pj, ((tu, B_tu), (tg, B_tg)) = item
                    if pj not in silu_done:
                        op("act", act(tg[:, 0:T], tg[:, 0:T], AF.Silu), reads=[B_tg], writes=[B_tg])
                    op("dve", lambda e: e.tensor_tensor(out=actT[:, pj, 0:T], in0=tu[:, 0:T], in1=tg[:, 0:T], op=ALU.mult),
                       reads=[B_tu, B_tg], writes=[B_actT], partial=True)

                for j in range(NJ):
                    pss = []
                    for br in range(2):
                        ps, B_ps = PS[2 + 2 * (j % 2) + br], B_PS[2 + 2 * (j % 2) + br]
                        col0 = br * DFF + j * 128
                        for k in range(8):
                            op("pe", mm(ps[:, 0:T], wup_v[:, k, col0:col0 + 128], xnT[:, k, 0:T], k == 0, k == 7),
                               reads=[B_Rup, B_xnT], writes=[B_ps])
                        pss.append((ps, B_ps))
                    t3 = []
                    for br in range(2):
                        ps, B_ps = pss[br]
                        ci = br * NJ + j
                        bi_ = 2 * (j % 2) + br
                        hbuf, B_hbuf = hbufs[bi_], B_hbufs[bi_]
                        t1, B_t1 = t1s[2 * (j % 3) + br], B_t1s[2 * (j % 3) + br]
                        op("pool", lambda e, ci=ci, hbuf=hbuf: e.tensor_copy(out=hbuf[:, 0:2], in_=halo[:, ci, :]),
                           reads=[B_halo], writes=[B_hhalo[bi_]])
                        op("act", act(hbuf[:, 2:2 + T], ps[:, 0:T], AF.Copy), reads=[B_ps], writes=[B_hbuf])
                        op("act", act(t1[:, 0:T], ps[:, 0:T], AF.Identity, bias=convb[:, ci:ci + 1], scale=convw[:, ci, 2:3]),
                           reads=[B_ps, B_cv], writes=[B_t1])
                        op("pool", lambda e, ci=ci, hbuf=hbuf: e.tensor_copy(out=halo[:, ci, :], in_=hbuf[:, T:T + 2]),
                           reads=[B_hbuf], writes=[B_halo], partial=True)
                        t3.append((t1, B_t1))
                    if pend:
                        pj, ((ptu, B_ptu), (ptg, B_ptg)) = pend[0]
                        op("act", act(ptg[:, 0:T], ptg[:, 0:T], AF.Silu), reads=[B_ptg], writes=[B_ptg])
                        silu_done.add(pj)
                    for br in range(2):
                        ci = br * NJ + j
                        bi_ = 2 * (j % 2) + br
                        hbuf, B_hbuf = hbufs[bi_], B_hbufs[bi_]
                        t1, B_t1 = t1s[2 * (j % 3) + br], B_t1s[2 * (j % 3) + br]
                        op("dve", lambda e, ci=ci, hbuf=hbuf, t1=t1: e.scalar_tensor_tensor(
                            out=t1[:, 0:T], in0=hbuf[:, 1:1 + T], scalar=convw[:, ci, 1:2], in1=t1[:, 0:T], op0=ALU.mult, op1=ALU.add),
                           reads=[B_hbuf, B_hhalo[bi_], B_cv, B_t1], writes=[B_t1])
                        op("dve", lambda e, ci=ci, hbuf=hbuf, t1=t1: e.scalar_tensor_tensor(
                            out=t1[:, 0:T], in0=hbuf[:, 0:T], scalar=convw[:, ci, 0:1], in1=t1[:, 0:T], op0=ALU.mult, op1=ALU.add),
                           reads=[B_hbuf, B_hhalo[bi_], B_cv, B_t1], writes=[B_t1])
                    pend.append((j, t3))
                    if len(pend) > 1:
                        fin(pend.pop(0))
                while pend:
                    fin(pend.pop(0))
                for bi, (b0, o, B) in enumerate(blks):
                    g = gblk(b0)
                    hb, B_hb = hbs[hcnt % 2], B_hbs[hcnt % 2]
                    hcnt += 1
                    dma("sp", hb[0:B, :], H_d[b0:b0 + B, :], reads=[B_H[g]], writes=[B_hb])
                    if nq:
                        nq.pop(0)()
                    for half in range(2):
                        ps, B_ps = PS[half], B_PS[half]
                        for j in range(NJ):
                            op("pe", mm(ps[0:B, :], actT[:, j, o:o + B], R_dn[:, j, half * 512:(half + 1) * 512], j == 0, j == NJ - 1),
                               reads=[B_actT, B_Rdn], writes=[B_ps])
                        op("dve", lambda e, half=half, ps=ps, hb=hb: e.tensor_tensor(
                            out=hb[0:B, half * 512:(half + 1) * 512], in0=ps[0:B, :], in1=hb[0:B, half * 512:(half + 1) * 512], op=ALU.add),
                           reads=[B_ps, B_hb], writes=[B_hb], partial=True)
                    if not last:
                        dma("pool", H_d[b0:b0 + B, :], hb[0:B, :], reads=[B_hb], writes=[B_H[g]])
                    elif ti > 0:
                        xn, B_xn, st, B_st = ws["xn"], ws["B_xn"], ws["st"], ws["B_st"]
                        op("act", act(xn[0:B, :], hb[0:B, :], AF.Square, accum_out=st[0:B, 0:1]), reads=[B_hb], writes=[B_xn, B_st])
                        op("act", act(st[0:B, 1:2], st[0:B, 0:1], AF.Ln, bias=EPS, scale=1.0 / D), reads=[B_st], writes=[B_st])
                        op("act", act(st[0:B, 2:3], st[0:B, 1:2], AF.Exp, scale=-0.5), reads=[B_st], writes=[B_st])
                        op("dve", lambda e, hb=hb: e.scalar_tensor_tensor(out=hb[0:B, :], in0=hb[0:B, :], scalar=st[0:B, 2:3],
                                                                        in1=fbc[0:B, :], op0=ALU.mult, op1=ALU.mult),
                           reads=[B_hb, B_st, B_fbc], writes=[B_hb])
                        dma("pool", y_d[b0 - NMETA:b0 - NMETA + B, :], hb[0:B, :], reads=[B_hb], writes=[B_H[g]])
                while nq:
                    nq.pop(0)()
            tk.barrier()
        lw.close()
    tk.final_wait()
    if tk.limit is not None:
        print("K_LIMIT", tk.limit, "total ops", tk.nops, "last emitted:", [x for x in tk.log if x[0] in (tk.limit - 1, tk.limit, tk.limit + 1)])


def _consts():
    s = np.arange(128)[:, None]
    t = np.arange(128)[None, :]
    tri = (s <= t).astype(np.float32)
    su = ((s > t) & (s // 64 == t // 64)).astype(np.float32)
    e127 = np.zeros((128, 128), np.float32); e127[127, :] = 1.0
    e15 = np.zeros((128, 128), np.float32); e15[15, :] = 1.0
    ch = (s // 64 == np.arange(8)[None, :]).astype(np.float32)
    cf32 = np.concatenate([tri, su, e127, e15, ch], axis=1)
    ident = np.eye(128, dtype=np.float32)
    ones128 = np.full((128, 128), 1.0 / 128, np.float32)
    wa = np.zeros((128, 128), np.float32); wa[0:64, 0:64] = 1.0 / 64; wa[64:128, 64:128] = 1.0 / 64
    wb = np.zeros((128, 128), np.float32); wb[0:64, 64:128] = EPS / 64; wb[64:128, 64:128] = 1.0 / 64
    cbf = np.concatenate([ident, tri, ones128, wa, wb], axis=1).astype(ml_dtypes.bfloat16)
    return np.ascontiguousarray(cf32), np.ascontiguousarray(cbf)


def make_in_maps(inputs, cores=range(8)):
    f = lambda a: np.ascontiguousarray(np.asarray(a, dtype=np.float32))
    x = f(inputs["x"])
    bc = lambda a: np.ascontiguousarray(np.broadcast_to(f(a)[:, None, :], (a.shape[0], 128, a.shape[1])))
    cf32, cbf = _consts()
    conv_w = f(inputs["conv_w"])
    shared = {
        "meta": f(inputs["meta_tokens"]),
        "w_in": f(inputs["w_in"]), "w_out": f(inputs["w_out"]), "w_up": f(inputs["w_up"]), "w_down": f(inputs["w_down"]),
        "anorm_bc": bc(np.asarray(inputs["attn_norm"])), "fnorm_bc": bc(np.asarray(inputs["ffn_norm"])),
        "final_bc": np.ascontiguousarray(np.broadcast_to(f(inputs["final_norm"])[None, :], (128, D))),
        "wau": f(inputs["w_alpha_up"]),
        "balpha_bc": bc(np.asarray(inputs["b_alpha"])), "bforget_bc": bc(np.asarray(inputs["b_forget"])),
        "gnorm_fm": np.ascontiguousarray(f(inputs["gla_norm"]).reshape(DEPTH, 4, 128).transpose(0, 2, 1)),
        "xnorm_fm": np.ascontiguousarray(f(inputs["fox_norm"]).reshape(DEPTH, 4, 128).transpose(0, 2, 1)),
        "convw_fm": np.ascontiguousarray(conv_w.reshape(DEPTH, 3, 2 * NJ, 128).transpose(0, 3, 2, 1)),
        "convb_fm": np.ascontiguousarray(f(inputs["conv_b"]).reshape(DEPTH, 2 * NJ, 128).transpose(0, 2, 1)),
        "cf32": cf32, "cbf": cbf,
    }
    return [dict(shared, x=np.ascontiguousarray(x[c])) for c in cores]


_NC_CACHE = {}


def kernel(**inputs):
    if "nc" not in _NC_CACHE:
        _NC_CACHE["nc"] = build_nc()
    nc = _NC_CACHE["nc"]
    in_maps = make_in_maps(inputs)
    res = run_bass_kernel_spmd(nc, in_maps, core_ids=list(range(8)))
    return np.stack([np.asarray(r["y"], dtype=np.float32) for r in res.results], axis=0)
```

```python
import contextlib
import numpy as np
import ml_dtypes
import concourse.bass as bass
import concourse.mybir as mybir
from concourse.bass_utils import run_bass_kernel_spmd

F32 = mybir.dt.float32
BF16 = mybir.dt.bfloat16
ALU = mybir.AluOpType
AF = mybir.ActivationFunctionType

D = 1024
SEQ = 4096
NMETA = 16
L = SEQ + NMETA
DEPTH = 4
NIN = 3096
DFF = 2816
NJ = DFF // 128
EPS = 1e-6
C_GQ, C_GK, C_GV, C_GR, C_LR, C_FQ, C_FK, C_FV, C_FF = 0, 256, 512, 1024, 1536, 1552, 2064, 2576, 3088

TILES = [(0, NMETA)] + [(NMETA + 512 * i, 512) for i in range(8)]
NT = len(TILES)
NKB = 33


def blocks_of(t0, T):
    return [(t0 + o, o, min(128, T - o)) for o in range(0, T, 128)]


def gblk(b0):
    return 0 if b0 == 0 else 1 + (b0 - NMETA) // 128


class Buf:
    __slots__ = ("name", "w", "r", "excl")

    def __init__(self, name, excl=False):
        self.name = name
        self.w = {}
        self.r = {}
        self.excl = excl


class _Eng:
    def __init__(self, name, h, sem):
        self.name, self.h, self.sem = name, h, sem
        self.count = 0
        self.seen = {}


class _Slot:
    def __init__(self, sem):
        self.sem = sem
        self.cnt = 0


class Tracker:
    def __init__(self, nc, es, n_work=24, n_wq=46):
        self.nc = nc
        self.e = {}
        for name, h in (("pe", nc.tensor), ("act", nc.scalar), ("dve", nc.vector),
                        ("pool", nc.gpsimd), ("sp", nc.sync)):
            self.e[name] = _Eng(name, h, es.enter_context(nc.semaphore("s_" + name)))
        self.work = [_Slot(es.enter_context(nc.semaphore(f"dw{i}"))) for i in range(n_work)]
        self.wq = [_Slot(es.enter_context(nc.semaphore(f"dq{i}"))) for i in range(n_wq)]
        self.pwork = [_Slot(es.enter_context(nc.semaphore(f"dp{i}"))) for i in range(10)]
        self.wi = 0
        self.qi = 0
        self.pi = 0
        import os, sys
        self.limit = int(os.environ.get("K_LIMIT", "0")) or None
        self.nops = 0
        self.log = []

    def _skip(self, engname):
        self.nops += 1
        if self.limit is not None:
            import sys
            self.log.append((self.nops, engname, sys._getframe(2).f_lineno))
            return self.nops > self.limit
        return False

    def _wait(self, eng, sem, val):
        k = id(sem)
        if eng.seen.get(k, 0) >= val:
            return
        eng.h.wait_ge(sem, val)
        eng.seen[k] = val

    def _deps(self, eng, reads, writes):
        need = {}

        def add(ev, raw):
            sem, val = ev
            if sem is eng.sem:
                if eng.name == "pe":
                    return
            k = id(sem)
            if k not in need or need[k][1] < val:
                need[k] = ev

        for b in reads:
            for ev in b.w.values():
                add(ev, True)
        for b in writes:
            for ev in b.w.values():
                add(ev, False)
            for ev in b.r.values():
                add(ev, False)
        for sem, val in need.values():
            self._wait(eng, sem, val)

    def op(self, engname, fn, reads=(), writes=(), partial=False):
        if self._skip(engname):
            return None
        eng = self.e[engname]
        xs = [b for b in reads if b.excl]
        if xs:
            writes = list(writes) + xs
        self._deps(eng, reads, writes)
        ins = fn(eng.h)
        eng.count += 1
        ins.then_inc(eng.sem, 1)
        ev = (eng.sem, eng.count)
        k = id(eng.sem)
        for b in reads:
            b.r[k] = ev
        for b in writes:
            if partial:
                b.w[k] = ev
            else:
                b.w = {k: ev}
                b.r = {}
        return ins

    def dma(self, qname, out, in_, reads=(), writes=(), weights=False):
        if self._skip("dma_" + qname):
            return None
        eng = self.e[qname]
        if weights:
            slot = self.wq[self.qi % len(self.wq)]
            self.qi += 1
        elif qname == "pool":
            slot = self.pwork[self.pi % len(self.pwork)]
            self.pi += 1
        else:
            slot = self.work[self.wi % len(self.work)]
            self.wi += 1
        self._deps(eng, reads, writes)
        if slot.cnt:
            self._wait(eng, slot.sem, slot.cnt)
        ins = eng.h.dma_start(out=out, in_=in_)
        slot.cnt += 16
        ins.then_inc(slot.sem, 16)
        ev = (slot.sem, slot.cnt)
        k = id(slot.sem)
        for b in reads:
            b.r[k] = ev
        for b in writes:
            b.w = {k: ev}
            b.r = {}

    def barrier(self):
        engs = list(self.e.values())
        for e in engs:
            for o in engs:
                if o is not e and o.count:
                    self._wait(e, o.sem, o.count)
            for s in self.work + self.pwork:
                if s.cnt:
                    self._wait(e, s.sem, s.cnt)

    def final_wait(self):
        sp = self.e["sp"]
        for s in self.work + self.wq + self.pwork:
            if s.cnt:
                self._wait(sp, s.sem, s.cnt)
        for o in self.e.values():
            if o is not sp and o.count:
                self._wait(sp, o.sem, o.count)


def build_nc(n_layers=DEPTH, dbg=False, stop_after=None):
    nc = bass.Bass("TRN2", target_bir_lowering=False)
    es = contextlib.ExitStack()
    with es:
        _build(nc, es, n_layers, dbg, stop_after)
    return nc


def _build(nc, es, n_layers, dbg, stop_after):
    def dram_in(name, shape, dt=F32):
        return nc.dram_tensor(name, list(shape), dt, kind="ExternalInput").ap()

    skind = "ExternalOutput" if dbg else "Internal"

    def dram_s(name, shape, dt):
        return nc.dram_tensor(name, list(shape), dt, kind=skind).ap()

    x_d = dram_in("x", [SEQ, D])
    meta_d = dram_in("meta", [NMETA, D])
    win_d = dram_in("w_in", [DEPTH, D, NIN])
    wout_d = dram_in("w_out", [DEPTH, D, D])
    wup_d = dram_in("w_up", [DEPTH, D, 2 * DFF])
    wdn_d = dram_in("w_down", [DEPTH, DFF, D])
    anorm_d = dram_in("anorm_bc", [DEPTH, 128, D])
    fnorm_d = dram_in("fnorm_bc", [DEPTH, 128, D])
    final_d = dram_in("final_bc", [128, D])
    wau_d = dram_in("wau", [DEPTH, 16, 256])
    balpha_d = dram_in("balpha_bc", [DEPTH, 128, 256])
    bforget_d = dram_in("bforget_bc", [DEPTH, 128, 8])
    gnorm_d = dram_in("gnorm_fm", [DEPTH, 128, 4])
    xnorm_d = dram_in("xnorm_fm", [DEPTH, 128, 4])
    convw_d = dram_in("convw_fm", [DEPTH, 128, 2 * NJ, 3])
    convb_d = dram_in("convb_fm", [DEPTH, 128, 2 * NJ])
    cf32_d = dram_in("cf32", [128, 4 * 128 + 8])
    cbf_d = dram_in("cbf", [128, 5 * 128], BF16)
    y_d = nc.dram_tensor("y", [SEQ, D], F32, kind="ExternalOutput").ap()

    H_d = dram_s("H", [L, D], F32)
    QT_d = dram_s("QT", [4, 128, L], BF16)
    KT_d = dram_s("KT", [4, 128, L], BF16)
    V_d = dram_s("V", [L, 768], BF16)
    CSB_d = dram_s("CSB", [128, NKB, 8], F32)
    CREF_d = dram_s("CREF", [128, NT, 8], F32)
    MIXT_d = dram_s("MIXT", [8, 128, L], BF16)

    tk = Tracker(nc, es)
    op, dma = tk.op, tk.dma

    uid = [0]

    def sb(name, shape, dt, stack=es):
        uid[0] += 1
        return stack.enter_context(nc.sbuf_tensor(f"sb{uid[0]}_{name}", list(shape), dt))

    R_up = sb("R_up", [128, 8 * 2 * DFF], BF16)
    B_Rup = Buf("R_up")
    win_v = R_up[:, 0:8 * NIN].rearrange("p (k n) -> p k n", k=8)
    wup_v = R_up[:, :].rearrange("p (k n) -> p k n", k=8)
    cbf = sb("cbf", [128, 5 * 128], BF16)
    B_const = Buf("const")
    IDENT = cbf[:, 0:128]
    TRIB = cbf[:, 128:256]
    ONES128 = cbf[:, 256:384]
    WA = cbf[:, 384:512]
    WB = cbf[:, 512:640]

    PT = es.enter_context(nc.psum_tensor("PT", [128, 1024], BF16))
    PS = [es.enter_context(nc.psum_tensor(f"PS{i}", [128, 512], F32)) for i in range(7)]
    B_PT = Buf("PT", True)
    B_PS = [Buf(f"PS{i}", True) for i in range(7)]

    dma("sp", cbf[:, :], cbf_d, writes=[B_const])

    B_H = [Buf(f"H{g}") for g in range(NKB)]
    B_QT = [Buf(f"QT{t}") for t in range(NT)]
    B_KT = [Buf(f"KT{t}") for t in range(NT)]
    B_V = [Buf(f"V{g}") for g in range(NKB)]
    B_CSB = Buf("CSB")
    B_CREF = Buf("CREF")
    B_MIXG = [Buf(f"MIXG{t}") for t in range(NT)]
    B_MIXF = [[Buf(f"MIXF{p}_{t}") for t in range(NT)] for p in range(4)]

    def h_rows(layer, b0, B, first_src):
        if layer == 0 and first_src:
            if b0 == 0:
                return meta_d[0:B, :]
            return x_d[b0 - NMETA:b0 - NMETA + B, :]
        return H_d[b0:b0 + B, :]

    def act(out, in_, func, bias=None, scale=None, accum_out=None):
        kw = {}
        if bias is not None:
            kw["bias"] = bias
        if scale is not None:
            kw["scale"] = scale
        if accum_out is not None:
            kw["accum_out"] = accum_out
        return lambda e: e.activation(out=out, in_=in_, func=func, **kw)

    def mm(out, lhsT, rhs, start=True, stop=True):
        return lambda e: e.matmul(out, lhsT=lhsT, rhs=rhs, start=start, stop=stop)

    def norm_transpose(ws, hb, B_hb, gbc, B_g, xnT, B_xnT, o, B):
        norm_part1(ws, hb, B_hb, gbc, B_g, B)
        norm_part2(ws, xnT, B_xnT, o, B)

    def norm_part1(ws, hb, B_hb, gbc, B_g, B):
        xn, B_xn, st, B_st = ws["xn"], ws["B_xn"], ws["st"], ws["B_st"]
        op("act", act(xn[0:B, :], hb[0:B, :], AF.Square, accum_out=st[0:B, 0:1]),
           reads=[B_hb], writes=[B_xn, B_st])
        op("act", act(st[0:B, 1:2], st[0:B, 0:1], AF.Ln, bias=EPS, scale=1.0 / D),
           reads=[B_st], writes=[B_st])
        op("act", act(st[0:B, 2:3], st[0:B, 1:2], AF.Exp, scale=-0.5),
           reads=[B_st], writes=[B_st])
        op("dve", lambda e: e.scalar_tensor_tensor(out=xn[0:B, :], in0=hb[0:B, :], scalar=st[0:B, 2:3],
                                                   in1=gbc[0:B, :], op0=ALU.mult, op1=ALU.mult),
           reads=[B_hb, B_st, B_g], writes=[B_xn])

    def norm_part2(ws, xnT, B_xnT, o, B):
        xn, B_xn = ws["xn"], ws["B_xn"]
        for k in range(8):
            op("pe", lambda e, k=k: e.transpose(out=PT[:, k * 128:k * 128 + B], in_=xn[0:B, k * 128:(k + 1) * 128],
                                                identity=IDENT[0:B, 0:B]),
               reads=[B_xn, B_const], writes=[B_PT])
        src = PT[:, :].rearrange("p (k t) -> p k t", k=8)[:, :, 0:B]
        op("act", lambda e: e.activation(out=xnT[:, :, o:o + B], in_=src, func=AF.Copy),
           reads=[B_PT], writes=[B_xnT], partial=True)

    for layer in range(n_layers):
        last = layer == n_layers - 1
        for k in range(8):
            dma("pool", win_v[:, k, :], win_d[layer, k * 128:(k + 1) * 128, :], writes=[B_Rup], weights=True)

        with contextlib.ExitStack() as pa, nc.named_scope(f"A{layer}"):
            def sba(name, shape, dt):
                return sb(name, shape, dt, pa)
            cf32 = sba("cf32", [128, 4 * 128 + 8], F32)
            B_cf = Buf("cf32")
            dma("sp", cf32[:, :], cf32_d, writes=[B_cf])
            TRI = cf32[:, 0:128]
            SU = cf32[:, 128:256]
            E127 = cf32[:, 256:384]
            E15 = cf32[:, 384:512]
            CH = cf32[:, 512:520]
            gbc = sba("gbc", [128, D], F32); B_gbc = Buf("gbc")
            balpha = sba("balpha", [128, 256], F32)
            bforget = sba("bforget", [128, 8], F32)
            wau32 = sba("wau32", [16, 256], F32)
            wau = sba("wau", [16, 256], BF16)
            gnorm = sba("gnorm", [128, 4], F32)
            B_pp = Buf("layer_params")
            dma("sp", gbc[:, :], anorm_d[layer], writes=[B_gbc])
            dma("sp", balpha[:, :], balpha_d[layer], writes=[B_pp])
            dma("sp", bforget[:, :], bforget_d[layer], writes=[B_pp])
            dma("sp", gnorm[:, :], gnorm_d[layer], writes=[B_pp])
            dma("sp", wau32[:, :], wau_d[layer], writes=[B_pp])
            op("dve", lambda e: e.tensor_copy(out=wau[:, :], in_=wau32[:, :]), reads=[B_pp], writes=[B_pp])

            hbs = [sba(f"hb{i}", [128, D], F32) for i in range(2)]
            B_hbs = [Buf(f"hb{i}") for i in range(2)]
            ws = dict(xn=sba("xn", [128, D], BF16), B_xn=Buf("xn"), st=sba("st", [128, 4], F32), B_st=Buf("st"))
            xnTs = [sba(f"xnT{i}", [128, 8, 512], BF16) for i in range(2)]
            B_xnTs = [Buf(f"xnT{i}") for i in range(2)]
            gqT = sba("gqT", [128, 2, 512], BF16); B_gqT = Buf("gqT")
            sgT = sba("sgT", [128, 4, 512], F32); B_sgT = Buf("sgT")
            glrT = sba("glrT", [16, 512], BF16); B_glrT = Buf("glrT")
            qT = sba("qT", [128, 4, 512], BF16); B_qT = Buf("qT")
            kT = sba("kT", [128, 4, 512], BF16); B_kT = Buf("kT")
            vts = [sba(f"vt{i}", [128, 768], BF16) for i in range(2)]
            B_vts = [Buf(f"vt{i}") for i in range(2)]
            gv = sba("gv", [128, 4, 512], BF16); B_gv = Buf("gv")
            kdec = sba("kdec", [128, 4, 256], BF16); B_kdec = Buf("kdec")
            zb = sba("zb", [128, 256], F32); B_zb = Buf("zb")
            lz = sba("lz", [128, 256], F32); B_lz = Buf("lz")
            wdec = sba("wdec", [128, 256], F32); B_wdec = Buf("wdec")
            f8 = sba("f8", [128, 16], F32); B_f8 = Buf("f8")
            cposs = [sba(f"cpos{i}", [128, 8], F32) for i in range(2)]
            B_cposs = [Buf(f"cpos{i}") for i in range(2)]
            crefs = sba("crefs", [128, NT, 8], F32); B_crefs = Buf("crefs")
            dec = sba("dec", [128, 2, 8], F32); B_dec = Buf("dec")
            S = sba("S", [128, 2, 128], F32); B_S = Buf("S")
            Sbs = [sba(f"Sb{i}", [128, 2, 128], BF16) for i in range(2)]
            B_Sbs = [Buf(f"Sb{i}") for i in range(2)]
            oraw = sba("oraw", [128, 4, 512], F32); B_oraw = Buf("oraw")
            sq = sba("sq", [128, 512], BF16); B_sq = Buf("sq")
            rs = sba("rs", [128, 512], F32); B_rs = Buf("rs")
            tmp = sba("tmp", [128, 512], F32); B_tmp = Buf("tmp")
            mixg = sba("mixg", [128, 4, 512], BF16); B_mixg = Buf("mixg")

            op("pool", lambda e: e.memset(S[:, :, :], 0.0), writes=[B_S])
            for i in range(2):
                op("pool", lambda e, i=i: e.memset(cposs[i][:, :], 0.0), writes=[B_cposs[i]])
            for i in range(2):
                op("pool", lambda e, i=i: e.memset(vts[i][:, :], 1.0), writes=[B_vts[i]])

            pa_i = [0]

            def next_pa():
                i = pa_i[0] % 2
                pa_i[0] += 1
                return PS[i], B_PS[i]
            PM1, B_PM1a, B_PM1b = PS[3], B_PS[3], B_PS[3]
            PM2 = PS[4]
            B_PM2w = B_PM2ff = B_PM2cum = B_PM2cref = B_PM2dec = B_PS[4]
            PU, B_PU = PS[5], [B_PS[5], B_PS[5]]
            PRs, B_PRs = [PS[6], PS[2]], [B_PS[6], B_PS[2]]
            PUv = PU[:, :].rearrange("p (b q c) -> p b q c", b=2, q=2)
            PRvs = [P_[:, :].rearrange("p (h c) -> p h c", h=8) for P_ in PRs]

            nblk_seen = 0
            chunk_g = 0
            for ti, (t0, T) in enumerate(TILES):
                blks = blocks_of(t0, T)
                C = 16 if ti == 0 else 64
                xnT, B_xnT = xnTs[ti % 2], B_xnTs[ti % 2]

                def norm_block(tj, bj):
                    tt0, TT = TILES[tj]
                    b0_, o_, B_ = blocks_of(tt0, TT)[bj]
                    g_ = gblk(b0_)
                    hb, B_hb = hbs[g_ % 2], B_hbs[g_ % 2]
                    dma("sp", hb[0:B_, :], h_rows(layer, b0_, B_, True), reads=[B_H[g_]], writes=[B_hb])
                    norm_part1(ws, hb, B_hb, gbc, B_gbc, B_)
                    return lambda: norm_part2(ws, xnTs[tj % 2], B_xnTs[tj % 2], o_, B_)

                if ti == 0:
                    norm_block(0, 0)()
                def proj_fm(col0, M, evac):
                    ps, B_ps = next_pa()
                    for k in range(8):
                        op("pe", mm(ps[0:M, 0:T], win_v[:, k, col0:col0 + M], xnT[:, k, 0:T], k == 0, k == 7),
                           reads=[B_Rup, B_xnT], writes=[B_ps])
                    evac(ps, B_ps)
                proj_fm(C_LR, 16, lambda ps, B_ps: op(
                    "dve", lambda e: e.tensor_copy(out=glrT[:, 0:T], in_=ps[0:16, 0:T]), reads=[B_ps], writes=[B_glrT]))
                for bi, (b0, o, B) in enumerate(blks):
                    g = gblk(b0)
                    nch = 1 if ti == 0 else 2

                    def proj_tm(ps_ap, B_ps, col0, N):
                        for k in range(8):
                            op("pe", mm(ps_ap, xnT[:, k, o:o + B], win_v[:, k, col0:col0 + N], k == 0, k == 7),
                               reads=[B_Rup, B_xnT], writes=[B_ps])
                    ps, B_ps = next_pa()
                    proj_tm(ps[0:B, 0:512], B_ps, C_FV, 512)
                    vt, B_vt = vts[g % 2], B_vts[g % 2]
                    vt4 = vt[:, :].rearrange("b (p s c) -> b p s c", p=4, s=3)
                    ps4 = ps[:, :].rearrange("b (p e c) -> b p e c", p=4, e=2)
                    for e_ in range(2):
                        op("dve", lambda e, e_=e_: e.tensor_copy(out=vt4[0:B, :, 2 * e_, :], in_=ps4[0:B, :, e_, :]),
                           reads=[B_ps], writes=[B_vt], partial=True)
                    dma("sp", V_d[b0:b0 + B, :], vt[0:B, :], reads=[B_vt], writes=[B_V[g]])
                    ps, B_ps = next_pa()
                    proj_tm(ps[0:B, 0:512], B_ps, C_GV, 512)
                    op("act", act(gv[0:B, bi, :], ps[0:B, 0:512], AF.Copy), reads=[B_ps], writes=[B_gv], partial=True)
                    proj_tm(PM1[0:B, 0:256], B_PM1a, C_GK, 256)
                    proj_tm(PM2[0:B, 256:264], B_PM2ff, C_FF, 8)
                    op("dve", lambda e: e.tensor_tensor(out=f8[0:B, 0:8], in0=PM2[0:B, 256:264], in1=bforget[0:B, :], op=ALU.add),
                       reads=[B_PM2ff, B_pp], writes=[B_f8])
                    op("act", act(f8[0:B, 8:16], f8[0:B, 0:8], AF.Exp, scale=-1.0), reads=[B_f8], writes=[B_f8])
                    op("act", act(f8[0:B, 0:8], f8[0:B, 8:16], AF.Ln, bias=1.0), reads=[B_f8], writes=[B_f8])
                    cur, B_cur = cposs[g % 2], B_cposs[g % 2]
                    prv, B_prv = cposs[(g + 1) % 2], B_cposs[(g + 1) % 2]
                    op("pe", mm(PM2[0:B, 264:272], TRI[0:B, 0:B], f8[0:B, 0:8], True, g == 0),
                       reads=[B_cf, B_f8], writes=[B_PM2cum])
                    if g > 0:
                        Bp = 16 if g == 1 else 128
                        Es = E15 if g == 1 else E127
                        op("pe", mm(PM2[0:B, 264:272], Es[0:Bp, 0:B], prv[0:Bp, :], False, True),
                           reads=[B_cf, B_prv], writes=[B_PM2cum])
                    op("dve", lambda e: e.tensor_copy(out=cur[0:B, :], in_=PM2[0:B, 264:272]), reads=[B_PM2cum], writes=[B_cur])
                    Bw = 128 if g == 0 else B
                    dma("sp", CSB_d[0:Bw, g, :], cur[0:Bw, :], reads=[B_cur], writes=[B_CSB])
                    if (ti == 0 and bi == 0) or (ti > 0 and bi == 1):
                        Es = E15 if ti == 0 else E127
                        op("pe", mm(PM2[:, 272:280], Es[0:B, :], cur[0:B, :]), reads=[B_cf, B_cur], writes=[B_PM2cref])
                        op("dve", lambda e: e.tensor_copy(out=crefs[:, ti, :], in_=PM2[:, 272:280]),
                           reads=[B_PM2cref], writes=[B_crefs], partial=True)
                    op("pe", mm(PM1[0:B, 256:512], glrT[0:16, o:o + B], wau[:, :]), reads=[B_glrT, B_pp], writes=[B_PM1b])
                    op("dve", lambda e: e.tensor_tensor(out=zb[0:B, :], in0=PM1[0:B, 256:512], in1=balpha[0:B, :], op=ALU.add),
                       reads=[B_PM1b, B_pp], writes=[B_zb])
                    op("act", act(zb[0:B, :], zb[0:B, :], AF.Exp, scale=-1.0), reads=[B_zb], writes=[B_zb])
                    op("act", act(lz[0:B, :], zb[0:B, :], AF.Ln, bias=1.0), reads=[B_zb], writes=[B_lz])
                    op("pe", mm(PM2[0:B, 0:256], SU[0:B, 0:B], lz[0:B, :]), reads=[B_cf, B_lz], writes=[B_PM2w])
                    op("act", act(wdec[0:B, :], PM2[0:B, 0:256], AF.Exp, scale=-1.0 / 16), reads=[B_PM2w], writes=[B_wdec])
                    op("dve", lambda e: e.tensor_tensor(out=kdec[0:B, bi, :], in0=PM1[0:B, 0:256], in1=wdec[0:B, :], op=ALU.mult),
                       reads=[B_PM1a, B_wdec], writes=[B_kdec], partial=True)
                    for p in range(2):
                        op("pe", mm(PM2[:, 280 + 8 * p:280 + 8 * p + 8], lz[0:B, p * 128:(p + 1) * 128], CH[0:B, 0:8]),
                           reads=[B_cf, B_lz], writes=[B_PM2dec])
                    PMd = PM2[:, 280:296].rearrange("p (q c) -> p q c", q=2)
                    op("act", act(dec[:, :, 2 * bi:2 * bi + nch], PMd[:, :, 0:nch], AF.Exp, scale=-1.0 / 16),
                       reads=[B_PM2dec], writes=[B_dec], partial=True)
                for j in range(2):
                    proj_fm(C_GQ + 128 * j, 128, lambda ps, B_ps, j=j: op(
                        "act", act(gqT[:, j, 0:T], ps[:, 0:T], AF.Copy, scale=0.125), reads=[B_ps], writes=[B_gqT], partial=True))
                pq = []
                for j in range(4):
                    pq.append(lambda j=j: proj_fm(C_GR + 128 * j, 128, lambda ps, B_ps, j=j: op(
                        "act", act(sgT[:, j, 0:T], ps[:, 0:T], AF.Silu), reads=[B_ps], writes=[B_sgT], partial=True)))
                for j in range(4):
                    pq.append(lambda j=j: proj_fm(C_FQ + 128 * j, 128, lambda ps, B_ps, j=j: op(
                        "dve", lambda e: e.tensor_copy(out=qT[:, j, 0:T], in_=ps[:, 0:T]), reads=[B_ps], writes=[B_qT], partial=True)))
                pq.append(lambda: dma("sp", QT_d[:, :, t0:t0 + T].rearrange("c p t -> p c t"), qT[:, :, 0:T], reads=[B_qT], writes=[B_QT[ti]]))
                for j in range(4):
                    pq.append(lambda j=j: proj_fm(C_FK + 128 * j, 128, lambda ps, B_ps, j=j: op(
                        "dve", lambda e: e.tensor_copy(out=kT[:, j, 0:T], in_=ps[:, 0:T]), reads=[B_ps], writes=[B_kT], partial=True)))
                pq.append(lambda: dma("sp", KT_d[:, :, t0:t0 + T].rearrange("c p t -> p c t"), kT[:, :, 0:T], reads=[B_kT], writes=[B_KT[ti]]))
                nq = []
                part2 = []
                if ti + 1 < NT:
                    for bj in range(len(blocks_of(*TILES[ti + 1]))):
                        nq.append(lambda bj=bj: norm_block(ti + 1, bj))
                nchunks = T // C
                for c in range(nchunks):
                    bi = (c * C) // 128
                    r0 = (c * C) % 128
                    cb = 0
                    for h in range(4):
                        p, e_ = h // 2, h % 2
                        op("pe", mm(PUv[e_ * 64:(e_ + 1) * 64, cb, p, :], kdec[r0:r0 + C, bi, h * 64:(h + 1) * 64],
                                    gv[r0:r0 + C, bi, h * 128:(h + 1) * 128]),
                           reads=[B_kdec, B_gv], writes=[B_PU[cb]])
                    for p in range(2):
                        op("dve", lambda e, p=p: e.scalar_tensor_tensor(out=S[:, p, :], in0=S[:, p, :], scalar=dec[:, p, c:c + 1],
                                                                        in1=PUv[:, cb, p, :], op0=ALU.mult, op1=ALU.add),
                           reads=[B_S, B_dec, B_PU[cb]], writes=[B_S])
                    Sb, B_Sb = Sbs[cb], B_Sbs[cb]
                    op("act", act(Sb[:, :, :], S[:, :, :], AF.Copy), reads=[B_S], writes=[B_Sb])
                    for h in range(4):
                        p, e_ = h // 2, h % 2
                        op("pe", mm(PRvs[e_][:, p, 0:C], Sb[e_ * 64:(e_ + 1) * 64, p, :], gqT[e_ * 64:(e_ + 1) * 64, p, c * C:(c + 1) * C]),
                           reads=[B_Sb, B_gqT], writes=[B_PRs[e_]])
                    orv = oraw[:, :, :].rearrange("p (q e) t -> p q e t", e=2)
                    for e_ in range(2):
                        op("act" if e_ == 0 else "dve",
                           (lambda e, e_=e_: e.activation(out=orv[:, :, e_, c * C:(c + 1) * C], in_=PRvs[e_][:, 0:2, 0:C], func=AF.Copy)) if e_ == 0 else
                           (lambda e, e_=e_: e.tensor_copy(out=orv[:, :, e_, c * C:(c + 1) * C], in_=PRvs[e_][:, 0:2, 0:C])),
                           reads=[B_PRs[e_]], writes=[B_oraw], partial=True)
                    for _ in range(2):
                        if pq:
                            pq.pop(0)()
                    if part2:
                        part2.pop(0)()
                    if nq and (c % 2 == 1 or nchunks == 1):
                        part2.append(nq.pop(0)())
                while pq:
                    pq.pop(0)()
                while nq or part2:
                    if part2:
                        part2.pop(0)()
                    if nq:
                        part2.append(nq.pop(0)())
                for h in range(4):
                    op("act", act(sq[:, 0:T], oraw[:, h, 0:T], AF.Square), reads=[B_oraw], writes=[B_sq])
                    ps, B_ps = next_pa()
                    op("pe", mm(ps[:, 0:T], ONES128, sq[:, 0:T]), reads=[B_const, B_sq], writes=[B_ps])
                    op("act", act(rs[:, 0:T], ps[:, 0:T], AF.Ln, bias=EPS), reads=[B_ps], writes=[B_rs])
                    op("act", act(rs[:, 0:T], rs[:, 0:T], AF.Exp, scale=-0.5), reads=[B_rs], writes=[B_rs])
                    op("dve", lambda e, h=h: e.scalar_tensor_tensor(out=tmp[:, 0:T], in0=oraw[:, h, 0:T], scalar=gnorm[:, h:h + 1],
                                                                    in1=rs[:, 0:T], op0=ALU.mult, op1=ALU.mult),
                       reads=[B_oraw, B_pp, B_rs], writes=[B_tmp])
                    op("dve", lambda e, h=h: e.tensor_tensor(out=mixg[:, h, 0:T], in0=tmp[:, 0:T], in1=sgT[:, h, 0:T], op=ALU.mult),
                       reads=[B_tmp, B_sgT], writes=[B_mixg], partial=True)
                dma("sp", MIXT_d[0:4, :, t0:t0 + T].rearrange("c p t -> p c t"), mixg[:, :, 0:T], reads=[B_mixg], writes=[B_MIXG[ti]])
            dma("sp", CREF_d, crefs[:, :, :], reads=[B_crefs], writes=[B_CREF])
            tk.barrier()
        if stop_after == "A":
            break

        lw = contextlib.ExitStack()
        R_dn = sb("R_dn", [128, NJ, D], BF16, lw); B_Rdn = Buf("R_dn")
        lo = contextlib.ExitStack()
        R_out = sb("R_out", [128, 8, D], BF16, lo); B_Rout = Buf("R_out")
        for k in range(8):
            dma("pool", R_out[:, k, :], wout_d[layer, k * 128:(k + 1) * 128, :], writes=[B_Rout], weights=True)
        for k in range(8):
            dma("pool", wup_v[:, k, :], wup_d[layer, k * 128:(k + 1) * 128, :], writes=[B_Rup], weights=True)
        for j in range(NJ):
            dma("pool", R_dn[:, j, :], wdn_d[layer, j * 128:(j + 1) * 128, :], writes=[B_Rdn], weights=True)

        with contextlib.ExitStack() as pb, nc.named_scope(f"B{layer}"):
            def sbb(name, shape, dt):
                return sb(name, shape, dt, pb)
            KTs = [sbb("KTs0", [128, L], BF16)] * 2
            B_KTs = [Buf("KTs0")] * 2
            Vps = [sbb(f"Vp{i}", [128, NKB, 192], BF16) for i in range(2)]
            B_Vps = [Buf(f"Vp{i}") for i in range(2)]
            csb = sbb("csb", [128, NKB, 8], F32); B_csb = Buf("csb")
            cref = sbb("cref", [128, NT, 8], F32); B_cref = Buf("cref")
            xnorm = sbb("xnorm", [128, 4], F32); B_xnorm = Buf("xnorm")
            biases = [sbb(f"bias{i}", [128, NKB, 2], F32) for i in range(2)]
            B_biases = [Buf(f"bias{i}") for i in range(2)]
            qts = [sbb(f"qt{i}", [128, 512], BF16) for i in range(2)]
            B_qts = [Buf(f"qt{i}") for i in range(2)]
            pts = [sbb(f"pt{i}", [128, 512], BF16) for i in range(4)]
            B_pts = [Buf(f"pt{i}") for i in range(4)]
            pcs = [sbb(f"pc{i}", [128, 512], F32) for i in range(2)]
            B_pcs = [Buf(f"pc{i}") for i in range(2)]
            sqs = [sbb("sqb0", [128, 512], BF16)]
            B_sqs = [Buf("sqb0")]
            rd = sbb("rd", [128, 512], F32); B_rd = Buf("rd")
            onorm = sbb("onorm", [128, 512], F32); B_on = Buf("onorm")
            rsb = sbb("rsb", [128, 512], F32); B_rsb = Buf("rsb")
            ots = [sbb(f"ot{i}", [128, 512], BF16) for i in range(2)]
            B_ots = [Buf(f"ot{i}") for i in range(2)]

            dma("sp", csb[:, :, :], CSB_d, reads=[B_CSB], writes=[B_csb])
            dma("sp", cref[:, :, :], CREF_d, reads=[B_CREF], writes=[B_cref])
            dma("sp", xnorm[:, :], xnorm_d[layer], writes=[B_xnorm])
            SBK = [[(PS[0], B_PS[0]), (PS[1], B_PS[1])], [(PS[2], B_PS[2]), (PS[6], B_PS[6])]]
            iters = [(p, ti) for p in range(4) for ti in range(NT)]

            def kbs_of(ti):
                if ti == 0:
                    return [(0, 0, 16)]
                return [(0, 0, 16)] + [(1 + j, NMETA + 128 * j, 128) for j in range(4 * ti)]

            def load_kt(p):
                dma("sp", KTs[p % 2][:, :], KT_d[p], reads=B_KT, writes=[B_KTs[p % 2]])

            def load_pair(p):
                Vp, B_Vp = Vps[p % 2], B_Vps[p % 2]
                dma("sp", Vp[0:16, 0, :], V_d[0:16, 192 * p:192 * p + 192], reads=B_V, writes=[B_Vp])
                dma("sp", Vp[:, 1:NKB, :], V_d[16:L, 192 * p:192 * p + 192].rearrange("(n q) c -> q n c", q=128),
                    reads=B_V, writes=[B_Vp])

            def load_q(n):
                p, ti = iters[n]
                t0, T = TILES[ti]
                nkb = len(kbs_of(ti))
                dma("sp", qts[n % 2][:, 0:T], QT_d[p, :, t0:t0 + T], reads=[B_QT[ti]], writes=[B_qts[n % 2]])
                op("dve", lambda e: e.tensor_tensor(
                    out=biases[n % 2][:, 0:nkb, :], in0=csb[:, 0:nkb, 2 * p:2 * p + 2],
                    in1=cref[:, ti:ti + 1, 2 * p:2 * p + 2].to_broadcast([128, nkb, 2]), op=ALU.subtract),
                   reads=[B_csb, B_cref], writes=[B_biases[n % 2]])

            def epilogue_dve(n):
                p, ti = iters[n]
                t0, T = TILES[ti]
                for e_ in range(2):
                    orow = slice(e_ * 64, (e_ + 1) * 64)
                    drow = slice((1 - e_) * 64, (2 - e_) * 64)
                    op("dve", lambda e: e.reciprocal(out=rd[orow, 0:T], in_=pcs[e_][drow, 0:T]), reads=[B_pcs[e_]], writes=[B_rd], partial=True)
                    op("dve", lambda e: e.tensor_tensor(out=onorm[orow, 0:T], in0=pcs[e_][orow, 0:T], in1=rd[orow, 0:T], op=ALU.mult),
                       reads=[B_pcs[e_], B_rd], writes=[B_on], partial=True)

            def epilogue(n):
                p, ti = iters[n]
                t0, T = TILES[ti]
                ot, B_ot = ots[n % 2], B_ots[n % 2]
                op("act", act(sqs[0][:, 0:T], onorm[:, 0:T], AF.Square), reads=[B_on], writes=[B_sqs[0]])
                pst, B_pst = PS[5], B_PS[5]
                op("pe", mm(pst[:, 0:T], WA, sqs[0][:, 0:T]), reads=[B_const, B_sqs[0]], writes=[B_pst])
                op("act", act(rsb[:, 0:T], pst[:, 0:T], AF.Ln, bias=EPS), reads=[B_pst], writes=[B_rsb])
                op("act", act(rsb[:, 0:T], rsb[:, 0:T], AF.Exp, scale=-0.5), reads=[B_rsb], writes=[B_rsb])
                op("dve", lambda e: e.scalar_tensor_tensor(
                    out=ot[:, 0:T], in0=onorm[:, 0:T], scalar=xnorm[:, p:p + 1],
                    in1=rsb[:, 0:T], op0=ALU.mult, op1=ALU.mult),
                   reads=[B_on, B_xnorm, B_rsb], writes=[B_ot])
                dma("sp", MIXT_d[4 + p, :, t0:t0 + T], ot[:, 0:T], reads=[B_ot], writes=[B_MIXF[p][ti]])

            load_kt(0)
            load_pair(0)
            load_q(0)
            deferred = []
            for n, (p, ti) in enumerate(iters):
                t0, T = TILES[ti]
                KTp, B_KTp = KTs[p % 2], B_KTs[p % 2]
                Vp, B_Vp = Vps[p % 2], B_Vps[p % 2]
                qt, B_qt = qts[n % 2], B_qts[n % 2]
                bias, B_bias = biases[n % 2], B_biases[n % 2]
                kbs = kbs_of(ti)
                nkb = len(kbs)

                def emit_s(idx):
                    g, k0, KB = kbs[idx]
                    qa = max(0, k0 - t0)
                    for e_ in range(2):
                        rows = slice(e_ * 64, (e_ + 1) * 64)
                        ps, B_ps = SBK[e_][idx % 2]
                        op("pe", mm(ps[0:KB, qa:T], KTp[rows, k0:k0 + KB], qt[rows, qa:T]),
                           reads=[B_KTp, B_qt], writes=[B_ps])

                emit_s(0)
                if n + 1 < len(iters):
                    if iters[n + 1][0] != p:
                        load_pair(iters[n + 1][0])
                    load_q(n + 1)
                for idx in range(nkb):
                    g, k0, KB = kbs[idx]
                    qa = max(0, k0 - t0)
                    diag = k0 + KB > t0
                    if idx + 1 < nkb:
                        emit_s(idx + 1)
                    for e_ in range(2):
                        ps, B_ps = SBK[e_][idx % 2]
                        pt, B_pt = pts[2 * e_ + idx % 2], B_pts[2 * e_ + idx % 2]
                        po, B_po = PS[3 + e_], B_PS[3 + e_]
                        op("act", act(pt[0:KB, qa:T], ps[0:KB, qa:T], AF.Exp, bias=bias[0:KB, g, e_:e_ + 1], scale=0.125),
                           reads=[B_ps, B_bias], writes=[B_pt])
                        if diag:
                            op("dve", lambda e: e.tensor_tensor(out=pt[0:KB, qa:qa + KB], in0=pt[0:KB, qa:qa + KB],
                                                                in1=TRIB[0:KB, 0:KB], op=ALU.mult),
                               reads=[B_pt, B_const], writes=[B_pt])
                        op("pe", mm(po[:, qa:T], Vp[0:KB, g, e_ * 64:e_ * 64 + 128], pt[0:KB, qa:T], idx == 0, idx == nkb - 1),
                           reads=[B_Vp, B_pt], writes=[B_po])
                    if deferred and idx == min(7, nkb - 1):
                        epilogue(deferred.pop(0))
                for e_ in range(2):
                    op("dve", lambda e, e_=e_: e.tensor_copy(out=pcs[e_][:, 0:T], in_=PS[3 + e_][:, 0:T]),
                       reads=[B_PS[3 + e_]], writes=[B_pcs[e_]])
                epilogue_dve(n)
                deferred.append(n)
                if n + 1 < len(iters) and iters[n + 1][0] != p:
                    load_kt(iters[n + 1][0])
            while deferred:
                epilogue(deferred.pop(0))
            tk.barrier()
        if stop_after == "B":
            lo.close()
            lw.close()
            break

        with contextlib.ExitStack() as pc, nc.named_scope(f"C{layer}"):
            def sbc(name, shape, dt):
                return sb(name, shape, dt, pc)
            mixts = [sbc(f"mixt{i}", [128, 8, 512], BF16) for i in range(2)]
            B_mixts = [Buf(f"mixt{i}") for i in range(2)]
            hbs = [sbc(f"hbc{i}", [128, D], F32) for i in range(2)]
            B_hbs = [Buf(f"hbc{i}") for i in range(2)]
            for ti, (t0, T) in enumerate(TILES):
                mt, B_mt = mixts[ti % 2], B_mixts[ti % 2]
                dma("sp", mt[:, :, 0:T], MIXT_d[:, :, t0:t0 + T].rearrange("c p t -> p c t"),
                    reads=[B_MIXG[ti]] + [B_MIXF[p][ti] for p in range(4)], writes=[B_mt])
                for bi, (b0, o, B) in enumerate(blocks_of(t0, T)):
                    g = gblk(b0)
                    hb, B_hb = hbs[g % 2], B_hbs[g % 2]
                    dma("sp", hb[0:B, :], h_rows(layer, b0, B, True), reads=[B_H[g]], writes=[B_hb])
                    for half in range(2):
                        ps, B_ps = PS[half], B_PS[half]
                        for c in range(8):
                            op("pe", mm(ps[0:B, :], mt[:, c, o:o + B], R_out[:, c, half * 512:(half + 1) * 512], c == 0, c == 7),
                               reads=[B_mt, B_Rout], writes=[B_ps])
                        op("dve", lambda e, half=half, ps=ps: e.tensor_tensor(
                            out=hb[0:B, half * 512:(half + 1) * 512], in0=ps[0:B, :], in1=hb[0:B, half * 512:(half + 1) * 512], op=ALU.add),
                           reads=[B_ps, B_hb], writes=[B_hb], partial=True)
                    dma("sp", H_d[b0:b0 + B, :], hb[0:B, :], reads=[B_hb], writes=[B_H[g]])
            tk.barrier()
        lo.close()
        if stop_after == "C":
            lw.close()
            break

        with contextlib.ExitStack() as pd, nc.named_scope(f"D{layer}"):
            def sbd(name, shape, dt):
                return sb(name, shape, dt, pd)
            g2bc = sbd("g2bc", [128, D], F32); B_g2 = Buf("g2bc")
            convw = sbd("convw", [128, 2 * NJ, 3], F32)
            convb = sbd("convb", [128, 2 * NJ], F32)
            B_cv = Buf("conv")
            dma("sp", g2bc[:, :], fnorm_d[layer], writes=[B_g2])
            dma("sp", convw[:, :, :], convw_d[layer], writes=[B_cv])
            dma("sp", convb[:, :], convb_d[layer], writes=[B_cv])
            if last:
                fbc = sbd("fbc", [128, D], F32); B_fbc = Buf("fbc")
                dma("sp", fbc[:, :], final_d, writes=[B_fbc])
            halo = sbd("halo", [128, 2 * NJ, 2], F32); B_halo = Buf("halo")
            op("pool", lambda e: e.memset(halo[:, :, :], 0.0), writes=[B_halo])
            hbs = [sbd(f"hbd{i}", [128, D], F32) for i in range(2)]
            B_hbs = [Buf(f"hbd{i}") for i in range(2)]
            if last:
                hbx = sbd("hbx", [128, D], F32)
                B_hbx = Buf("hbx")
                hbn, B_hbn = [hbx, hbx], [B_hbx, B_hbx]
            else:
                hbn = [sbd(f"hbn{i}", [128, D], F32) for i in range(2)]
                B_hbn = [Buf(f"hbn{i}") for i in range(2)]
            ws = dict(xn=sbd("xnd", [128, D], BF16), B_xn=Buf("xnd"), st=sbd("std", [128, 4], F32), B_st=Buf("std"))
            xnT = sbd("xnTd", [128, 8, 512], BF16); B_xnT = Buf("xnTd")
            hbufs = [sbd(f"hbuf{i}", [128, 514], F32) for i in range(4)]
            B_hbufs = [Buf(f"hbuf{i}") for i in range(4)]
            B_hhalo = [Buf(f"hhalo{i}") for i in range(4)]
            t1s = [sbd(f"t1_{i}", [128, 512], F32) for i in range(6)]
            B_t1s = [Buf(f"t1_{i}") for i in range(6)]
            actT = sbd("actT", [128, NJ, 512], BF16); B_actT = Buf("actT")
            hcnt = 0
            for ti, (t0, T) in enumerate(TILES):
                blks = blocks_of(t0, T)

                def norm_block_d(tj, bj):
                    tt0, TT = TILES[tj]
                    b0_, o_, B_ = blocks_of(tt0, TT)[bj]
                    g_ = gblk(b0_)
                    hb, B_hb = hbn[g_ % 2], B_hbn[g_ % 2]
                    dma("sp", hb[0:B_, :], H_d[b0_:b0_ + B_, :], reads=[B_H[g_]], writes=[B_hb])
                    norm_part1(ws, hb, B_hb, g2bc, B_g2, B_)
                    return lambda: norm_part2(ws, xnT, B_xnT, o_, B_)

                if ti == 0:
                    norm_block_d(0, 0)()
                nq = []
                part2 = []
                if ti + 1 < NT:
                    for bj in range(len(blocks_of(*TILES[ti + 1]))):
                        nq.append(lambda bj=bj: norm_block_d(ti + 1, bj))
                pend = []
                silu_done = set()

                def fin(item):
                    pj, ((tu, B_tu), (tg, B_tg)) = item
                    if pj not in silu_done:
                        op("act", act(tg[:, 0:T], tg[:, 0:T], AF.Silu), reads=[B_tg], writes=[B_tg])
                    op("dve", lambda e: e.tensor_tensor(out=actT[:, pj, 0:T], in0=tu[:, 0:T], in1=tg[:, 0:T], op=ALU.mult),
                       reads=[B_tu, B_tg], writes=[B_actT], partial=True)

                for j in range(NJ):
                    pss = []
                    for br in range(2):
                        ps, B_ps = PS[2 + 2 * (j % 2) + br], B_PS[2 + 2 * (j % 2) + br]
                        col0 = br * DFF + j * 128
                        for k in range(8):
                            op("pe", mm(ps[:, 0:T], wup_v[:, k, col0:col0 + 128], xnT[:, k, 0:T], k == 0, k == 7),
                               reads=[B_Rup, B_xnT], writes=[B_ps])
                        pss.append((ps, B_ps))
                    t3 = []
                    for br in range(2):
                        ps, B_ps = pss[br]
                        ci = br * NJ + j
                        bi_ = 2 * (j % 2) + br
                        hbuf, B_hbuf = hbufs[bi_], B_hbufs[bi_]
                        t1, B_t1 = t1s[2 * (j % 3) + br], B_t1s[2 * (j % 3) + br]
                        op("pool", lambda e, ci=ci, hbuf=hbuf: e.tensor_copy(out=hbuf[:, 0:2], in_=halo[:, ci, :]),
                           reads=[B_halo], writes=[B_hhalo[bi_]])
                        op("act", act(hbuf[:, 2:2 + T], ps[:, 0:T], AF.Copy), reads=[B_ps], writes=[B_hbuf])
                        op("act", act(t1[:, 0:T], ps[:, 0:T], AF.Identity, bias=convb[:, ci:ci + 1], scale=convw[:, ci, 2:3]),
                           reads=[B_ps, B_cv], writes=[B_t1])
                        op("pool", lambda e, ci=ci, hbuf=hbuf: e.tensor_copy(out=halo[:, ci, :], in_=hbuf[:, T:T + 2]),
                           reads=[B_hbuf], writes=[B_halo], partial=True)
                        t3.append((t1, B_t1))
                    if pend:
                        pj, ((ptu, B_ptu), (ptg, B_ptg)) = pend[0]
                        op("act", act(ptg[:, 0:T], ptg[:, 0:T], AF.Silu), reads=[B_ptg], writes=[B_ptg])
                        silu_done.add(pj)
                    for br in range(2):
                        ci = br * NJ + j
                        bi_ = 2 * (j % 2) + br
                        hbuf, B_hbuf = hbufs[bi_], B_hbufs[bi_]
                        t1, B_t1 = t1s[2 * (j % 3) + br], B_t1s[2 * (j % 3) + br]
                        op("dve", lambda e, ci=ci, hbuf=hbuf, t1=t1: e.scalar_tensor_tensor(
                            out=t1[:, 0:T], in0=hbuf[:, 1:1 + T], scalar=convw[:, ci, 1:2], in1=t1[:, 0:T], op0=ALU.mult, op1=ALU.add),
                           reads=[B_hbuf, B_hhalo[bi_], B_cv, B_t1], writes=[B_t1])
                        op("dve", lambda e, ci=ci, hbuf=hbuf, t1=t1: e.scalar_tensor_tensor(
                            out=t1[:, 0:T], in0=hbuf[:, 0:T], scalar=convw[:, ci, 0:1], in1=t1[:, 0:T], op0=ALU.mult, op1=ALU.add),
                           reads=[B_hbuf, B_hhalo[bi_], B_cv, B_t1], writes=[B_t1])
                    pend.append((j, t3))
                    if len(pend) > 1:
                        fin(pend.pop(0))
                while pend:
                    fin(pend.pop(0))
                for bi, (b0, o, B) in enumerate(blks):
                    g = gblk(b0)
                    hb, B_hb = hbs[hcnt % 2], B_hbs[hcnt % 2]
                    hcnt += 1
                    dma("sp", hb[0:B, :], H_d[b0:b0 + B, :], reads=[B_H[g]], writes=[B_hb])
                    if nq:
                        part2.append(nq.pop(0)())
                    for half in range(2):
                        ps, B_ps = PS[half], B_PS[half]
                        for j in range(NJ):
                            op("pe", mm(ps[0:B, :], actT[:, j, o:o + B], R_dn[:, j, half * 512:(half + 1) * 512], j == 0, j == NJ - 1),
                               reads=[B_actT, B_Rdn], writes=[B_ps])
                        op("dve", lambda e, half=half, ps=ps, hb=hb: e.tensor_tensor(
                            out=hb[0:B, half * 512:(half + 1) * 512], in0=ps[0:B, :], in1=hb[0:B, half * 512:(half + 1) * 512], op=ALU.add),
                           reads=[B_ps, B_hb], writes=[B_hb], partial=True)
                    if not last:
                        dma("pool", H_d[b0:b0 + B, :], hb[0:B, :], reads=[B_hb], writes=[B_H[g]])
                    elif ti > 0:
                        st, B_st = ws["st"], ws["B_st"]
                        xn, B_xn = t1s[0][:, :].bitcast(BF16), B_t1s[0]
                        op("act", act(xn[0:B, :], hb[0:B, :], AF.Square, accum_out=st[0:B, 0:1]), reads=[B_hb], writes=[B_xn, B_st])
                        op("act", act(st[0:B, 1:2], st[0:B, 0:1], AF.Ln, bias=EPS, scale=1.0 / D), reads=[B_st], writes=[B_st])
                        op("act", act(st[0:B, 2:3], st[0:B, 1:2], AF.Exp, scale=-0.5), reads=[B_st], writes=[B_st])
                        op("dve", lambda e, hb=hb: e.scalar_tensor_tensor(out=hb[0:B, :], in0=hb[0:B, :], scalar=st[0:B, 2:3],
                                                                        in1=fbc[0:B, :], op0=ALU.mult, op1=ALU.mult),
                           reads=[B_hb, B_st, B_fbc], writes=[B_hb])
                        dma("pool", y_d[b0 - NMETA:b0 - NMETA + B, :], hb[0:B, :], reads=[B_hb], writes=[B_H[g]])
                    if part2:
                        part2.pop(0)()
                while nq or part2:
                    if part2:
                        part2.pop(0)()
                    if nq:
                        part2.append(nq.pop(0)())
            tk.barrier()
        lw.close()
    tk.final_wait()
    if tk.limit is not None:
        print("K_LIMIT", tk.limit, "total ops", tk.nops, "last emitted:", [x for x in tk.log if x[0] in (tk.limit - 1, tk.limit, tk.limit + 1)])


def _consts():
    s = np.arange(128)[:, None]
    t = np.arange(128)[None, :]
    tri = (s <= t).astype(np.float32)
    su = ((s > t) & (s // 64 == t // 64)).astype(np.float32)
    e127 = np.zeros((128, 128), np.float32); e127[127, :] = 1.0
    e15 = np.zeros((128, 128), np.float32); e15[15, :] = 1.0
    ch = (s // 64 == np.arange(8)[None, :]).astype(np.float32)
    cf32 = np.concatenate([tri, su, e127, e15, ch], axis=1)
    ident = np.eye(128, dtype=np.float32)
    ones128 = np.full((128, 128), 1.0 / 128, np.float32)
    wa = np.zeros((128, 128), np.float32); wa[0:64, 0:64] = 1.0 / 64; wa[64:128, 64:128] = 1.0 / 64
    wb = np.zeros((128, 128), np.float32); wb[0:64, 64:128] = EPS / 64; wb[64:128, 64:128] = 1.0 / 64
    cbf = np.concatenate([ident, tri, ones128, wa, wb], axis=1).astype(ml_dtypes.bfloat16)
    return np.ascontiguousarray(cf32), np.ascontiguousarray(cbf)


def make_in_maps(inputs, cores=range(8)):
    f = lambda a: np.ascontiguousarray(np.asarray(a, dtype=np.float32))
    x = f(inputs["x"])
    bc = lambda a: np.ascontiguousarray(np.broadcast_to(f(a)[:, None, :], (a.shape[0], 128, a.shape[1])))
    cf32, cbf = _consts()
    conv_w = f(inputs["conv_w"])
    shared = {
        "meta": f(inputs["meta_tokens"]),
        "w_in": f(inputs["w_in"]), "w_out": f(inputs["w_out"]), "w_up": f(inputs["w_up"]), "w_down": f(inputs["w_down"]),
        "anorm_bc": bc(np.asarray(inputs["attn_norm"])), "fnorm_bc": bc(np.asarray(inputs["ffn_norm"])),
        "final_bc": np.ascontiguousarray(np.broadcast_to(f(inputs["final_norm"])[None, :], (128, D))),
        "wau": f(inputs["w_alpha_up"]),
        "balpha_bc": bc(np.asarray(inputs["b_alpha"])), "bforget_bc": bc(np.asarray(inputs["b_forget"])),
        "gnorm_fm": np.ascontiguousarray(f(inputs["gla_norm"]).reshape(DEPTH, 4, 128).transpose(0, 2, 1)),
        "xnorm_fm": np.ascontiguousarray(f(inputs["fox_norm"]).reshape(DEPTH, 4, 128).transpose(0, 2, 1)),
        "convw_fm": np.ascontiguousarray(conv_w.reshape(DEPTH, 3, 2 * NJ, 128).transpose(0, 3, 2, 1)),
        "convb_fm": np.ascontiguousarray(f(inputs["conv_b"]).reshape(DEPTH, 2 * NJ, 128).transpose(0, 2, 1)),
        "cf32": cf32, "cbf": cbf,
    }
    return [dict(shared, x=np.ascontiguousarray(x[c])) for c in cores]


_NC_CACHE = {}


def kernel(**inputs):
    if "nc" not in _NC_CACHE:
        _NC_CACHE["nc"] = build_nc()
    nc = _NC_CACHE["nc"]
    in_maps = make_in_maps(inputs)
    res = run_bass_kernel_spmd(nc, in_maps, core_ids=list(range(8)))
    return np.stack([np.asarray(r["y"], dtype=np.float32) for r in res.results], axis=0)
```

```python
import contextlib
import numpy as np
import ml_dtypes
import concourse.bass as bass
import concourse.mybir as mybir
from concourse.bass_utils import run_bass_kernel_spmd

F32 = mybir.dt.float32
BF16 = mybir.dt.bfloat16
ALU = mybir.AluOpType
AF = mybir.ActivationFunctionType

D = 1024
SEQ = 4096
NMETA = 16
L = SEQ + NMETA
DEPTH = 4
NIN = 3096
DFF = 2816
NJ = DFF // 128
EPS = 1e-6
C_GQ, C_GK, C_GV, C_GR, C_LR, C_FQ, C_FK, C_FV, C_FF = 0, 256, 512, 1024, 1536, 1552, 2064, 2576, 3088

TILES = [(0, NMETA)] + [(NMETA + 512 * i, 512) for i in range(8)]
NT = len(TILES)
NKB = 33


def blocks_of(t0, T):
    return [(t0 + o, o, min(128, T - o)) for o in range(0, T, 128)]


def gblk(b0):
    return 0 if b0 == 0 else 1 + (b0 - NMETA) // 128


class Buf:
    __slots__ = ("name", "w", "r", "excl")

    def __init__(self, name, excl=False):
        self.name = name
        self.w = {}
        self.r = {}
        self.excl = excl


class _Eng:
    def __init__(self, name, h, sem):
        self.name, self.h, self.sem = name, h, sem
        self.count = 0
        self.seen = {}


class _Slot:
    def __init__(self, sem):
        self.sem = sem
        self.cnt = 0


class Tracker:
    def __init__(self, nc, es, n_work=24, n_wq=46):
        self.nc = nc
        self.e = {}
        for name, h in (("pe", nc.tensor), ("act", nc.scalar), ("dve", nc.vector),
                        ("pool", nc.gpsimd), ("sp", nc.sync)):
            self.e[name] = _Eng(name, h, es.enter_context(nc.semaphore("s_" + name)))
        self.work = [_Slot(es.enter_context(nc.semaphore(f"dw{i}"))) for i in range(n_work)]
        self.wq = [_Slot(es.enter_context(nc.semaphore(f"dq{i}"))) for i in range(n_wq)]
        self.pwork = [_Slot(es.enter_context(nc.semaphore(f"dp{i}"))) for i in range(10)]
        self.wi = 0
        self.qi = 0
        self.pi = 0
        import os, sys
        self.limit = int(os.environ.get("K_LIMIT", "0")) or None
        self.nops = 0
        self.log = []

    def _skip(self, engname):
        self.nops += 1
        if self.limit is not None:
            import sys
            self.log.append((self.nops, engname, sys._getframe(2).f_lineno))
            return self.nops > self.limit
        return False

    def _wait(self, eng, sem, val):
        k = id(sem)
        if eng.seen.get(k, 0) >= val:
            return
        eng.h.wait_ge(sem, val)
        eng.seen[k] = val

    def _deps(self, eng, reads, writes):
        need = {}

        def add(ev, raw):
            sem, val = ev
            if sem is eng.sem:
                if eng.name == "pe":
                    return
            k = id(sem)
            if k not in need or need[k][1] < val:
                need[k] = ev

        for b in reads:
            for ev in b.w.values():
                add(ev, True)
        for b in writes:
            for ev in b.w.values():
                add(ev, False)
            for ev in b.r.values():
                add(ev, False)
        for sem, val in need.values():
            self._wait(eng, sem, val)

    def op(self, engname, fn, reads=(), writes=(), partial=False):
        if self._skip(engname):
            return None
        eng = self.e[engname]
        xs = [b for b in reads if b.excl]
        if xs:
            writes = list(writes) + xs
        self._deps(eng, reads, writes)
        ins = fn(eng.h)
        eng.count += 1
        ins.then_inc(eng.sem, 1)
        ev = (eng.sem, eng.count)
        k = id(eng.sem)
        for b in reads:
            b.r[k] = ev
        for b in writes:
            if partial:
                b.w[k] = ev
            else:
                b.w = {k: ev}
                b.r = {}
        return ins

    def dma(self, qname, out, in_, reads=(), writes=(), weights=False):
        if self._skip("dma_" + qname):
            return None
        eng = self.e[qname]
        if weights:
            slot = self.wq[self.qi % len(self.wq)]
            self.qi += 1
        elif qname == "pool":
            slot = self.pwork[self.pi % len(self.pwork)]
            self.pi += 1
        else:
            slot = self.work[self.wi % len(self.work)]
            self.wi += 1
        self._deps(eng, reads, writes)
        if slot.cnt:
            self._wait(eng, slot.sem, slot.cnt)
        ins = eng.h.dma_start(out=out, in_=in_)
        slot.cnt += 16
        ins.then_inc(slot.sem, 16)
        ev = (slot.sem, slot.cnt)
        k = id(slot.sem)
        for b in reads:
            b.r[k] = ev
        for b in writes:
            b.w = {k: ev}
            b.r = {}

    def barrier(self):
        engs = list(self.e.values())
        for e in engs:
            for o in engs:
                if o is not e and o.count:
                    self._wait(e, o.sem, o.count)
            for s in self.work + self.pwork:
                if s.cnt:
                    self._wait(e, s.sem, s.cnt)

    def final_wait(self):
        sp = self.e["sp"]
        for s in self.work + self.wq + self.pwork:
            if s.cnt:
                self._wait(sp, s.sem, s.cnt)
        for o in self.e.values():
            if o is not sp and o.count:
                self._wait(sp, o.sem, o.count)


def build_nc(n_layers=DEPTH, dbg=False, stop_after=None):
    nc = bass.Bass("TRN2", target_bir_lowering=False)
    es = contextlib.ExitStack()
    with es:
        _build(nc, es, n_layers, dbg, stop_after)
    return nc


def _build(nc, es, n_layers, dbg, stop_after):
    def dram_in(name, shape, dt=F32):
        return nc.dram_tensor(name, list(shape), dt, kind="ExternalInput").ap()

    skind = "ExternalOutput" if dbg else "Internal"

    def dram_s(name, shape, dt):
        return nc.dram_tensor(name, list(shape), dt, kind=skind).ap()

    x_d = dram_in("x", [SEQ, D])
    meta_d = dram_in("meta", [NMETA, D])
    win_d = dram_in("w_in", [DEPTH, D, NIN])
    wout_d = dram_in("w_out", [DEPTH, D, D])
    wup_d = dram_in("w_up", [DEPTH, D, 2 * DFF])
    wdn_d = dram_in("w_down", [DEPTH, DFF, D])
    anorm_d = dram_in("anorm_bc", [DEPTH, 128, D])
    fnorm_d = dram_in("fnorm_bc", [DEPTH, 128, D])
    final_d = dram_in("final_bc", [128, D])
    wau_d = dram_in("wau", [DEPTH, 16, 256])
    balpha_d = dram_in("balpha_bc", [DEPTH, 128, 256])
    bforget_d = dram_in("bforget_bc", [DEPTH, 128, 8])
    gnorm_d = dram_in("gnorm_fm", [DEPTH, 128, 4])
    xnorm_d = dram_in("xnorm_fm", [DEPTH, 128, 4])
    convw_d = dram_in("convw_fm", [DEPTH, 128, 2 * NJ, 3])
    convb_d = dram_in("convb_fm", [DEPTH, 128, 2 * NJ])
    cf32_d = dram_in("cf32", [128, 4 * 128 + 8])
    cbf_d = dram_in("cbf", [128, 5 * 128], BF16)
    y_d = nc.dram_tensor("y", [SEQ, D], F32, kind="ExternalOutput").ap()

    H_d = dram_s("H", [L, D], F32)
    QT_d = dram_s("QT", [4, 128, L], BF16)
    KT_d = dram_s("KT", [4, 128, L], BF16)
    V_d = dram_s("V", [L, 768], BF16)
    CSB_d = dram_s("CSB", [128, NKB, 8], F32)
    CREF_d = dram_s("CREF", [128, NT, 8], F32)
    MIXT_d = dram_s("MIXT", [8, 128, L], BF16)

    tk = Tracker(nc, es)
    op, dma = tk.op, tk.dma

    uid = [0]

    def sb(name, shape, dt, stack=es):
        uid[0] += 1
        return stack.enter_context(nc.sbuf_tensor(f"sb{uid[0]}_{name}", list(shape), dt))

    R_up = sb("R_up", [128, 8 * 2 * DFF], BF16)
    B_Rup = Buf("R_up")
    win_v = R_up[:, 0:8 * NIN].rearrange("p (k n) -> p k n", k=8)
    wup_v = R_up[:, :].rearrange("p (k n) -> p k n", k=8)
    cbf = sb("cbf", [128, 5 * 128], BF16)
    B_const = Buf("const")
    IDENT = cbf[:, 0:128]
    TRIB = cbf[:, 128:256]
    ONES128 = cbf[:, 256:384]
    WA = cbf[:, 384:512]
    WB = cbf[:, 512:640]

    PT = es.enter_context(nc.psum_tensor("PT", [128, 1024], BF16))
    PS = [es.enter_context(nc.psum_tensor(f"PS{i}", [128, 512], F32)) for i in range(7)]
    B_PT = Buf("PT", True)
    B_PS = [Buf(f"PS{i}", True) for i in range(7)]

    dma("sp", cbf[:, :], cbf_d, writes=[B_const])

    B_H = [Buf(f"H{g}") for g in range(NKB)]
    B_QT = [Buf(f"QT{t}") for t in range(NT)]
    B_KT = [Buf(f"KT{t}") for t in range(NT)]
    B_V = [Buf(f"V{g}") for g in range(NKB)]
    B_CSB = Buf("CSB")
    B_CREF = Buf("CREF")
    B_MIXG = [Buf(f"MIXG{t}") for t in range(NT)]
    B_MIXF = [[Buf(f"MIXF{p}_{t}") for t in range(NT)] for p in range(4)]

    def h_rows(layer, b0, B, first_src):
        if layer == 0 and first_src:
            if b0 == 0:
                return meta_d[0:B, :]
            return x_d[b0 - NMETA:b0 - NMETA + B, :]
        return H_d[b0:b0 + B, :]

    def act(out, in_, func, bias=None, scale=None, accum_out=None):
        kw = {}
        if bias is not None:
            kw["bias"] = bias
        if scale is not None:
            kw["scale"] = scale
        if accum_out is not None:
            kw["accum_out"] = accum_out
        return lambda e: e.activation(out=out, in_=in_, func=func, **kw)

    def mm(out, lhsT, rhs, start=True, stop=True):
        return lambda e: e.matmul(out, lhsT=lhsT, rhs=rhs, start=start, stop=stop)

    def norm_transpose(ws, hb, B_hb, gbc, B_g, xnT, B_xnT, o, B):
        norm_part1(ws, hb, B_hb, gbc, B_g, B)
        norm_part2(ws, xnT, B_xnT, o, B)

    def norm_part1(ws, hb, B_hb, gbc, B_g, B):
        xn, B_xn, st, B_st = ws["xn"], ws["B_xn"], ws["st"], ws["B_st"]
        op("act", act(xn[0:B, :], hb[0:B, :], AF.Square, accum_out=st[0:B, 0:1]),
           reads=[B_hb], writes=[B_xn, B_st])
        op("act", act(st[0:B, 1:2], st[0:B, 0:1], AF.Ln, bias=EPS, scale=1.0 / D),
           reads=[B_st], writes=[B_st])
        op("act", act(st[0:B, 2:3], st[0:B, 1:2], AF.Exp, scale=-0.5),
           reads=[B_st], writes=[B_st])
        op("dve", lambda e: e.scalar_tensor_tensor(out=xn[0:B, :], in0=hb[0:B, :], scalar=st[0:B, 2:3],
                                                   in1=gbc[0:B, :], op0=ALU.mult, op1=ALU.mult),
           reads=[B_hb, B_st, B_g], writes=[B_xn])

    def norm_part2(ws, xnT, B_xnT, o, B):
        xn, B_xn = ws["xn"], ws["B_xn"]
        for k in range(8):
            op("pe", lambda e, k=k: e.transpose(out=PT[:, k * 128:k * 128 + B], in_=xn[0:B, k * 128:(k + 1) * 128],
                                                identity=IDENT[0:B, 0:B]),
               reads=[B_xn, B_const], writes=[B_PT])
        src = PT[:, :].rearrange("p (k t) -> p k t", k=8)[:, :, 0:B]
        op("act", lambda e: e.activation(out=xnT[:, :, o:o + B], in_=src, func=AF.Copy),
           reads=[B_PT], writes=[B_xnT], partial=True)

    for layer in range(n_layers):
        last = layer == n_layers - 1
        for k in range(8):
            dma("pool", win_v[:, k, :], win_d[layer, k * 128:(k + 1) * 128, :], writes=[B_Rup], weights=True)

        with contextlib.ExitStack() as pa, nc.named_scope(f"A{layer}"):
            def sba(name, shape, dt):
                return sb(name, shape, dt, pa)
            cf32 = sba("cf32", [128, 4 * 128 + 8], F32)
            B_cf = Buf("cf32")
            dma("sp", cf32[:, :], cf32_d, writes=[B_cf])
            TRI = cf32[:, 0:128]
            SU = cf32[:, 128:256]
            E127 = cf32[:, 256:384]
            E15 = cf32[:, 384:512]
            CH = cf32[:, 512:520]
            gbc = sba("gbc", [128, D], F32); B_gbc = Buf("gbc")
            balpha = sba("balpha", [128, 256], F32)
            bforget = sba("bforget", [128, 8], F32)
            wau32 = sba("wau32", [16, 256], F32)
            wau = sba("wau", [16, 256], BF16)
            gnorm = sba("gnorm", [128, 4], F32)
            B_pp = Buf("layer_params")
            dma("sp", gbc[:, :], anorm_d[layer], writes=[B_gbc])
            dma("sp", balpha[:, :], balpha_d[layer], writes=[B_pp])
            dma("sp", bforget[:, :], bforget_d[layer], writes=[B_pp])
            dma("sp", gnorm[:, :], gnorm_d[layer], writes=[B_pp])
            dma("sp", wau32[:, :], wau_d[layer], writes=[B_pp])
            op("dve", lambda e: e.tensor_copy(out=wau[:, :], in_=wau32[:, :]), reads=[B_pp], writes=[B_pp])

            hbs = [sba(f"hb{i}", [128, D], F32) for i in range(2)]
            B_hbs = [Buf(f"hb{i}") for i in range(2)]
            ws = dict(xn=sba("xn", [128, D], BF16), B_xn=Buf("xn"), st=sba("st", [128, 4], F32), B_st=Buf("st"))
            xnTs = [sba(f"xnT{i}", [128, 8, 512], BF16) for i in range(2)]
            B_xnTs = [Buf(f"xnT{i}") for i in range(2)]
            gqT = sba("gqT", [128, 2, 512], BF16); B_gqT = Buf("gqT")
            sgT = sba("sgT", [128, 4, 512], F32); B_sgT = Buf("sgT")
            glrT = sba("glrT", [16, 512], BF16); B_glrT = Buf("glrT")
            qT = sba("qT", [128, 4, 512], BF16); B_qT = Buf("qT")
            kT = sba("kT", [128, 4, 512], BF16); B_kT = Buf("kT")
            vts = [sba(f"vt{i}", [128, 768], BF16) for i in range(2)]
            B_vts = [Buf(f"vt{i}") for i in range(2)]
            gv = sba("gv", [128, 4, 512], BF16); B_gv = Buf("gv")
            kdec = sba("kdec", [128, 4, 256], BF16); B_kdec = Buf("kdec")
            zb = sba("zb", [128, 256], F32); B_zb = Buf("zb")
            lz = sba("lz", [128, 256], F32); B_lz = Buf("lz")
            wdec = sba("wdec", [128, 256], F32); B_wdec = Buf("wdec")
            f8 = sba("f8", [128, 16], F32); B_f8 = Buf("f8")
            cposs = [sba(f"cpos{i}", [128, 8], F32) for i in range(2)]
            B_cposs = [Buf(f"cpos{i}") for i in range(2)]
            crefs = sba("crefs", [128, NT, 8], F32); B_crefs = Buf("crefs")
            dec = sba("dec", [128, 2, 8], F32); B_dec = Buf("dec")
            S = sba("S", [128, 2, 128], F32); B_S = Buf("S")
            Sbs = [sba(f"Sb{i}", [128, 2, 128], BF16) for i in range(2)]
            B_Sbs = [Buf(f"Sb{i}") for i in range(2)]
            oraw = sba("oraw", [128, 4, 512], F32); B_oraw = Buf("oraw")
            sq = sba("sq", [128, 512], BF16); B_sq = Buf("sq")
            rs = sba("rs", [128, 512], F32); B_rs = Buf("rs")
            tmp = sba("tmp", [128, 512], F32); B_tmp = Buf("tmp")
            mixg = sba("mixg", [128, 4, 512], BF16); B_mixg = Buf("mixg")

            op("pool", lambda e: e.memset(S[:, :, :], 0.0), writes=[B_S])
            for i in range(2):
                op("pool", lambda e, i=i: e.memset(cposs[i][:, :], 0.0), writes=[B_cposs[i]])
            for i in range(2):
                op("pool", lambda e, i=i: e.memset(vts[i][:, :], 1.0), writes=[B_vts[i]])

            pa_i = [0]

            def next_pa():
                i = pa_i[0] % 2
                pa_i[0] += 1
                return PS[i], B_PS[i]
            PM1, B_PM1a, B_PM1b = PS[3], B_PS[3], B_PS[3]
            PM2 = PS[4]
            B_PM2w = B_PM2ff = B_PM2cum = B_PM2cref = B_PM2dec = B_PS[4]
            PU, B_PU = PS[5], [B_PS[5], B_PS[5]]
            PRs, B_PRs = [PS[6], PS[2]], [B_PS[6], B_PS[2]]
            PUv = PU[:, :].rearrange("p (b q c) -> p b q c", b=2, q=2)
            PRvs = [P_[:, :].rearrange("p (h c) -> p h c", h=8) for P_ in PRs]

            nblk_seen = 0
            chunk_g = 0
            for ti, (t0, T) in enumerate(TILES):
                blks = blocks_of(t0, T)
                C = 16 if ti == 0 else 64
                xnT, B_xnT = xnTs[ti % 2], B_xnTs[ti % 2]

                def norm_block(tj, bj):
                    tt0, TT = TILES[tj]
                    b0_, o_, B_ = blocks_of(tt0, TT)[bj]
                    g_ = gblk(b0_)
                    hb, B_hb = hbs[g_ % 2], B_hbs[g_ % 2]
                    dma("sp", hb[0:B_, :], h_rows(layer, b0_, B_, True), reads=[B_H[g_]], writes=[B_hb])
                    norm_part1(ws, hb, B_hb, gbc, B_gbc, B_)
                    return lambda: norm_part2(ws, xnTs[tj % 2], B_xnTs[tj % 2], o_, B_)

                if ti == 0:
                    norm_block(0, 0)()
                def proj_fm(col0, M, evac):
                    ps, B_ps = next_pa()
                    for k in range(8):
                        op("pe", mm(ps[0:M, 0:T], win_v[:, k, col0:col0 + M], xnT[:, k, 0:T], k == 0, k == 7),
                           reads=[B_Rup, B_xnT], writes=[B_ps])
                    evac(ps, B_ps)
                proj_fm(C_LR, 16, lambda ps, B_ps: op(
                    "dve", lambda e: e.tensor_copy(out=glrT[:, 0:T], in_=ps[0:16, 0:T]), reads=[B_ps], writes=[B_glrT]))
                for bi, (b0, o, B) in enumerate(blks):
                    g = gblk(b0)
                    nch = 1 if ti == 0 else 2

                    def proj_tm(ps_ap, B_ps, col0, N):
                        for k in range(8):
                            op("pe", mm(ps_ap, xnT[:, k, o:o + B], win_v[:, k, col0:col0 + N], k == 0, k == 7),
                               reads=[B_Rup, B_xnT], writes=[B_ps])
                    ps, B_ps = next_pa()
                    proj_tm(ps[0:B, 0:512], B_ps, C_FV, 512)
                    vt, B_vt = vts[g % 2], B_vts[g % 2]
                    vt4 = vt[:, :].rearrange("b (p s c) -> b p s c", p=4, s=3)
                    ps4 = ps[:, :].rearrange("b (p e c) -> b p e c", p=4, e=2)
                    for e_ in range(2):
                        op("dve", lambda e, e_=e_: e.tensor_copy(out=vt4[0:B, :, 2 * e_, :], in_=ps4[0:B, :, e_, :]),
                           reads=[B_ps], writes=[B_vt], partial=True)
                    dma("pool", V_d[b0:b0 + B, :], vt[0:B, :], reads=[B_vt], writes=[B_V[g]])
                    ps, B_ps = next_pa()
                    proj_tm(ps[0:B, 0:512], B_ps, C_GV, 512)
                    op("act", act(gv[0:B, bi, :], ps[0:B, 0:512], AF.Copy), reads=[B_ps], writes=[B_gv], partial=True)
                    proj_tm(PM1[0:B, 0:256], B_PM1a, C_GK, 256)
                    proj_tm(PM2[0:B, 256:264], B_PM2ff, C_FF, 8)
                    op("dve", lambda e: e.tensor_tensor(out=f8[0:B, 0:8], in0=PM2[0:B, 256:264], in1=bforget[0:B, :], op=ALU.add),
                       reads=[B_PM2ff, B_pp], writes=[B_f8])
                    op("act", act(f8[0:B, 8:16], f8[0:B, 0:8], AF.Exp, scale=-1.0), reads=[B_f8], writes=[B_f8])
                    op("act", act(f8[0:B, 0:8], f8[0:B, 8:16], AF.Ln, bias=1.0), reads=[B_f8], writes=[B_f8])
                    cur, B_cur = cposs[g % 2], B_cposs[g % 2]
                    prv, B_prv = cposs[(g + 1) % 2], B_cposs[(g + 1) % 2]
                    op("pe", mm(PM2[0:B, 264:272], TRI[0:B, 0:B], f8[0:B, 0:8], True, g == 0),
                       reads=[B_cf, B_f8], writes=[B_PM2cum])
                    if g > 0:
                        Bp = 16 if g == 1 else 128
                        Es = E15 if g == 1 else E127
                        op("pe", mm(PM2[0:B, 264:272], Es[0:Bp, 0:B], prv[0:Bp, :], False, True),
                           reads=[B_cf, B_prv], writes=[B_PM2cum])
                    op("dve", lambda e: e.tensor_copy(out=cur[0:B, :], in_=PM2[0:B, 264:272]), reads=[B_PM2cum], writes=[B_cur])
                    Bw = 128 if g == 0 else B
                    dma("pool", CSB_d[0:Bw, g, :], cur[0:Bw, :], reads=[B_cur], writes=[B_CSB])
                    if (ti == 0 and bi == 0) or (ti > 0 and bi == 1):
                        Es = E15 if ti == 0 else E127
                        op("pe", mm(PM2[:, 272:280], Es[0:B, :], cur[0:B, :]), reads=[B_cf, B_cur], writes=[B_PM2cref])
                        op("dve", lambda e: e.tensor_copy(out=crefs[:, ti, :], in_=PM2[:, 272:280]),
                           reads=[B_PM2cref], writes=[B_crefs], partial=True)
                    op("pe", mm(PM1[0:B, 256:512], glrT[0:16, o:o + B], wau[:, :]), reads=[B_glrT, B_pp], writes=[B_PM1b])
                    op("dve", lambda e: e.tensor_tensor(out=zb[0:B, :], in0=PM1[0:B, 256:512], in1=balpha[0:B, :], op=ALU.add),
                       reads=[B_PM1b, B_pp], writes=[B_zb])
                    op("act", act(zb[0:B, :], zb[0:B, :], AF.Exp, scale=-1.0), reads=[B_zb], writes=[B_zb])
                    op("act", act(lz[0:B, :], zb[0:B, :], AF.Ln, bias=1.0), reads=[B_zb], writes=[B_lz])
                    op("pe", mm(PM2[0:B, 0:256], SU[0:B, 0:B], lz[0:B, :]), reads=[B_cf, B_lz], writes=[B_PM2w])
                    op("act", act(wdec[0:B, :], PM2[0:B, 0:256], AF.Exp, scale=-1.0 / 16), reads=[B_PM2w], writes=[B_wdec])
                    op("dve", lambda e: e.tensor_tensor(out=kdec[0:B, bi, :], in0=PM1[0:B, 0:256], in1=wdec[0:B, :], op=ALU.mult),
                       reads=[B_PM1a, B_wdec], writes=[B_kdec], partial=True)
                    for p in range(2):
                        op("pe", mm(PM2[:, 280 + 8 * p:280 + 8 * p + 8], lz[0:B, p * 128:(p + 1) * 128], CH[0:B, 0:8]),
                           reads=[B_cf, B_lz], writes=[B_PM2dec])
                    PMd = PM2[:, 280:296].rearrange("p (q c) -> p q c", q=2)
                    op("act", act(dec[:, :, 2 * bi:2 * bi + nch], PMd[:, :, 0:nch], AF.Exp, scale=-1.0 / 16),
                       reads=[B_PM2dec], writes=[B_dec], partial=True)
                for j in range(2):
                    proj_fm(C_GQ + 128 * j, 128, lambda ps, B_ps, j=j: op(
                        "act", act(gqT[:, j, 0:T], ps[:, 0:T], AF.Copy, scale=0.125), reads=[B_ps], writes=[B_gqT], partial=True))
                pq = []
                for j in range(4):
                    pq.append(lambda j=j: proj_fm(C_GR + 128 * j, 128, lambda ps, B_ps, j=j: op(
                        "act", act(sgT[:, j, 0:T], ps[:, 0:T], AF.Silu), reads=[B_ps], writes=[B_sgT], partial=True)))
                for j in range(4):
                    pq.append(lambda j=j: proj_fm(C_FQ + 128 * j, 128, lambda ps, B_ps, j=j: op(
                        "dve", lambda e: e.tensor_copy(out=qT[:, j, 0:T], in_=ps[:, 0:T]), reads=[B_ps], writes=[B_qT], partial=True)))
                pq.append(lambda: dma("pool", QT_d[:, :, t0:t0 + T].rearrange("c p t -> p c t"), qT[:, :, 0:T], reads=[B_qT], writes=[B_QT[ti]]))
                for j in range(4):
                    pq.append(lambda j=j: proj_fm(C_FK + 128 * j, 128, lambda ps, B_ps, j=j: op(
                        "dve", lambda e: e.tensor_copy(out=kT[:, j, 0:T], in_=ps[:, 0:T]), reads=[B_ps], writes=[B_kT], partial=True)))
                pq.append(lambda: dma("pool", KT_d[:, :, t0:t0 + T].rearrange("c p t -> p c t"), kT[:, :, 0:T], reads=[B_kT], writes=[B_KT[ti]]))
                nq = []
                part2 = []
                if ti + 1 < NT:
                    for bj in range(len(blocks_of(*TILES[ti + 1]))):
                        nq.append(lambda bj=bj: norm_block(ti + 1, bj))
                nchunks = T // C
                for c in range(nchunks):
                    bi = (c * C) // 128
                    r0 = (c * C) % 128
                    cb = 0
                    for h in range(4):
                        p, e_ = h // 2, h % 2
                        op("pe", mm(PUv[e_ * 64:(e_ + 1) * 64, cb, p, :], kdec[r0:r0 + C, bi, h * 64:(h + 1) * 64],
                                    gv[r0:r0 + C, bi, h * 128:(h + 1) * 128]),
                           reads=[B_kdec, B_gv], writes=[B_PU[cb]])
                    for p in range(2):
                        op("dve", lambda e, p=p: e.scalar_tensor_tensor(out=S[:, p, :], in0=S[:, p, :], scalar=dec[:, p, c:c + 1],
                                                                        in1=PUv[:, cb, p, :], op0=ALU.mult, op1=ALU.add),
                           reads=[B_S, B_dec, B_PU[cb]], writes=[B_S])
                    Sb, B_Sb = Sbs[cb], B_Sbs[cb]
                    op("act", act(Sb[:, :, :], S[:, :, :], AF.Copy), reads=[B_S], writes=[B_Sb])
                    for h in range(4):
                        p, e_ = h // 2, h % 2
                        op("pe", mm(PRvs[e_][:, p, 0:C], Sb[e_ * 64:(e_ + 1) * 64, p, :], gqT[e_ * 64:(e_ + 1) * 64, p, c * C:(c + 1) * C]),
                           reads=[B_Sb, B_gqT], writes=[B_PRs[e_]])
                    orv = oraw[:, :, :].rearrange("p (q e) t -> p q e t", e=2)
                    for e_ in range(2):
                        op("act" if e_ == 0 else "dve",
                           (lambda e, e_=e_: e.activation(out=orv[:, :, e_, c * C:(c + 1) * C], in_=PRvs[e_][:, 0:2, 0:C], func=AF.Copy)) if e_ == 0 else
                           (lambda e, e_=e_: e.tensor_copy(out=orv[:, :, e_, c * C:(c + 1) * C], in_=PRvs[e_][:, 0:2, 0:C])),
                           reads=[B_PRs[e_]], writes=[B_oraw], partial=True)
                    for _ in range(2):
                        if pq:
                            pq.pop(0)()
                    if part2:
                        part2.pop(0)()
                    if nq and (c % 2 == 1 or nchunks == 1):
                        part2.append(nq.pop(0)())
                while pq:
                    pq.pop(0)()
                while nq or part2:
                    if part2:
                        part2.pop(0)()
                    if nq:
                        part2.append(nq.pop(0)())
                for h in range(4):
                    op("act", act(sq[:, 0:T], oraw[:, h, 0:T], AF.Square), reads=[B_oraw], writes=[B_sq])
                    ps, B_ps = next_pa()
                    op("pe", mm(ps[:, 0:T], ONES128, sq[:, 0:T]), reads=[B_const, B_sq], writes=[B_ps])
                    op("act", act(rs[:, 0:T], ps[:, 0:T], AF.Ln, bias=EPS), reads=[B_ps], writes=[B_rs])
                    op("act", act(rs[:, 0:T], rs[:, 0:T], AF.Exp, scale=-0.5), reads=[B_rs], writes=[B_rs])
                    op("dve", lambda e, h=h: e.scalar_tensor_tensor(out=tmp[:, 0:T], in0=oraw[:, h, 0:T], scalar=gnorm[:, h:h + 1],
                                                                    in1=rs[:, 0:T], op0=ALU.mult, op1=ALU.mult),
                       reads=[B_oraw, B_pp, B_rs], writes=[B_tmp])
                    op("dve", lambda e, h=h: e.tensor_tensor(out=mixg[:, h, 0:T], in0=tmp[:, 0:T], in1=sgT[:, h, 0:T], op=ALU.mult),
                       reads=[B_tmp, B_sgT], writes=[B_mixg], partial=True)
                dma("pool", MIXT_d[0:4, :, t0:t0 + T].rearrange("c p t -> p c t"), mixg[:, :, 0:T], reads=[B_mixg], writes=[B_MIXG[ti]])
            dma("pool", CREF_d, crefs[:, :, :], reads=[B_crefs], writes=[B_CREF])
            tk.barrier()
        if stop_after == "A":
            break

        lw = contextlib.ExitStack()
        R_dn = sb("R_dn", [128, NJ, D], BF16, lw); B_Rdn = Buf("R_dn")
        lo = contextlib.ExitStack()
        R_out = sb("R_out", [128, 8, D], BF16, lo); B_Rout = Buf("R_out")
        for k in range(8):
            dma("pool", R_out[:, k, :], wout_d[layer, k * 128:(k + 1) * 128, :], writes=[B_Rout], weights=True)
        for k in range(8):
            dma("pool", wup_v[:, k, :], wup_d[layer, k * 128:(k + 1) * 128, :], writes=[B_Rup], weights=True)
        for j in range(NJ):
            dma("pool", R_dn[:, j, :], wdn_d[layer, j * 128:(j + 1) * 128, :], writes=[B_Rdn], weights=True)

        with contextlib.ExitStack() as pb, nc.named_scope(f"B{layer}"):
            def sbb(name, shape, dt):
                return sb(name, shape, dt, pb)
            KTs = [sbb("KTs0", [128, L], BF16)] * 2
            B_KTs = [Buf("KTs0")] * 2
            Vps = [sbb(f"Vp{i}", [128, NKB, 192], BF16) for i in range(2)]
            B_Vps = [Buf(f"Vp{i}") for i in range(2)]
            csb = sbb("csb", [128, NKB, 8], F32); B_csb = Buf("csb")
            cref = sbb("cref", [128, NT, 8], F32); B_cref = Buf("cref")
            xnorm = sbb("xnorm", [128, 4], F32); B_xnorm = Buf("xnorm")
            biases = [sbb(f"bias{i}", [128, NKB, 2], F32) for i in range(2)]
            B_biases = [Buf(f"bias{i}") for i in range(2)]
            qts = [sbb(f"qt{i}", [128, 512], BF16) for i in range(2)]
            B_qts = [Buf(f"qt{i}") for i in range(2)]
            pts = [sbb(f"pt{i}", [128, 512], BF16) for i in range(4)]
            B_pts = [Buf(f"pt{i}") for i in range(4)]
            pcs = [sbb(f"pc{i}", [128, 512], F32) for i in range(2)]
            B_pcs = [Buf(f"pc{i}") for i in range(2)]
            sqs = [sbb("sqb0", [128, 512], BF16)]
            B_sqs = [Buf("sqb0")]
            rd = sbb("rd", [128, 512], F32); B_rd = Buf("rd")
            onorm = sbb("onorm", [128, 512], F32); B_on = Buf("onorm")
            rsb = sbb("rsb", [128, 512], F32); B_rsb = Buf("rsb")
            ots = [sbb(f"ot{i}", [128, 512], BF16) for i in range(2)]
            B_ots = [Buf(f"ot{i}") for i in range(2)]

            dma("sp", csb[:, :, :], CSB_d, reads=[B_CSB], writes=[B_csb])
            dma("sp", cref[:, :, :], CREF_d, reads=[B_CREF], writes=[B_cref])
            dma("sp", xnorm[:, :], xnorm_d[layer], writes=[B_xnorm])
            SBK = [[(PS[0], B_PS[0]), (PS[1], B_PS[1])], [(PS[2], B_PS[2]), (PS[6], B_PS[6])]]
            iters = [(p, ti) for p in range(4) for ti in range(NT)]

            def kbs_of(ti):
                if ti == 0:
                    return [(0, 0, 16)]
                return [(0, 0, 16)] + [(1 + j, NMETA + 128 * j, 128) for j in range(4 * ti)]

            def load_kt(p):
                dma("sp", KTs[p % 2][:, :], KT_d[p], reads=B_KT, writes=[B_KTs[p % 2]])

            def load_pair(p):
                Vp, B_Vp = Vps[p % 2], B_Vps[p % 2]
                dma("sp", Vp[0:16, 0, :], V_d[0:16, 192 * p:192 * p + 192], reads=B_V, writes=[B_Vp])
                dma("sp", Vp[:, 1:NKB, :], V_d[16:L, 192 * p:192 * p + 192].rearrange("(n q) c -> q n c", q=128),
                    reads=B_V, writes=[B_Vp])

            def load_q(n):
                p, ti = iters[n]
                t0, T = TILES[ti]
                nkb = len(kbs_of(ti))
                dma("sp", qts[n % 2][:, 0:T], QT_d[p, :, t0:t0 + T], reads=[B_QT[ti]], writes=[B_qts[n % 2]])
                op("dve", lambda e: e.tensor_tensor(
                    out=biases[n % 2][:, 0:nkb, :], in0=csb[:, 0:nkb, 2 * p:2 * p + 2],
                    in1=cref[:, ti:ti + 1, 2 * p:2 * p + 2].to_broadcast([128, nkb, 2]), op=ALU.subtract),
                   reads=[B_csb, B_cref], writes=[B_biases[n % 2]])

            def epilogue_dve(n):
                p, ti = iters[n]
                t0, T = TILES[ti]
                for e_ in range(2):
                    orow = slice(e_ * 64, (e_ + 1) * 64)
                    drow = slice((1 - e_) * 64, (2 - e_) * 64)
                    op("dve", lambda e: e.reciprocal(out=rd[orow, 0:T], in_=pcs[e_][drow, 0:T]), reads=[B_pcs[e_]], writes=[B_rd], partial=True)
                    op("dve", lambda e: e.tensor_tensor(out=onorm[orow, 0:T], in0=pcs[e_][orow, 0:T], in1=rd[orow, 0:T], op=ALU.mult),
                       reads=[B_pcs[e_], B_rd], writes=[B_on], partial=True)

            def epilogue(n):
                p, ti = iters[n]
                t0, T = TILES[ti]
                ot, B_ot = ots[n % 2], B_ots[n % 2]
                op("act", act(sqs[0][:, 0:T], onorm[:, 0:T], AF.Square), reads=[B_on], writes=[B_sqs[0]])
                pst, B_pst = PS[5], B_PS[5]
                op("pe", mm(pst[:, 0:T], WA, sqs[0][:, 0:T]), reads=[B_const, B_sqs[0]], writes=[B_pst])
                op("act", act(rsb[:, 0:T], pst[:, 0:T], AF.Ln, bias=EPS), reads=[B_pst], writes=[B_rsb])
                op("act", act(rsb[:, 0:T], rsb[:, 0:T], AF.Exp, scale=-0.5), reads=[B_rsb], writes=[B_rsb])
                op("dve", lambda e: e.scalar_tensor_tensor(
                    out=ot[:, 0:T], in0=onorm[:, 0:T], scalar=xnorm[:, p:p + 1],
                    in1=rsb[:, 0:T], op0=ALU.mult, op1=ALU.mult),
                   reads=[B_on, B_xnorm, B_rsb], writes=[B_ot])
                dma("sp", MIXT_d[4 + p, :, t0:t0 + T], ot[:, 0:T], reads=[B_ot], writes=[B_MIXF[p][ti]])

            load_kt(0)
            load_pair(0)
            load_q(0)
            deferred = []
            for n, (p, ti) in enumerate(iters):
                t0, T = TILES[ti]
                KTp, B_KTp = KTs[p % 2], B_KTs[p % 2]
                Vp, B_Vp = Vps[p % 2], B_Vps[p % 2]
                qt, B_qt = qts[n % 2], B_qts[n % 2]
                bias, B_bias = biases[n % 2], B_biases[n % 2]
                kbs = kbs_of(ti)
                nkb = len(kbs)

                def emit_s(idx):
                    g, k0, KB = kbs[idx]
                    qa = max(0, k0 - t0)
                    for e_ in range(2):
                        rows = slice(e_ * 64, (e_ + 1) * 64)
                        ps, B_ps = SBK[e_][idx % 2]
                        op("pe", mm(ps[0:KB, qa:T], KTp[rows, k0:k0 + KB], qt[rows, qa:T]),
                           reads=[B_KTp, B_qt], writes=[B_ps])

                emit_s(0)
                if n + 1 < len(iters):
                    if iters[n + 1][0] != p:
                        load_pair(iters[n + 1][0])
                    load_q(n + 1)
                for idx in range(nkb):
                    g, k0, KB = kbs[idx]
                    qa = max(0, k0 - t0)
                    diag = k0 + KB > t0
                    if idx + 1 < nkb:
                        emit_s(idx + 1)
                    for e_ in range(2):
                        ps, B_ps = SBK[e_][idx % 2]
                        pt, B_pt = pts[2 * e_ + idx % 2], B_pts[2 * e_ + idx % 2]
                        po, B_po = PS[3 + e_], B_PS[3 + e_]
                        op("act", act(pt[0:KB, qa:T], ps[0:KB, qa:T], AF.Exp, bias=bias[0:KB, g, e_:e_ + 1], scale=0.125),
                           reads=[B_ps, B_bias], writes=[B_pt])
                        if diag:
                            op("dve", lambda e: e.tensor_tensor(out=pt[0:KB, qa:qa + KB], in0=pt[0:KB, qa:qa + KB],
                                                                in1=TRIB[0:KB, 0:KB], op=ALU.mult),
                               reads=[B_pt, B_const], writes=[B_pt])
                        op("pe", mm(po[:, qa:T], Vp[0:KB, g, e_ * 64:e_ * 64 + 128], pt[0:KB, qa:T], idx == 0, idx == nkb - 1),
                           reads=[B_Vp, B_pt], writes=[B_po])
                    if deferred and idx == min(7, nkb - 1):
                        epilogue(deferred.pop(0))
                for e_ in range(2):
                    op("dve", lambda e, e_=e_: e.tensor_copy(out=pcs[e_][:, 0:T], in_=PS[3 + e_][:, 0:T]),
                       reads=[B_PS[3 + e_]], writes=[B_pcs[e_]])
                epilogue_dve(n)
                deferred.append(n)
                if n + 1 < len(iters) and iters[n + 1][0] != p:
                    load_kt(iters[n + 1][0])
            while deferred:
                epilogue(deferred.pop(0))
            tk.barrier()
        if stop_after == "B":
            lo.close()
            lw.close()
            break

        with contextlib.ExitStack() as pc, nc.named_scope(f"C{layer}"):
            def sbc(name, shape, dt):
                return sb(name, shape, dt, pc)
            mixts = [sbc(f"mixt{i}", [128, 8, 512], BF16) for i in range(2)]
            B_mixts = [Buf(f"mixt{i}") for i in range(2)]
            hbs = [sbc(f"hbc{i}", [128, D], F32) for i in range(2)]
            B_hbs = [Buf(f"hbc{i}") for i in range(2)]
            for ti, (t0, T) in enumerate(TILES):
                mt, B_mt = mixts[ti % 2], B_mixts[ti % 2]
                dma("sp", mt[:, :, 0:T], MIXT_d[:, :, t0:t0 + T].rearrange("c p t -> p c t"),
                    reads=[B_MIXG[ti]] + [B_MIXF[p][ti] for p in range(4)], writes=[B_mt])
                for bi, (b0, o, B) in enumerate(blocks_of(t0, T)):
                    g = gblk(b0)
                    hb, B_hb = hbs[g % 2], B_hbs[g % 2]
                    dma("sp", hb[0:B, :], h_rows(layer, b0, B, True), reads=[B_H[g]], writes=[B_hb])
                    for half in range(2):
                        ps, B_ps = PS[half], B_PS[half]
                        for c in range(8):
                            op("pe", mm(ps[0:B, :], mt[:, c, o:o + B], R_out[:, c, half * 512:(half + 1) * 512], c == 0, c == 7),
                               reads=[B_mt, B_Rout], writes=[B_ps])
                        op("dve", lambda e, half=half, ps=ps: e.tensor_tensor(
                            out=hb[0:B, half * 512:(half + 1) * 512], in0=ps[0:B, :], in1=hb[0:B, half * 512:(half + 1) * 512], op=ALU.add),
                           reads=[B_ps, B_hb], writes=[B_hb], partial=True)
                    dma("pool", H_d[b0:b0 + B, :], hb[0:B, :], reads=[B_hb], writes=[B_H[g]])
            tk.barrier()
        lo.close()
        if stop_after == "C":
            lw.close()
            break

        with contextlib.ExitStack() as pd, nc.named_scope(f"D{layer}"):
            def sbd(name, shape, dt):
                return sb(name, shape, dt, pd)
            g2bc = sbd("g2bc", [128, D], F32); B_g2 = Buf("g2bc")
            convw = sbd("convw", [128, 2 * NJ, 3], F32)
            convb = sbd("convb", [128, 2 * NJ], F32)
            B_cv = Buf("conv")
            dma("sp", g2bc[:, :], fnorm_d[layer], writes=[B_g2])
            dma("sp", convw[:, :, :], convw_d[layer], writes=[B_cv])
            dma("sp", convb[:, :], convb_d[layer], writes=[B_cv])
            if last:
                fbc = sbd("fbc", [128, D], F32); B_fbc = Buf("fbc")
                dma("sp", fbc[:, :], final_d, writes=[B_fbc])
            halo = sbd("halo", [128, 2 * NJ, 2], F32); B_halo = Buf("halo")
            op("pool", lambda e: e.memset(halo[:, :, :], 0.0), writes=[B_halo])
            hbs = [sbd(f"hbd{i}", [128, D], F32) for i in range(2)]
            B_hbs = [Buf(f"hbd{i}") for i in range(2)]
            if last:
                hbx = sbd("hbx", [128, D], F32)
                B_hbx = Buf("hbx")
                hbn, B_hbn = [hbx, hbx], [B_hbx, B_hbx]
            else:
                hbn = [sbd(f"hbn{i}", [128, D], F32) for i in range(2)]
                B_hbn = [Buf(f"hbn{i}") for i in range(2)]
            ws = dict(xn=sbd("xnd", [128, D], BF16), B_xn=Buf("xnd"), st=sbd("std", [128, 4], F32), B_st=Buf("std"))
            xnT = sbd("xnTd", [128, 8, 512], BF16); B_xnT = Buf("xnTd")
            hbufs = [sbd(f"hbuf{i}", [128, 514], F32) for i in range(4)]
            B_hbufs = [Buf(f"hbuf{i}") for i in range(4)]
            B_hhalo = [Buf(f"hhalo{i}") for i in range(4)]
            t1s = [sbd(f"t1_{i}", [128, 512], F32) for i in range(6)]
            B_t1s = [Buf(f"t1_{i}") for i in range(6)]
            actT = sbd("actT", [128, NJ, 512], BF16); B_actT = Buf("actT")
            hcnt = 0
            for ti, (t0, T) in enumerate(TILES):
                blks = blocks_of(t0, T)

                def norm_block_d(tj, bj):
                    tt0, TT = TILES[tj]
                    b0_, o_, B_ = blocks_of(tt0, TT)[bj]
                    g_ = gblk(b0_)
                    hb, B_hb = hbn[g_ % 2], B_hbn[g_ % 2]
                    dma("sp", hb[0:B_, :], H_d[b0_:b0_ + B_, :], reads=[B_H[g_]], writes=[B_hb])
                    norm_part1(ws, hb, B_hb, g2bc, B_g2, B_)
                    return lambda: norm_part2(ws, xnT, B_xnT, o_, B_)

                if ti == 0:
                    norm_block_d(0, 0)()
                nq = []
                part2 = []
                if ti + 1 < NT:
                    for bj in range(len(blocks_of(*TILES[ti + 1]))):
                        nq.append(lambda bj=bj: norm_block_d(ti + 1, bj))
                pend = []
                silu_done = set()

                def fin(item):
                    pj, ((tu, B_tu), (tg, B_tg)) = item
                    if pj not in silu_done:
                        op("act", act(tg[:, 0:T], tg[:, 0:T], AF.Silu), reads=[B_tg], writes=[B_tg])
                    op("dve", lambda e: e.tensor_tensor(out=actT[:, pj, 0:T], in0=tu[:, 0:T], in1=tg[:, 0:T], op=ALU.mult),
                       reads=[B_tu, B_tg], writes=[B_actT], partial=True)

                for j in range(NJ):
                    pss = []
                    for br in range(2):
                        ps, B_ps = PS[2 + 2 * (j % 2) + br], B_PS[2 + 2 * (j % 2) + br]
                        col0 = br * DFF + j * 128
                        for k in range(8):
                            op("pe", mm(ps[:, 0:T], wup_v[:, k, col0:col0 + 128], xnT[:, k, 0:T], k == 0, k == 7),
                               reads=[B_Rup, B_xnT], writes=[B_ps])
                        pss.append((ps, B_ps))
                    t3 = []
                    for br in range(2):
                        ps, B_ps = pss[br]
                        ci = br * NJ + j
                        bi_ = 2 * (j % 2) + br
                        hbuf, B_hbuf = hbufs[bi_], B_hbufs[bi_]
                        t1, B_t1 = t1s[2 * (j % 3) + br], B_t1s[2 * (j % 3) + br]
                        op("pool", lambda e, ci=ci, hbuf=hbuf: e.tensor_copy(out=hbuf[:, 0:2], in_=halo[:, ci, :]),
                           reads=[B_halo], writes=[B_hhalo[bi_]])
                        op("act", act(hbuf[:, 2:2 + T], ps[:, 0:T], AF.Copy), reads=[B_ps], writes=[B_hbuf])
                        op("act", act(t1[:, 0:T], ps[:, 0:T], AF.Identity, bias=convb[:, ci:ci + 1], scale=convw[:, ci, 2:3]),
                           reads=[B_ps, B_cv], writes=[B_t1])
                        op("pool", lambda e, ci=ci, hbuf=hbuf: e.tensor_copy(out=halo[:, ci, :], in_=hbuf[:, T:T + 2]),
                           reads=[B_hbuf], writes=[B_halo], partial=True)
                        t3.append((t1, B_t1))
                    if pend:
                        pj, ((ptu, B_ptu), (ptg, B_ptg)) = pend[0]
                        op("act", act(ptg[:, 0:T], ptg[:, 0:T], AF.Silu), reads=[B_ptg], writes=[B_ptg])
                        silu_done.add(pj)
                    for br in range(2):
                        ci = br * NJ + j
                        bi_ = 2 * (j % 2) + br
                        hbuf, B_hbuf = hbufs[bi_], B_hbufs[bi_]
                        t1, B_t1 = t1s[2 * (j % 3) + br], B_t1s[2 * (j % 3) + br]
                        op("dve", lambda e, ci=ci, hbuf=hbuf, t1=t1: e.scalar_tensor_tensor(
                            out=t1[:, 0:T], in0=hbuf[:, 1:1 + T], scalar=convw[:, ci, 1:2], in1=t1[:, 0:T], op0=ALU.mult, op1=ALU.add),
                           reads=[B_hbuf, B_hhalo[bi_], B_cv, B_t1], writes=[B_t1])
                        op("dve", lambda e, ci=ci, hbuf=hbuf, t1=t1: e.scalar_tensor_tensor(
                            out=t1[:, 0:T], in0=hbuf[:, 0:T], scalar=convw[:, ci, 0:1], in1=t1[:, 0:T], op0=ALU.mult, op1=ALU.add),
                           reads=[B_hbuf, B_hhalo[bi_], B_cv, B_t1], writes=[B_t1])
                    pend.append((j, t3))
                    if len(pend) > 1:
                        fin(pend.pop(0))
                while pend:
                    fin(pend.pop(0))
                for bi, (b0, o, B) in enumerate(blks):
                    g = gblk(b0)
                    hb, B_hb = hbs[hcnt % 2], B_hbs[hcnt % 2]
                    hcnt += 1
                    dma("sp", hb[0:B, :], H_d[b0:b0 + B, :], reads=[B_H[g]], writes=[B_hb])
                    if nq:
                        part2.append(nq.pop(0)())
                    for half in range(2):
                        ps, B_ps = PS[half], B_PS[half]
                        for j in range(NJ):
                            op("pe", mm(ps[0:B, :], actT[:, j, o:o + B], R_dn[:, j, half * 512:(half + 1) * 512], j == 0, j == NJ - 1),
                               reads=[B_actT, B_Rdn], writes=[B_ps])
                        op("dve", lambda e, half=half, ps=ps, hb=hb: e.tensor_tensor(
                            out=hb[0:B, half * 512:(half + 1) * 512], in0=ps[0:B, :], in1=hb[0:B, half * 512:(half + 1) * 512], op=ALU.add),
                           reads=[B_ps, B_hb], writes=[B_hb], partial=True)
                    if not last:
                        dma("pool", H_d[b0:b0 + B, :], hb[0:B, :], reads=[B_hb], writes=[B_H[g]])
                    elif ti > 0:
                        st, B_st = ws["st"], ws["B_st"]
                        xn, B_xn = t1s[0][:, :].bitcast(BF16), B_t1s[0]
                        op("act", act(xn[0:B, :], hb[0:B, :], AF.Square, accum_out=st[0:B, 0:1]), reads=[B_hb], writes=[B_xn, B_st])
                        op("act", act(st[0:B, 1:2], st[0:B, 0:1], AF.Ln, bias=EPS, scale=1.0 / D), reads=[B_st], writes=[B_st])
                        op("act", act(st[0:B, 2:3], st[0:B, 1:2], AF.Exp, scale=-0.5), reads=[B_st], writes=[B_st])
                        op("dve", lambda e, hb=hb: e.scalar_tensor_tensor(out=hb[0:B, :], in0=hb[0:B, :], scalar=st[0:B, 2:3],
                                                                        in1=fbc[0:B, :], op0=ALU.mult, op1=ALU.mult),
                           reads=[B_hb, B_st, B_fbc], writes=[B_hb])
                        dma("pool", y_d[b0 - NMETA:b0 - NMETA + B, :], hb[0:B, :], reads=[B_hb], writes=[B_H[g]])
                    if part2:
                        part2.pop(0)()
                while nq or part2:
                    if part2:
                        part2.pop(0)()
                    if nq:
                        part2.append(nq.pop(0)())
            tk.barrier()
        lw.close()
    tk.final_wait()
    if tk.limit is not None:
        print("K_LIMIT", tk.limit, "total ops", tk.nops, "last emitted:", [x for x in tk.log if x[0] in (tk.limit - 1, tk.limit, tk.limit + 1)])


def _consts():
    s = np.arange(128)[:, None]
    t = np.arange(128)[None, :]
    tri = (s <= t).astype(np.float32)
    su = ((s > t) & (s // 64 == t // 64)).astype(np.float32)
    e127 = np.zeros((128, 128), np.float32); e127[127, :] = 1.0
    e15 = np.zeros((128, 128), np.float32); e15[15, :] = 1.0
    ch = (s // 64 == np.arange(8)[None, :]).astype(np.float32)
    cf32 = np.concatenate([tri, su, e127, e15, ch], axis=1)
    ident = np.eye(128, dtype=np.float32)
    ones128 = np.full((128, 128), 1.0 / 128, np.float32)
    wa = np.zeros((128, 128), np.float32); wa[0:64, 0:64] = 1.0 / 64; wa[64:128, 64:128] = 1.0 / 64
    wb = np.zeros((128, 128), np.float32); wb[0:64, 64:128] = EPS / 64; wb[64:128, 64:128] = 1.0 / 64
    cbf = np.concatenate([ident, tri, ones128, wa, wb], axis=1).astype(ml_dtypes.bfloat16)
    return np.ascontiguousarray(cf32), np.ascontiguousarray(cbf)


def make_in_maps(inputs, cores=range(8)):
    f = lambda a: np.ascontiguousarray(np.asarray(a, dtype=np.float32))
    x = f(inputs["x"])
    bc = lambda a: np.ascontiguousarray(np.broadcast_to(f(a)[:, None, :], (a.shape[0], 128, a.shape[1])))
    cf32, cbf = _consts()
    conv_w = f(inputs["conv_w"])
    shared = {
        "meta": f(inputs["meta_tokens"]),
        "w_in": f(inputs["w_in"]), "w_out": f(inputs["w_out"]), "w_up": f(inputs["w_up"]), "w_down": f(inputs["w_down"]),
        "anorm_bc": bc(np.asarray(inputs["attn_norm"])), "fnorm_bc": bc(np.asarray(inputs["ffn_norm"])),
        "final_bc": np.ascontiguousarray(np.broadcast_to(f(inputs["final_norm"])[None, :], (128, D))),
        "wau": f(inputs["w_alpha_up"]),
        "balpha_bc": bc(np.asarray(inputs["b_alpha"])), "bforget_bc": bc(np.asarray(inputs["b_forget"])),
        "gnorm_fm": np.ascontiguousarray(f(inputs["gla_norm"]).reshape(DEPTH, 4, 128).transpose(0, 2, 1)),
        "xnorm_fm": np.ascontiguousarray(f(inputs["fox_norm"]).reshape(DEPTH, 4, 128).transpose(0, 2, 1)),
        "convw_fm": np.ascontiguousarray(conv_w.reshape(DEPTH, 3, 2 * NJ, 128).transpose(0, 3, 2, 1)),
        "convb_fm": np.ascontiguousarray(f(inputs["conv_b"]).reshape(DEPTH, 2 * NJ, 128).transpose(0, 2, 1)),
        "cf32": cf32, "cbf": cbf,
    }
    return [dict(shared, x=np.ascontiguousarray(x[c])) for c in cores]


_NC_CACHE = {}


def kernel(**inputs):
    if "nc" not in _NC_CACHE:
        _NC_CACHE["nc"] = build_nc()
    nc = _NC_CACHE["nc"]
    in_maps = make_in_maps(inputs)
    res = run_bass_kernel_spmd(nc, in_maps, core_ids=list(range(8)))
    return np.stack([np.asarray(r["y"], dtype=np.float32) for r in res.results], axis=0)
```

```python
import contextlib
import numpy as np
import ml_dtypes
import concourse.bass as bass
import concourse.mybir as mybir
from concourse.bass_utils import run_bass_kernel_spmd

F32 = mybir.dt.float32
BF16 = mybir.dt.bfloat16
ALU = mybir.AluOpType
AF = mybir.ActivationFunctionType

D = 1024
SEQ = 4096
NMETA = 16
L = SEQ + NMETA
DEPTH = 4
NIN = 3096
DFF = 2816
NJ = DFF // 128
EPS = 1e-6
C_GQ, C_GK, C_GV, C_GR, C_LR, C_FQ, C_FK, C_FV, C_FF = 0, 256, 512, 1024, 1536, 1552, 2064, 2576, 3088

TILES = [(0, NMETA)] + [(NMETA + 512 * i, 512) for i in range(8)]
NT = len(TILES)
NKB = 33


def blocks_of(t0, T):
    return [(t0 + o, o, min(128, T - o)) for o in range(0, T, 128)]


def gblk(b0):
    return 0 if b0 == 0 else 1 + (b0 - NMETA) // 128


class Buf:
    __slots__ = ("name", "w", "r", "excl")

    def __init__(self, name, excl=False):
        self.name = name
        self.w = {}
        self.r = {}
        self.excl = excl


class _Eng:
    def __init__(self, name, h, sem):
        self.name, self.h, self.sem = name, h, sem
        self.count = 0
        self.seen = {}


class _Slot:
    def __init__(self, sem):
        self.sem = sem
        self.cnt = 0


class Tracker:
    def __init__(self, nc, es, n_work=24, n_wq=46):
        self.nc = nc
        self.e = {}
        for name, h in (("pe", nc.tensor), ("act", nc.scalar), ("dve", nc.vector),
                        ("pool", nc.gpsimd), ("sp", nc.sync)):
            self.e[name] = _Eng(name, h, es.enter_context(nc.semaphore("s_" + name)))
        self.work = [_Slot(es.enter_context(nc.semaphore(f"dw{i}"))) for i in range(n_work)]
        self.wq = [_Slot(es.enter_context(nc.semaphore(f"dq{i}"))) for i in range(n_wq)]
        self.pwork = [_Slot(es.enter_context(nc.semaphore(f"dp{i}"))) for i in range(10)]
        self.wi = 0
        self.qi = 0
        self.pi = 0
        import os, sys
        self.limit = int(os.environ.get("K_LIMIT", "0")) or None
        self.nops = 0
        self.log = []

    def _skip(self, engname):
        self.nops += 1
        if self.limit is not None:
            import sys
            self.log.append((self.nops, engname, sys._getframe(2).f_lineno))
            return self.nops > self.limit
        return False

    def _wait(self, eng, sem, val):
        k = id(sem)
        if eng.seen.get(k, 0) >= val:
            return
        eng.h.wait_ge(sem, val)
        eng.seen[k] = val

    def _deps(self, eng, reads, writes):
        need = {}

        def add(ev, raw):
            sem, val = ev
            if sem is eng.sem:
                if eng.name == "pe":
                    return
            k = id(sem)
            if k not in need or need[k][1] < val:
                need[k] = ev

        for b in reads:
            for ev in b.w.values():
                add(ev, True)
        for b in writes:
            for ev in b.w.values():
                add(ev, False)
            for ev in b.r.values():
                add(ev, False)
        for sem, val in need.values():
            self._wait(eng, sem, val)

    def op(self, engname, fn, reads=(), writes=(), partial=False):
        if self._skip(engname):
            return None
        eng = self.e[engname]
        xs = [b for b in reads if b.excl]
        if xs:
            writes = list(writes) + xs
        self._deps(eng, reads, writes)
        ins = fn(eng.h)
        eng.count += 1
        ins.then_inc(eng.sem, 1)
        ev = (eng.sem, eng.count)
        k = id(eng.sem)
        for b in reads:
            b.r[k] = ev
        for b in writes:
            if partial:
                b.w[k] = ev
            else:
                b.w = {k: ev}
                b.r = {}
        return ins

    def dma(self, qname, out, in_, reads=(), writes=(), weights=False):
        if self._skip("dma_" + qname):
            return None
        eng = self.e[qname]
        if weights:
            slot = self.wq[self.qi % len(self.wq)]
            self.qi += 1
        elif qname == "pool":
            slot = self.pwork[self.pi % len(self.pwork)]
            self.pi += 1
        else:
            slot = self.work[self.wi % len(self.work)]
            self.wi += 1
        self._deps(eng, reads, writes)
        if slot.cnt:
            self._wait(eng, slot.sem, slot.cnt)
        ins = eng.h.dma_start(out=out, in_=in_)
        slot.cnt += 16
        ins.then_inc(slot.sem, 16)
        ev = (slot.sem, slot.cnt)
        k = id(slot.sem)
        for b in reads:
            b.r[k] = ev
        for b in writes:
            b.w = {k: ev}
            b.r = {}

    def barrier(self):
        engs = list(self.e.values())
        for e in engs:
            for o in engs:
                if o is not e and o.count:
                    self._wait(e, o.sem, o.count)
            for s in self.work + self.pwork:
                if s.cnt:
                    self._wait(e, s.sem, s.cnt)

    def final_wait(self):
        sp = self.e["sp"]
        for s in self.work + self.wq + self.pwork:
            if s.cnt:
                self._wait(sp, s.sem, s.cnt)
        for o in self.e.values():
            if o is not sp and o.count:
                self._wait(sp, o.sem, o.count)


def build_nc(n_layers=DEPTH, dbg=False, stop_after=None):
    nc = bass.Bass("TRN2", target_bir_lowering=False)
    es = contextlib.ExitStack()
    with es:
        _build(nc, es, n_layers, dbg, stop_after)
    return nc


def _build(nc, es, n_layers, dbg, stop_after):
    def dram_in(name, shape, dt=F32):
        return nc.dram_tensor(name, list(shape), dt, kind="ExternalInput").ap()

    skind = "ExternalOutput" if dbg else "Internal"

    def dram_s(name, shape, dt):
        return nc.dram_tensor(name, list(shape), dt, kind=skind).ap()

    x_d = dram_in("x", [SEQ, D])
    meta_d = dram_in("meta", [NMETA, D])
    win_d = dram_in("w_in", [DEPTH, D, NIN])
    wout_d = dram_in("w_out", [DEPTH, D, D])
    wup_d = dram_in("w_up", [DEPTH, D, 2 * DFF])
    wdn_d = dram_in("w_down", [DEPTH, DFF, D])
    anorm_d = dram_in("anorm_bc", [DEPTH, 128, D])
    fnorm_d = dram_in("fnorm_bc", [DEPTH, 128, D])
    final_d = dram_in("final_bc", [128, D])
    wau_d = dram_in("wau", [DEPTH, 16, 256])
    balpha_d = dram_in("balpha_bc", [DEPTH, 128, 256])
    bforget_d = dram_in("bforget_bc", [DEPTH, 128, 8])
    gnorm_d = dram_in("gnorm_fm", [DEPTH, 128, 4])
    xnorm_d = dram_in("xnorm_fm", [DEPTH, 128, 4])
    convw_d = dram_in("convw_fm", [DEPTH, 128, 2 * NJ, 3])
    convb_d = dram_in("convb_fm", [DEPTH, 128, 2 * NJ])
    cf32_d = dram_in("cf32", [128, 4 * 128 + 8])
    cbf_d = dram_in("cbf", [128, 5 * 128], BF16)
    y_d = nc.dram_tensor("y", [SEQ, D], F32, kind="ExternalOutput").ap()

    H_d = dram_s("H", [L, D], F32)
    QT_d = dram_s("QT", [4, 128, L], BF16)
    KT_d = dram_s("KT", [4, 128, L], BF16)
    V_d = dram_s("V", [L, 768], BF16)
    CSB_d = dram_s("CSB", [128, NKB, 8], F32)
    CREF_d = dram_s("CREF", [128, NT, 8], F32)
    MIXT_d = dram_s("MIXT", [8, 128, L], BF16)

    tk = Tracker(nc, es)
    op, dma = tk.op, tk.dma

    uid = [0]

    def sb(name, shape, dt, stack=es):
        uid[0] += 1
        return stack.enter_context(nc.sbuf_tensor(f"sb{uid[0]}_{name}", list(shape), dt))

    R_up = sb("R_up", [128, 8 * 2 * DFF], BF16)
    B_Rup = Buf("R_up")
    win_v = R_up[:, 0:8 * NIN].rearrange("p (k n) -> p k n", k=8)
    wup_v = R_up[:, :].rearrange("p (k n) -> p k n", k=8)
    cbf = sb("cbf", [128, 5 * 128], BF16)
    B_const = Buf("const")
    IDENT = cbf[:, 0:128]
    TRIB = cbf[:, 128:256]
    ONES128 = cbf[:, 256:384]
    WA = cbf[:, 384:512]
    WB = cbf[:, 512:640]

    PT = es.enter_context(nc.psum_tensor("PT", [128, 1024], BF16))
    PS = [es.enter_context(nc.psum_tensor(f"PS{i}", [128, 512], F32)) for i in range(7)]
    B_PT = Buf("PT", True)
    B_PS = [Buf(f"PS{i}", True) for i in range(7)]

    dma("sp", cbf[:, :], cbf_d, writes=[B_const])

    B_H = [Buf(f"H{g}") for g in range(NKB)]
    B_QT = [Buf(f"QT{t}") for t in range(NT)]
    B_KT = [Buf(f"KT{t}") for t in range(NT)]
    B_V = [Buf(f"V{g}") for g in range(NKB)]
    B_CSB = Buf("CSB")
    B_CREF = Buf("CREF")
    B_MIXG = [Buf(f"MIXG{t}") for t in range(NT)]
    B_MIXF = [[Buf(f"MIXF{p}_{t}") for t in range(NT)] for p in range(4)]

    def h_rows(layer, b0, B, first_src):
        if layer == 0 and first_src:
            if b0 == 0:
                return meta_d[0:B, :]
            return x_d[b0 - NMETA:b0 - NMETA + B, :]
        return H_d[b0:b0 + B, :]

    def act(out, in_, func, bias=None, scale=None, accum_out=None):
        kw = {}
        if bias is not None:
            kw["bias"] = bias
        if scale is not None:
            kw["scale"] = scale
        if accum_out is not None:
            kw["accum_out"] = accum_out
        return lambda e: e.activation(out=out, in_=in_, func=func, **kw)

    def mm(out, lhsT, rhs, start=True, stop=True):
        return lambda e: e.matmul(out, lhsT=lhsT, rhs=rhs, start=start, stop=stop)

    def norm_transpose(ws, hb, B_hb, gbc, B_g, xnT, B_xnT, o, B):
        norm_part1(ws, hb, B_hb, gbc, B_g, B)
        norm_part2(ws, xnT, B_xnT, o, B)

    def norm_part1(ws, hb, B_hb, gbc, B_g, B):
        xn, B_xn, st, B_st = ws["xn"], ws["B_xn"], ws["st"], ws["B_st"]
        op("act", act(xn[0:B, :], hb[0:B, :], AF.Square, accum_out=st[0:B, 0:1]),
           reads=[B_hb], writes=[B_xn, B_st])
        op("act", act(st[0:B, 1:2], st[0:B, 0:1], AF.Ln, bias=EPS, scale=1.0 / D),
           reads=[B_st], writes=[B_st])
        op("act", act(st[0:B, 2:3], st[0:B, 1:2], AF.Exp, scale=-0.5),
           reads=[B_st], writes=[B_st])
        op("dve", lambda e: e.scalar_tensor_tensor(out=xn[0:B, :], in0=hb[0:B, :], scalar=st[0:B, 2:3],
                                                   in1=gbc[0:B, :], op0=ALU.mult, op1=ALU.mult),
           reads=[B_hb, B_st, B_g], writes=[B_xn])

    def norm_part2(ws, xnT, B_xnT, o, B):
        xn, B_xn = ws["xn"], ws["B_xn"]
        for k in range(8):
            op("pe", lambda e, k=k: e.transpose(out=PT[:, k * 128:k * 128 + B], in_=xn[0:B, k * 128:(k + 1) * 128],
                                                identity=IDENT[0:B, 0:B]),
               reads=[B_xn, B_const], writes=[B_PT])
        src = PT[:, :].rearrange("p (k t) -> p k t", k=8)[:, :, 0:B]
        op("act", lambda e: e.activation(out=xnT[:, :, o:o + B], in_=src, func=AF.Copy),
           reads=[B_PT], writes=[B_xnT], partial=True)

    for layer in range(n_layers):
        last = layer == n_layers - 1
        for k in range(8):
            dma("pool", win_v[:, k, :], win_d[layer, k * 128:(k + 1) * 128, :], writes=[B_Rup], weights=True)

        with contextlib.ExitStack() as pa, nc.named_scope(f"A{layer}"):
            def sba(name, shape, dt):
                return sb(name, shape, dt, pa)
            cf32 = sba("cf32", [128, 4 * 128 + 8], F32)
            B_cf = Buf("cf32")
            dma("sp", cf32[:, :], cf32_d, writes=[B_cf])
            TRI = cf32[:, 0:128]
            SU = cf32[:, 128:256]
            E127 = cf32[:, 256:384]
            E15 = cf32[:, 384:512]
            CH = cf32[:, 512:520]
            gbc = sba("gbc", [128, D], F32); B_gbc = Buf("gbc")
            balpha = sba("balpha", [128, 256], F32)
            bforget = sba("bforget", [128, 8], F32)
            wau32 = sba("wau32", [16, 256], F32)
            wau = sba("wau", [16, 256], BF16)
            gnorm = sba("gnorm", [128, 4], F32)
            B_pp = Buf("layer_params")
            dma("sp", gbc[:, :], anorm_d[layer], writes=[B_gbc])
            dma("sp", balpha[:, :], balpha_d[layer], writes=[B_pp])
            dma("sp", bforget[:, :], bforget_d[layer], writes=[B_pp])
            dma("sp", gnorm[:, :], gnorm_d[layer], writes=[B_pp])
            dma("sp", wau32[:, :], wau_d[layer], writes=[B_pp])
            op("dve", lambda e: e.tensor_copy(out=wau[:, :], in_=wau32[:, :]), reads=[B_pp], writes=[B_pp])

            hbs = [sba(f"hb{i}", [128, D], F32) for i in range(2)]
            B_hbs = [Buf(f"hb{i}") for i in range(2)]
            ws = dict(xn=sba("xn", [128, D], BF16), B_xn=Buf("xn"), st=sba("st", [128, 4], F32), B_st=Buf("st"))
            xnTs = [sba(f"xnT{i}", [128, 8, 512], BF16) for i in range(2)]
            B_xnTs = [Buf(f"xnT{i}") for i in range(2)]
            gqT = sba("gqT", [128, 2, 512], BF16); B_gqT = Buf("gqT")
            sgT = sba("sgT", [128, 4, 512], F32); B_sgT = Buf("sgT")
            glrT = sba("glrT", [16, 512], BF16); B_glrT = Buf("glrT")
            qT = sba("qT", [128, 4, 512], BF16); B_qT = Buf("qT")
            kT = sba("kT", [128, 4, 512], BF16); B_kT = Buf("kT")
            vts = [sba(f"vt{i}", [128, 768], BF16) for i in range(2)]
            B_vts = [Buf(f"vt{i}") for i in range(2)]
            gv = sba("gv", [128, 4, 512], BF16); B_gv = Buf("gv")
            kdec = sba("kdec", [128, 4, 256], BF16); B_kdec = Buf("kdec")
            zb = sba("zb", [128, 256], F32); B_zb = Buf("zb")
            lz = sba("lz", [128, 256], F32); B_lz = Buf("lz")
            wdec = sba("wdec", [128, 256], F32); B_wdec = Buf("wdec")
            f8 = sba("f8", [128, 16], F32); B_f8 = Buf("f8")
            cposs = [sba(f"cpos{i}", [128, 8], F32) for i in range(2)]
            B_cposs = [Buf(f"cpos{i}") for i in range(2)]
            crefs = sba("crefs", [128, NT, 8], F32); B_crefs = Buf("crefs")
            dec = sba("dec", [128, 2, 8], F32); B_dec = Buf("dec")
            S = sba("S", [128, 2, 128], F32); B_S = Buf("S")
            Sbs = [sba(f"Sb{i}", [128, 2, 128], BF16) for i in range(2)]
            B_Sbs = [Buf(f"Sb{i}") for i in range(2)]
            oraw = sba("oraw", [128, 4, 512], F32); B_oraw = Buf("oraw")
            sq2 = [sba(f"sq{i}", [128, 512], BF16) for i in range(2)]
            B_sq2 = [Buf(f"sq{i}") for i in range(2)]
            rs2 = [sba(f"rs{i}", [128, 512], F32) for i in range(2)]
            B_rs2 = [Buf(f"rs{i}") for i in range(2)]
            mixg = sba("mixg", [128, 4, 512], BF16); B_mixg = Buf("mixg")

            op("pool", lambda e: e.memset(S[:, :, :], 0.0), writes=[B_S])
            for i in range(2):
                op("pool", lambda e, i=i: e.memset(cposs[i][:, :], 0.0), writes=[B_cposs[i]])
            for i in range(2):
                op("pool", lambda e, i=i: e.memset(vts[i][:, :], 1.0), writes=[B_vts[i]])

            pa_i = [0]

            def next_pa():
                i = pa_i[0] % 2
                pa_i[0] += 1
                return PS[i], B_PS[i]
            PM1, B_PM1a, B_PM1b = PS[3], B_PS[3], B_PS[3]
            PM2 = PS[4]
            B_PM2w = B_PM2ff = B_PM2cum = B_PM2cref = B_PM2dec = B_PS[4]
            PU, B_PU = PS[5], [B_PS[5], B_PS[5]]
            PRs, B_PRs = [PS[6], PS[2]], [B_PS[6], B_PS[2]]
            PUv = PU[:, :].rearrange("p (b q c) -> p b q c", b=2, q=2)
            PRvs = [P_[:, :].rearrange("p (h c) -> p h c", h=8) for P_ in PRs]

            nblk_seen = 0
            chunk_g = 0
            for ti, (t0, T) in enumerate(TILES):
                blks = blocks_of(t0, T)
                C = 16 if ti == 0 else 64
                xnT, B_xnT = xnTs[ti % 2], B_xnTs[ti % 2]

                def norm_block(tj, bj):
                    tt0, TT = TILES[tj]
                    b0_, o_, B_ = blocks_of(tt0, TT)[bj]
                    g_ = gblk(b0_)
                    hb, B_hb = hbs[g_ % 2], B_hbs[g_ % 2]
                    dma("sp", hb[0:B_, :], h_rows(layer, b0_, B_, True), reads=[B_H[g_]], writes=[B_hb])
                    norm_part1(ws, hb, B_hb, gbc, B_gbc, B_)
                    return lambda: norm_part2(ws, xnTs[tj % 2], B_xnTs[tj % 2], o_, B_)

                if ti == 0:
                    norm_block(0, 0)()
                def proj_fm(col0, M, evac):
                    ps, B_ps = next_pa()
                    for k in range(8):
                        op("pe", mm(ps[0:M, 0:T], win_v[:, k, col0:col0 + M], xnT[:, k, 0:T], k == 0, k == 7),
                           reads=[B_Rup, B_xnT], writes=[B_ps])
                    evac(ps, B_ps)
                proj_fm(C_LR, 16, lambda ps, B_ps: op(
                    "dve", lambda e: e.tensor_copy(out=glrT[:, 0:T], in_=ps[0:16, 0:T]), reads=[B_ps], writes=[B_glrT]))
                for bi, (b0, o, B) in enumerate(blks):
                    g = gblk(b0)
                    nch = 1 if ti == 0 else 2

                    def proj_tm(ps_ap, B_ps, col0, N):
                        for k in range(8):
                            op("pe", mm(ps_ap, xnT[:, k, o:o + B], win_v[:, k, col0:col0 + N], k == 0, k == 7),
                               reads=[B_Rup, B_xnT], writes=[B_ps])
                    ps, B_ps = next_pa()
                    proj_tm(ps[0:B, 0:512], B_ps, C_FV, 512)
                    vt, B_vt = vts[g % 2], B_vts[g % 2]
                    vt4 = vt[:, :].rearrange("b (p s c) -> b p s c", p=4, s=3)
                    ps4 = ps[:, :].rearrange("b (p e c) -> b p e c", p=4, e=2)
                    for e_ in range(2):
                        op("dve", lambda e, e_=e_: e.tensor_copy(out=vt4[0:B, :, 2 * e_, :], in_=ps4[0:B, :, e_, :]),
                           reads=[B_ps], writes=[B_vt], partial=True)
                    dma("pool", V_d[b0:b0 + B, :], vt[0:B, :], reads=[B_vt], writes=[B_V[g]])
                    ps, B_ps = next_pa()
                    proj_tm(ps[0:B, 0:512], B_ps, C_GV, 512)
                    op("act", act(gv[0:B, bi, :], ps[0:B, 0:512], AF.Copy), reads=[B_ps], writes=[B_gv], partial=True)
                    proj_tm(PM1[0:B, 0:256], B_PM1a, C_GK, 256)
                    proj_tm(PM2[0:B, 256:264], B_PM2ff, C_FF, 8)
                    op("dve", lambda e: e.tensor_tensor(out=f8[0:B, 0:8], in0=PM2[0:B, 256:264], in1=bforget[0:B, :], op=ALU.add),
                       reads=[B_PM2ff, B_pp], writes=[B_f8])
                    op("act", act(f8[0:B, 8:16], f8[0:B, 0:8], AF.Exp, scale=-1.0), reads=[B_f8], writes=[B_f8])
                    op("act", act(f8[0:B, 0:8], f8[0:B, 8:16], AF.Ln, bias=1.0), reads=[B_f8], writes=[B_f8])
                    cur, B_cur = cposs[g % 2], B_cposs[g % 2]
                    prv, B_prv = cposs[(g + 1) % 2], B_cposs[(g + 1) % 2]
                    op("pe", mm(PM2[0:B, 264:272], TRI[0:B, 0:B], f8[0:B, 0:8], True, g == 0),
                       reads=[B_cf, B_f8], writes=[B_PM2cum])
                    if g > 0:
                        Bp = 16 if g == 1 else 128
                        Es = E15 if g == 1 else E127
                        op("pe", mm(PM2[0:B, 264:272], Es[0:Bp, 0:B], prv[0:Bp, :], False, True),
                           reads=[B_cf, B_prv], writes=[B_PM2cum])
                    op("dve", lambda e: e.tensor_copy(out=cur[0:B, :], in_=PM2[0:B, 264:272]), reads=[B_PM2cum], writes=[B_cur])
                    Bw = 128 if g == 0 else B
                    dma("pool", CSB_d[0:Bw, g, :], cur[0:Bw, :], reads=[B_cur], writes=[B_CSB])
                    if (ti == 0 and bi == 0) or (ti > 0 and bi == 1):
                        Es = E15 if ti == 0 else E127
                        op("pe", mm(PM2[:, 272:280], Es[0:B, :], cur[0:B, :]), reads=[B_cf, B_cur], writes=[B_PM2cref])
                        op("dve", lambda e: e.tensor_copy(out=crefs[:, ti, :], in_=PM2[:, 272:280]),
                           reads=[B_PM2cref], writes=[B_crefs], partial=True)
                    op("pe", mm(PM1[0:B, 256:512], glrT[0:16, o:o + B], wau[:, :]), reads=[B_glrT, B_pp], writes=[B_PM1b])
                    op("dve", lambda e: e.tensor_tensor(out=zb[0:B, :], in0=PM1[0:B, 256:512], in1=balpha[0:B, :], op=ALU.add),
                       reads=[B_PM1b, B_pp], writes=[B_zb])
                    op("act", act(zb[0:B, :], zb[0:B, :], AF.Exp, scale=-1.0), reads=[B_zb], writes=[B_zb])
                    op("act", act(lz[0:B, :], zb[0:B, :], AF.Ln, bias=1.0), reads=[B_zb], writes=[B_lz])
                    op("pe", mm(PM2[0:B, 0:256], SU[0:B, 0:B], lz[0:B, :]), reads=[B_cf, B_lz], writes=[B_PM2w])
                    op("act", act(wdec[0:B, :], PM2[0:B, 0:256], AF.Exp, scale=-1.0 / 16), reads=[B_PM2w], writes=[B_wdec])
                    op("dve", lambda e: e.tensor_tensor(out=kdec[0:B, bi, :], in0=PM1[0:B, 0:256], in1=wdec[0:B, :], op=ALU.mult),
                       reads=[B_PM1a, B_wdec], writes=[B_kdec], partial=True)
                    for p in range(2):
                        op("pe", mm(PM2[:, 280 + 8 * p:280 + 8 * p + 8], lz[0:B, p * 128:(p + 1) * 128], CH[0:B, 0:8]),
                           reads=[B_cf, B_lz], writes=[B_PM2dec])
                    PMd = PM2[:, 280:296].rearrange("p (q c) -> p q c", q=2)
                    op("act", act(dec[:, :, 2 * bi:2 * bi + nch], PMd[:, :, 0:nch], AF.Exp, scale=-1.0 / 16),
                       reads=[B_PM2dec], writes=[B_dec], partial=True)
                for j in range(2):
                    proj_fm(C_GQ + 128 * j, 128, lambda ps, B_ps, j=j: op(
                        "act", act(gqT[:, j, 0:T], ps[:, 0:T], AF.Copy, scale=0.125), reads=[B_ps], writes=[B_gqT], partial=True))
                pq = []
                for j in range(4):
                    pq.append(lambda j=j: proj_fm(C_GR + 128 * j, 128, lambda ps, B_ps, j=j: op(
                        "act", act(sgT[:, j, 0:T], ps[:, 0:T], AF.Silu), reads=[B_ps], writes=[B_sgT], partial=True)))
                for j in range(4):
                    pq.append(lambda j=j: proj_fm(C_FQ + 128 * j, 128, lambda ps, B_ps, j=j: op(
                        "dve", lambda e: e.tensor_copy(out=qT[:, j, 0:T], in_=ps[:, 0:T]), reads=[B_ps], writes=[B_qT], partial=True)))
                pq.append(lambda: dma("pool", QT_d[:, :, t0:t0 + T].rearrange("c p t -> p c t"), qT[:, :, 0:T], reads=[B_qT], writes=[B_QT[ti]]))
                for j in range(4):
                    pq.append(lambda j=j: proj_fm(C_FK + 128 * j, 128, lambda ps, B_ps, j=j: op(
                        "dve", lambda e: e.tensor_copy(out=kT[:, j, 0:T], in_=ps[:, 0:T]), reads=[B_ps], writes=[B_kT], partial=True)))
                pq.append(lambda: dma("pool", KT_d[:, :, t0:t0 + T].rearrange("c p t -> p c t"), kT[:, :, 0:T], reads=[B_kT], writes=[B_KT[ti]]))
                nq = []
                part2 = []
                if ti + 1 < NT:
                    for bj in range(len(blocks_of(*TILES[ti + 1]))):
                        nq.append(lambda bj=bj: norm_block(ti + 1, bj))
                nchunks = T // C
                for c in range(nchunks):
                    bi = (c * C) // 128
                    r0 = (c * C) % 128
                    cb = 0
                    for h in range(4):
                        p, e_ = h // 2, h % 2
                        op("pe", mm(PUv[e_ * 64:(e_ + 1) * 64, cb, p, :], kdec[r0:r0 + C, bi, h * 64:(h + 1) * 64],
                                    gv[r0:r0 + C, bi, h * 128:(h + 1) * 128]),
                           reads=[B_kdec, B_gv], writes=[B_PU[cb]])
                    for p in range(2):
                        op("dve", lambda e, p=p: e.scalar_tensor_tensor(out=S[:, p, :], in0=S[:, p, :], scalar=dec[:, p, c:c + 1],
                                                                        in1=PUv[:, cb, p, :], op0=ALU.mult, op1=ALU.add),
                           reads=[B_S, B_dec, B_PU[cb]], writes=[B_S])
                    Sb, B_Sb = Sbs[cb], B_Sbs[cb]
                    op("act", act(Sb[:, :, :], S[:, :, :], AF.Copy), reads=[B_S], writes=[B_Sb])
                    for h in range(4):
                        p, e_ = h // 2, h % 2
                        op("pe", mm(PRvs[e_][:, p, 0:C], Sb[e_ * 64:(e_ + 1) * 64, p, :], gqT[e_ * 64:(e_ + 1) * 64, p, c * C:(c + 1) * C]),
                           reads=[B_Sb, B_gqT], writes=[B_PRs[e_]])
                    orv = oraw[:, :, :].rearrange("p (q e) t -> p q e t", e=2)
                    for e_ in range(2):
                        op("act" if e_ == 0 else "dve",
                           (lambda e, e_=e_: e.activation(out=orv[:, :, e_, c * C:(c + 1) * C], in_=PRvs[e_][:, 0:2, 0:C], func=AF.Copy)) if e_ == 0 else
                           (lambda e, e_=e_: e.tensor_copy(out=orv[:, :, e_, c * C:(c + 1) * C], in_=PRvs[e_][:, 0:2, 0:C])),
                           reads=[B_PRs[e_]], writes=[B_oraw], partial=True)
                    for _ in range(2):
                        if pq:
                            pq.pop(0)()
                    if part2:
                        part2.pop(0)()
                    if nq and (c % 2 == 1 or nchunks == 1):
                        part2.append(nq.pop(0)())
                while pq:
                    pq.pop(0)()
                while nq or part2:
                    if part2:
                        part2.pop(0)()
                    if nq:
                        part2.append(nq.pop(0)())
                stat = {}

                def fin_a(h):
                    op("act", act(sq2[h % 2][:, 0:T], oraw[:, h, 0:T], AF.Square), reads=[B_oraw], writes=[B_sq2[h % 2]])
                    ps, B_ps = next_pa()
                    op("pe", mm(ps[:, 0:T], ONES128, sq2[h % 2][:, 0:T]), reads=[B_const, B_sq2[h % 2]], writes=[B_ps])
                    stat[h] = (ps, B_ps)

                def fin_b(h):
                    ps, B_ps = stat[h]
                    r_, B_r = rs2[h % 2], B_rs2[h % 2]
                    op("act", act(r_[:, 0:T], ps[:, 0:T], AF.Ln, bias=EPS), reads=[B_ps], writes=[B_r])
                    op("act", act(r_[:, 0:T], r_[:, 0:T], AF.Exp, scale=-0.5), reads=[B_r], writes=[B_r])
                    op("dve", lambda e: e.scalar_tensor_tensor(out=r_[:, 0:T], in0=oraw[:, h, 0:T], scalar=gnorm[:, h:h + 1],
                                                               in1=r_[:, 0:T], op0=ALU.mult, op1=ALU.mult),
                       reads=[B_oraw, B_pp, B_r], writes=[B_r])
                    op("dve", lambda e: e.tensor_tensor(out=mixg[:, h, 0:T], in0=r_[:, 0:T], in1=sgT[:, h, 0:T], op=ALU.mult),
                       reads=[B_r, B_sgT], writes=[B_mixg], partial=True)

                fin_a(0)
                for h in range(4):
                    if h + 1 < 4:
                        fin_a(h + 1)
                    fin_b(h)
                dma("pool", MIXT_d[0:4, :, t0:t0 + T].rearrange("c p t -> p c t"), mixg[:, :, 0:T], reads=[B_mixg], writes=[B_MIXG[ti]])
            dma("pool", CREF_d, crefs[:, :, :], reads=[B_crefs], writes=[B_CREF])
            tk.barrier()
        if stop_after == "A":
            break

        lw = contextlib.ExitStack()
        R_dn = sb("R_dn", [128, NJ, D], BF16, lw); B_Rdn = Buf("R_dn")
        lo = contextlib.ExitStack()
        R_out = sb("R_out", [128, 8, D], BF16, lo); B_Rout = Buf("R_out")
        for k in range(8):
            dma("pool", R_out[:, k, :], wout_d[layer, k * 128:(k + 1) * 128, :], writes=[B_Rout], weights=True)
        for k in range(8):
            dma("pool", wup_v[:, k, :], wup_d[layer, k * 128:(k + 1) * 128, :], writes=[B_Rup], weights=True)
        for j in range(NJ):
            dma("pool", R_dn[:, j, :], wdn_d[layer, j * 128:(j + 1) * 128, :], writes=[B_Rdn], weights=True)

        with contextlib.ExitStack() as pb, nc.named_scope(f"B{layer}"):
            def sbb(name, shape, dt):
                return sb(name, shape, dt, pb)
            KTs = [sbb("KTs0", [128, L], BF16)] * 2
            B_KTs = [Buf("KTs0")] * 2
            Vps = [sbb(f"Vp{i}", [128, NKB, 192], BF16) for i in range(2)]
            B_Vps = [Buf(f"Vp{i}") for i in range(2)]
            csb = sbb("csb", [128, NKB, 8], F32); B_csb = Buf("csb")
            cref = sbb("cref", [128, NT, 8], F32); B_cref = Buf("cref")
            xnorm = sbb("xnorm", [128, 4], F32); B_xnorm = Buf("xnorm")
            biases = [sbb(f"bias{i}", [128, NKB, 2], F32) for i in range(2)]
            B_biases = [Buf(f"bias{i}") for i in range(2)]
            qts = [sbb(f"qt{i}", [128, 512], BF16) for i in range(2)]
            B_qts = [Buf(f"qt{i}") for i in range(2)]
            pts = [sbb(f"pt{i}", [128, 512], BF16) for i in range(4)]
            B_pts = [Buf(f"pt{i}") for i in range(4)]
            pcs = [sbb(f"pc{i}", [128, 512], F32) for i in range(2)]
            B_pcs = [Buf(f"pc{i}") for i in range(2)]
            sqs = [sbb("sqb0", [128, 512], BF16)]
            B_sqs = [Buf("sqb0")]
            rd = sbb("rd", [128, 512], F32); B_rd = Buf("rd")
            onorm = sbb("onorm", [128, 512], F32); B_on = Buf("onorm")
            rsb = sbb("rsb", [128, 512], F32); B_rsb = Buf("rsb")
            ots = [sbb(f"ot{i}", [128, 512], BF16) for i in range(2)]
            B_ots = [Buf(f"ot{i}") for i in range(2)]

            dma("sp", csb[:, :, :], CSB_d, reads=[B_CSB], writes=[B_csb])
            dma("sp", cref[:, :, :], CREF_d, reads=[B_CREF], writes=[B_cref])
            dma("sp", xnorm[:, :], xnorm_d[layer], writes=[B_xnorm])
            SBK = [[(PS[0], B_PS[0]), (PS[1], B_PS[1])], [(PS[2], B_PS[2]), (PS[6], B_PS[6])]]
            iters = [(p, ti) for p in range(4) for ti in range(NT)]

            def kbs_of(ti):
                if ti == 0:
                    return [(0, 0, 16)]
                return [(0, 0, 16)] + [(1 + j, NMETA + 128 * j, 128) for j in range(4 * ti)]

            def load_kt(p):
                dma("sp", KTs[p % 2][:, :], KT_d[p], reads=B_KT, writes=[B_KTs[p % 2]])

            def load_pair(p):
                Vp, B_Vp = Vps[p % 2], B_Vps[p % 2]
                dma("sp", Vp[0:16, 0, :], V_d[0:16, 192 * p:192 * p + 192], reads=B_V, writes=[B_Vp])
                dma("sp", Vp[:, 1:NKB, :], V_d[16:L, 192 * p:192 * p + 192].rearrange("(n q) c -> q n c", q=128),
                    reads=B_V, writes=[B_Vp])

            def load_q(n):
                p, ti = iters[n]
                t0, T = TILES[ti]
                nkb = len(kbs_of(ti))
                dma("sp", qts[n % 2][:, 0:T], QT_d[p, :, t0:t0 + T], reads=[B_QT[ti]], writes=[B_qts[n % 2]])
                op("dve", lambda e: e.tensor_tensor(
                    out=biases[n % 2][:, 0:nkb, :], in0=csb[:, 0:nkb, 2 * p:2 * p + 2],
                    in1=cref[:, ti:ti + 1, 2 * p:2 * p + 2].to_broadcast([128, nkb, 2]), op=ALU.subtract),
                   reads=[B_csb, B_cref], writes=[B_biases[n % 2]])

            def epilogue_dve(n):
                p, ti = iters[n]
                t0, T = TILES[ti]
                for e_ in range(2):
                    orow = slice(e_ * 64, (e_ + 1) * 64)
                    drow = slice((1 - e_) * 64, (2 - e_) * 64)
                    op("dve", lambda e: e.reciprocal(out=rd[orow, 0:T], in_=pcs[e_][drow, 0:T]), reads=[B_pcs[e_]], writes=[B_rd], partial=True)
                    op("dve", lambda e: e.tensor_tensor(out=onorm[orow, 0:T], in0=pcs[e_][orow, 0:T], in1=rd[orow, 0:T], op=ALU.mult),
                       reads=[B_pcs[e_], B_rd], writes=[B_on], partial=True)

            def epilogue(n):
                p, ti = iters[n]
                t0, T = TILES[ti]
                ot, B_ot = ots[n % 2], B_ots[n % 2]
                op("act", act(sqs[0][:, 0:T], onorm[:, 0:T], AF.Square), reads=[B_on], writes=[B_sqs[0]])
                pst, B_pst = PS[5], B_PS[5]
                op("pe", mm(pst[:, 0:T], WA, sqs[0][:, 0:T]), reads=[B_const, B_sqs[0]], writes=[B_pst])
                op("act", act(rsb[:, 0:T], pst[:, 0:T], AF.Ln, bias=EPS), reads=[B_pst], writes=[B_rsb])
                op("act", act(rsb[:, 0:T], rsb[:, 0:T], AF.Exp, scale=-0.5), reads=[B_rsb], writes=[B_rsb])
                op("dve", lambda e: e.scalar_tensor_tensor(
                    out=ot[:, 0:T], in0=onorm[:, 0:T], scalar=xnorm[:, p:p + 1],
                    in1=rsb[:, 0:T], op0=ALU.mult, op1=ALU.mult),
                   reads=[B_on, B_xnorm, B_rsb], writes=[B_ot])
                dma("sp", MIXT_d[4 + p, :, t0:t0 + T], ot[:, 0:T], reads=[B_ot], writes=[B_MIXF[p][ti]])

            load_kt(0)
            load_pair(0)
            load_q(0)
            deferred = []
            for n, (p, ti) in enumerate(iters):
                t0, T = TILES[ti]
                KTp, B_KTp = KTs[p % 2], B_KTs[p % 2]
                Vp, B_Vp = Vps[p % 2], B_Vps[p % 2]
                qt, B_qt = qts[n % 2], B_qts[n % 2]
                bias, B_bias = biases[n % 2], B_biases[n % 2]
                kbs = kbs_of(ti)
                nkb = len(kbs)

                def emit_s(idx):
                    g, k0, KB = kbs[idx]
                    qa = max(0, k0 - t0)
                    for e_ in range(2):
                        rows = slice(e_ * 64, (e_ + 1) * 64)
                        ps, B_ps = SBK[e_][idx % 2]
                        op("pe", mm(ps[0:KB, qa:T], KTp[rows, k0:k0 + KB], qt[rows, qa:T]),
                           reads=[B_KTp, B_qt], writes=[B_ps])

                emit_s(0)
                if n + 1 < len(iters):
                    if iters[n + 1][0] != p:
                        load_pair(iters[n + 1][0])
                    load_q(n + 1)
                for idx in range(nkb):
                    g, k0, KB = kbs[idx]
                    qa = max(0, k0 - t0)
                    diag = k0 + KB > t0
                    if idx + 1 < nkb:
                        emit_s(idx + 1)
                    for e_ in range(2):
                        ps, B_ps = SBK[e_][idx % 2]
                        pt, B_pt = pts[2 * e_ + idx % 2], B_pts[2 * e_ + idx % 2]
                        po, B_po = PS[3 + e_], B_PS[3 + e_]
                        op("act", act(pt[0:KB, qa:T], ps[0:KB, qa:T], AF.Exp, bias=bias[0:KB, g, e_:e_ + 1], scale=0.125),
                           reads=[B_ps, B_bias], writes=[B_pt])
                        if diag:
                            op("dve", lambda e: e.tensor_tensor(out=pt[0:KB, qa:qa + KB], in0=pt[0:KB, qa:qa + KB],
                                                                in1=TRIB[0:KB, 0:KB], op=ALU.mult),
                               reads=[B_pt, B_const], writes=[B_pt])
                        op("pe", mm(po[:, qa:T], Vp[0:KB, g, e_ * 64:e_ * 64 + 128], pt[0:KB, qa:T], idx == 0, idx == nkb - 1),
                           reads=[B_Vp, B_pt], writes=[B_po])
                    if deferred and idx == min(7, nkb - 1):
                        epilogue(deferred.pop(0))
                for e_ in range(2):
                    op("dve", lambda e, e_=e_: e.tensor_copy(out=pcs[e_][:, 0:T], in_=PS[3 + e_][:, 0:T]),
                       reads=[B_PS[3 + e_]], writes=[B_pcs[e_]])
                epilogue_dve(n)
                deferred.append(n)
                if n + 1 < len(iters) and iters[n + 1][0] != p:
                    load_kt(iters[n + 1][0])
            while deferred:
                epilogue(deferred.pop(0))
            tk.barrier()
        if stop_after == "B":
            lo.close()
            lw.close()
            break

        with contextlib.ExitStack() as pc, nc.named_scope(f"C{layer}"):
            def sbc(name, shape, dt):
                return sb(name, shape, dt, pc)
            mixts = [sbc(f"mixt{i}", [128, 8, 512], BF16) for i in range(2)]
            B_mixts = [Buf(f"mixt{i}") for i in range(2)]
            hbs = [sbc(f"hbc{i}", [128, D], F32) for i in range(2)]
            B_hbs = [Buf(f"hbc{i}") for i in range(2)]
            for ti, (t0, T) in enumerate(TILES):
                mt, B_mt = mixts[ti % 2], B_mixts[ti % 2]
                dma("sp", mt[:, :, 0:T], MIXT_d[:, :, t0:t0 + T].rearrange("c p t -> p c t"),
                    reads=[B_MIXG[ti]] + [B_MIXF[p][ti] for p in range(4)], writes=[B_mt])
                for bi, (b0, o, B) in enumerate(blocks_of(t0, T)):
                    g = gblk(b0)
                    hb, B_hb = hbs[g % 2], B_hbs[g % 2]
                    dma("sp", hb[0:B, :], h_rows(layer, b0, B, True), reads=[B_H[g]], writes=[B_hb])
                    for half in range(2):
                        ps, B_ps = PS[half], B_PS[half]
                        for c in range(8):
                            op("pe", mm(ps[0:B, :], mt[:, c, o:o + B], R_out[:, c, half * 512:(half + 1) * 512], c == 0, c == 7),
                               reads=[B_mt, B_Rout], writes=[B_ps])
                        op("dve", lambda e, half=half, ps=ps: e.tensor_tensor(
                            out=hb[0:B, half * 512:(half + 1) * 512], in0=ps[0:B, :], in1=hb[0:B, half * 512:(half + 1) * 512], op=ALU.add),
                           reads=[B_ps, B_hb], writes=[B_hb], partial=True)
                    dma("pool", H_d[b0:b0 + B, :], hb[0:B, :], reads=[B_hb], writes=[B_H[g]])
            tk.barrier()
        lo.close()
        if stop_after == "C":
            lw.close()
            break

        with contextlib.ExitStack() as pd, nc.named_scope(f"D{layer}"):
            def sbd(name, shape, dt):
                return sb(name, shape, dt, pd)
            g2bc = sbd("g2bc", [128, D], F32); B_g2 = Buf("g2bc")
            convw = sbd("convw", [128, 2 * NJ, 3], F32)
            convb = sbd("convb", [128, 2 * NJ], F32)
            B_cv = Buf("conv")
            dma("sp", g2bc[:, :], fnorm_d[layer], writes=[B_g2])
            dma("sp", convw[:, :, :], convw_d[layer], writes=[B_cv])
            dma("sp", convb[:, :], convb_d[layer], writes=[B_cv])
            if last:
                fbc = sbd("fbc", [128, D], F32); B_fbc = Buf("fbc")
                dma("sp", fbc[:, :], final_d, writes=[B_fbc])
            halo = sbd("halo", [128, 2 * NJ, 2], F32); B_halo = Buf("halo")
            op("pool", lambda e: e.memset(halo[:, :, :], 0.0), writes=[B_halo])
            hbs = [sbd(f"hbd{i}", [128, D], F32) for i in range(2)]
            B_hbs = [Buf(f"hbd{i}") for i in range(2)]
            if last:
                hbx = sbd("hbx", [128, D], F32)
                B_hbx = Buf("hbx")
                hbn, B_hbn = [hbx, hbx], [B_hbx, B_hbx]
            else:
                hbn = [sbd(f"hbn{i}", [128, D], F32) for i in range(2)]
                B_hbn = [Buf(f"hbn{i}") for i in range(2)]
            ws = dict(xn=sbd("xnd", [128, D], BF16), B_xn=Buf("xnd"), st=sbd("std", [128, 4], F32), B_st=Buf("std"))
            xnT = sbd("xnTd", [128, 8, 512], BF16); B_xnT = Buf("xnTd")
            hbufs = [sbd(f"hbuf{i}", [128, 514], F32) for i in range(4)]
            B_hbufs = [Buf(f"hbuf{i}") for i in range(4)]
            B_hhalo = [Buf(f"hhalo{i}") for i in range(4)]
            t1s = [sbd(f"t1_{i}", [128, 512], F32) for i in range(6)]
            B_t1s = [Buf(f"t1_{i}") for i in range(6)]
            actT = sbd("actT", [128, NJ, 512], BF16); B_actT = Buf("actT")
            hcnt = 0
            for ti, (t0, T) in enumerate(TILES):
                blks = blocks_of(t0, T)

                def norm_block_d(tj, bj):
                    tt0, TT = TILES[tj]
                    b0_, o_, B_ = blocks_of(tt0, TT)[bj]
                    g_ = gblk(b0_)
                    hb, B_hb = hbn[g_ % 2], B_hbn[g_ % 2]
                    dma("sp", hb[0:B_, :], H_d[b0_:b0_ + B_, :], reads=[B_H[g_]], writes=[B_hb])
                    norm_part1(ws, hb, B_hb, g2bc, B_g2, B_)
                    return lambda: norm_part2(ws, xnT, B_xnT, o_, B_)

                if ti == 0:
                    norm_block_d(0, 0)()
                nq = []
                part2 = []
                if ti + 1 < NT:
                    for bj in range(len(blocks_of(*TILES[ti + 1]))):
                        nq.append(lambda bj=bj: norm_block_d(ti + 1, bj))
                pend = []
                silu_done = set()

                def fin(item):
                    pj, ((tu, B_tu), (tg, B_tg)) = item
                    if pj not in silu_done:
                        op("act", act(tg[:, 0:T], tg[:, 0:T], AF.Silu), reads=[B_tg], writes=[B_tg])
                    op("dve", lambda e: e.tensor_tensor(out=actT[:, pj, 0:T], in0=tu[:, 0:T], in1=tg[:, 0:T], op=ALU.mult),
                       reads=[B_tu, B_tg], writes=[B_actT], partial=True)

                for j in range(NJ):
                    pss = []
                    for br in range(2):
                        ps, B_ps = PS[2 + 2 * (j % 2) + br], B_PS[2 + 2 * (j % 2) + br]
                        col0 = br * DFF + j * 128
                        for k in range(8):
                            op("pe", mm(ps[:, 0:T], wup_v[:, k, col0:col0 + 128], xnT[:, k, 0:T], k == 0, k == 7),
                               reads=[B_Rup, B_xnT], writes=[B_ps])
                        pss.append((ps, B_ps))
                    t3 = []
                    for br in range(2):
                        ps, B_ps = pss[br]
                        ci = br * NJ + j
                        bi_ = 2 * (j % 2) + br
                        hbuf, B_hbuf = hbufs[bi_], B_hbufs[bi_]
                        t1, B_t1 = t1s[2 * (j % 3) + br], B_t1s[2 * (j % 3) + br]
                        op("pool", lambda e, ci=ci, hbuf=hbuf: e.tensor_copy(out=hbuf[:, 0:2], in_=halo[:, ci, :]),
                           reads=[B_halo], writes=[B_hhalo[bi_]])
                        op("act", act(hbuf[:, 2:2 + T], ps[:, 0:T], AF.Copy), reads=[B_ps], writes=[B_hbuf])
                        op("act", act(t1[:, 0:T], ps[:, 0:T], AF.Identity, bias=convb[:, ci:ci + 1], scale=convw[:, ci, 2:3]),
                           reads=[B_ps, B_cv], writes=[B_t1])
                        op("pool", lambda e, ci=ci, hbuf=hbuf: e.tensor_copy(out=halo[:, ci, :], in_=hbuf[:, T:T + 2]),
                           reads=[B_hbuf], writes=[B_halo], partial=True)
                        t3.append((t1, B_t1))
                    if pend:
                        pj, ((ptu, B_ptu), (ptg, B_ptg)) = pend[0]
                        op("act", act(ptg[:, 0:T], ptg[:, 0:T], AF.Silu), reads=[B_ptg], writes=[B_ptg])
                        silu_done.add(pj)
                    for br in range(2):
                        ci = br * NJ + j
                        bi_ = 2 * (j % 2) + br
                        hbuf, B_hbuf = hbufs[bi_], B_hbufs[bi_]
                        t1, B_t1 = t1s[2 * (j % 3) + br], B_t1s[2 * (j % 3) + br]
                        op("dve", lambda e, ci=ci, hbuf=hbuf, t1=t1: e.scalar_tensor_tensor(
                            out=t1[:, 0:T], in0=hbuf[:, 1:1 + T], scalar=convw[:, ci, 1:2], in1=t1[:, 0:T], op0=ALU.mult, op1=ALU.add),
                           reads=[B_hbuf, B_hhalo[bi_], B_cv, B_t1], writes=[B_t1])
                        op("dve", lambda e, ci=ci, hbuf=hbuf, t1=t1: e.scalar_tensor_tensor(
                            out=t1[:, 0:T], in0=hbuf[:, 0:T], scalar=convw[:, ci, 0:1], in1=t1[:, 0:T], op0=ALU.mult, op1=ALU.add),
                           reads=[B_hbuf, B_hhalo[bi_], B_cv, B_t1], writes=[B_t1])
                    pend.append((j, t3))
                    if len(pend) > 1:
                        fin(pend.pop(0))
                while pend:
                    fin(pend.pop(0))
                for bi, (b0, o, B) in enumerate(blks):
                    g = gblk(b0)
                    hb, B_hb = hbs[hcnt % 2], B_hbs[hcnt % 2]
                    hcnt += 1
                    dma("sp", hb[0:B, :], H_d[b0:b0 + B, :], reads=[B_H[g]], writes=[B_hb])
                    if nq:
                        part2.append(nq.pop(0)())
                    for half in range(2):
                        ps, B_ps = PS[half], B_PS[half]
                        for j in range(NJ):
                            op("pe", mm(ps[0:B, :], actT[:, j, o:o + B], R_dn[:, j, half * 512:(half + 1) * 512], j == 0, j == NJ - 1),
                               reads=[B_actT, B_Rdn], writes=[B_ps])
                        op("dve", lambda e, half=half, ps=ps, hb=hb: e.tensor_tensor(
                            out=hb[0:B, half * 512:(half + 1) * 512], in0=ps[0:B, :], in1=hb[0:B, half * 512:(half + 1) * 512], op=ALU.add),
                           reads=[B_ps, B_hb], writes=[B_hb], partial=True)
                    if not last:
                        dma("pool", H_d[b0:b0 + B, :], hb[0:B, :], reads=[B_hb], writes=[B_H[g]])
                    elif ti > 0:
                        st, B_st = ws["st"], ws["B_st"]
                        xn, B_xn = t1s[0][:, :].bitcast(BF16), B_t1s[0]
                        op("act", act(xn[0:B, :], hb[0:B, :], AF.Square, accum_out=st[0:B, 0:1]), reads=[B_hb], writes=[B_xn, B_st])
                        op("act", act(st[0:B, 1:2], st[0:B, 0:1], AF.Ln, bias=EPS, scale=1.0 / D), reads=[B_st], writes=[B_st])
                        op("act", act(st[0:B, 2:3], st[0:B, 1:2], AF.Exp, scale=-0.5), reads=[B_st], writes=[B_st])
                        op("dve", lambda e, hb=hb: e.scalar_tensor_tensor(out=hb[0:B, :], in0=hb[0:B, :], scalar=st[0:B, 2:3],
                                                                        in1=fbc[0:B, :], op0=ALU.mult, op1=ALU.mult),
                           reads=[B_hb, B_st, B_fbc], writes=[B_hb])
                        dma("pool", y_d[b0 - NMETA:b0 - NMETA + B, :], hb[0:B, :], reads=[B_hb], writes=[B_H[g]])
                    if part2:
                        part2.pop(0)()
                while nq or part2:
                    if part2:
                        part2.pop(0)()
                    if nq:
                        part2.append(nq.pop(0)())
            tk.barrier()
        lw.close()
    tk.final_wait()
    if tk.limit is not None:
        print("K_LIMIT", tk.limit, "total ops", tk.nops, "last emitted:", [x for x in tk.log if x[0] in (tk.limit - 1, tk.limit, tk.limit + 1)])


def _consts():
    s = np.arange(128)[:, None]
    t = np.arange(128)[None, :]
    tri = (s <= t).astype(np.float32)
    su = ((s > t) & (s // 64 == t // 64)).astype(np.float32)
    e127 = np.zeros((128, 128), np.float32); e127[127, :] = 1.0
    e15 = np.zeros((128, 128), np.float32); e15[15, :] = 1.0
    ch = (s // 64 == np.arange(8)[None, :]).astype(np.float32)
    cf32 = np.concatenate([tri, su, e127, e15, ch], axis=1)
    ident = np.eye(128, dtype=np.float32)
    ones128 = np.full((128, 128), 1.0 / 128, np.float32)
    wa = np.zeros((128, 128), np.float32); wa[0:64, 0:64] = 1.0 / 64; wa[64:128, 64:128] = 1.0 / 64
    wb = np.zeros((128, 128), np.float32); wb[0:64, 64:128] = EPS / 64; wb[64:128, 64:128] = 1.0 / 64
    cbf = np.concatenate([ident, tri, ones128, wa, wb], axis=1).astype(ml_dtypes.bfloat16)
    return np.ascontiguousarray(cf32), np.ascontiguousarray(cbf)


def make_in_maps(inputs, cores=range(8)):
    f = lambda a: np.ascontiguousarray(np.asarray(a, dtype=np.float32))
    x = f(inputs["x"])
    bc = lambda a: np.ascontiguousarray(np.broadcast_to(f(a)[:, None, :], (a.shape[0], 128, a.shape[1])))
    cf32, cbf = _consts()
    conv_w = f(inputs["conv_w"])
    shared = {
        "meta": f(inputs["meta_tokens"]),
        "w_in": f(inputs["w_in"]), "w_out": f(inputs["w_out"]), "w_up": f(inputs["w_up"]), "w_down": f(inputs["w_down"]),
        "anorm_bc": bc(np.asarray(inputs["attn_norm"])), "fnorm_bc": bc(np.asarray(inputs["ffn_norm"])),
        "final_bc": np.ascontiguousarray(np.broadcast_to(f(inputs["final_norm"])[None, :], (128, D))),
        "wau": f(inputs["w_alpha_up"]),
        "balpha_bc": bc(np.asarray(inputs["b_alpha"])), "bforget_bc": bc(np.asarray(inputs["b_forget"])),
        "gnorm_fm": np.ascontiguousarray(f(inputs["gla_norm"]).reshape(DEPTH, 4, 128).transpose(0, 2, 1)),
        "xnorm_fm": np.ascontiguousarray(f(inputs["fox_norm"]).reshape(DEPTH, 4, 128).transpose(0, 2, 1)),
        "convw_fm": np.ascontiguousarray(conv_w.reshape(DEPTH, 3, 2 * NJ, 128).transpose(0, 3, 2, 1)),
        "convb_fm": np.ascontiguousarray(f(inputs["conv_b"]).reshape(DEPTH, 2 * NJ, 128).transpose(0, 2, 1)),
        "cf32": cf32, "cbf": cbf,
    }
    return [dict(shared, x=np.ascontiguousarray(x[c])) for c in cores]


_NC_CACHE = {}


def kernel(**inputs):
    if "nc" not in _NC_CACHE:
        _NC_CACHE["nc"] = build_nc()
    nc = _NC_CACHE["nc"]
    in_maps = make_in_maps(inputs)
    res = run_bass_kernel_spmd(nc, in_maps, core_ids=list(range(8)))
    return np.stack([np.asarray(r["y"], dtype=np.float32) for r in res.results], axis=0)
```

```python
import contextlib
import numpy as np
import ml_dtypes
import concourse.bass as bass
import concourse.mybir as mybir
from concourse.bass_utils import run_bass_kernel_spmd

F32 = mybir.dt.float32
BF16 = mybir.dt.bfloat16
ALU = mybir.AluOpType
AF = mybir.ActivationFunctionType

D = 1024
SEQ = 4096
NMETA = 16
L = SEQ + NMETA
DEPTH = 4
NIN = 3096
DFF = 2816
NJ = DFF // 128
EPS = 1e-6
C_GQ, C_GK, C_GV, C_GR, C_LR, C_FQ, C_FK, C_FV, C_FF = 0, 256, 512, 1024, 1536, 1552, 2064, 2576, 3088

TILES = [(0, NMETA)] + [(NMETA + 512 * i, 512) for i in range(8)]
NT = len(TILES)
NKB = 33


def blocks_of(t0, T):
    return [(t0 + o, o, min(128, T - o)) for o in range(0, T, 128)]


def gblk(b0):
    return 0 if b0 == 0 else 1 + (b0 - NMETA) // 128


class Buf:
    __slots__ = ("name", "w", "r", "excl")

    def __init__(self, name, excl=False):
        self.name = name
        self.w = {}
        self.r = {}
        self.excl = excl


class _Eng:
    def __init__(self, name, h, sem):
        self.name, self.h, self.sem = name, h, sem
        self.count = 0
        self.seen = {}


class _Slot:
    def __init__(self, sem):
        self.sem = sem
        self.cnt = 0


class Tracker:
    def __init__(self, nc, es, n_work=24, n_wq=46):
        self.nc = nc
        self.e = {}
        for name, h in (("pe", nc.tensor), ("act", nc.scalar), ("dve", nc.vector),
                        ("pool", nc.gpsimd), ("sp", nc.sync)):
            self.e[name] = _Eng(name, h, es.enter_context(nc.semaphore("s_" + name)))
        self.work = [_Slot(es.enter_context(nc.semaphore(f"dw{i}"))) for i in range(n_work)]
        self.wq = [_Slot(es.enter_context(nc.semaphore(f"dq{i}"))) for i in range(n_wq)]
        self.pwork = [_Slot(es.enter_context(nc.semaphore(f"dp{i}"))) for i in range(10)]
        self.wi = 0
        self.qi = 0
        self.pi = 0
        import os, sys
        self.limit = int(os.environ.get("K_LIMIT", "0")) or None
        self.nops = 0
        self.log = []

    def _skip(self, engname):
        self.nops += 1
        if self.limit is not None:
            import sys
            self.log.append((self.nops, engname, sys._getframe(2).f_lineno))
            return self.nops > self.limit
        return False

    def _wait(self, eng, sem, val):
        k = id(sem)
        if eng.seen.get(k, 0) >= val:
            return
        eng.h.wait_ge(sem, val)
        eng.seen[k] = val

    def _deps(self, eng, reads, writes):
        need = {}

        def add(ev, raw):
            sem, val = ev
            if sem is eng.sem:
                if eng.name == "pe":
                    return
            k = id(sem)
            if k not in need or need[k][1] < val:
                need[k] = ev

        for b in reads:
            for ev in b.w.values():
                add(ev, True)
        for b in writes:
            for ev in b.w.values():
                add(ev, False)
            for ev in b.r.values():
                add(ev, False)
        for sem, val in need.values():
            self._wait(eng, sem, val)

    def op(self, engname, fn, reads=(), writes=(), partial=False):
        if self._skip(engname):
            return None
        eng = self.e[engname]
        xs = [b for b in reads if b.excl]
        if xs:
            writes = list(writes) + xs
        self._deps(eng, reads, writes)
        ins = fn(eng.h)
        eng.count += 1
        ins.then_inc(eng.sem, 1)
        ev = (eng.sem, eng.count)
        k = id(eng.sem)
        for b in reads:
            b.r[k] = ev
        for b in writes:
            if partial:
                b.w[k] = ev
            else:
                b.w = {k: ev}
                b.r = {}
        return ins

    def dma(self, qname, out, in_, reads=(), writes=(), weights=False):
        if self._skip("dma_" + qname):
            return None
        eng = self.e[qname]
        if weights:
            slot = self.wq[self.qi % len(self.wq)]
            self.qi += 1
        elif qname == "pool":
            slot = self.pwork[self.pi % len(self.pwork)]
            self.pi += 1
        else:
            slot = self.work[self.wi % len(self.work)]
            self.wi += 1
        self._deps(eng, reads, writes)
        if slot.cnt:
            self._wait(eng, slot.sem, slot.cnt)
        ins = eng.h.dma_start(out=out, in_=in_)
        slot.cnt += 16
        ins.then_inc(slot.sem, 16)
        ev = (slot.sem, slot.cnt)
        k = id(slot.sem)
        for b in reads:
            b.r[k] = ev
        for b in writes:
            b.w = {k: ev}
            b.r = {}

    def barrier(self):
        engs = list(self.e.values())
        for e in engs:
            for o in engs:
                if o is not e and o.count:
                    self._wait(e, o.sem, o.count)
            for s in self.work + self.pwork:
                if s.cnt:
                    self._wait(e, s.sem, s.cnt)

    def final_wait(self):
        sp = self.e["sp"]
        for s in self.work + self.wq + self.pwork:
            if s.cnt:
                self._wait(sp, s.sem, s.cnt)
        for o in self.e.values():
            if o is not sp and o.count:
                self._wait(sp, o.sem, o.count)


def build_nc(n_layers=DEPTH, dbg=False, stop_after=None):
    nc = bass.Bass("TRN2", target_bir_lowering=False)
    es = contextlib.ExitStack()
    with es:
        _build(nc, es, n_layers, dbg, stop_after)
    return nc


def _build(nc, es, n_layers, dbg, stop_after):
    def dram_in(name, shape, dt=F32):
        return nc.dram_tensor(name, list(shape), dt, kind="ExternalInput").ap()

    skind = "ExternalOutput" if dbg else "Internal"

    def dram_s(name, shape, dt):
        return nc.dram_tensor(name, list(shape), dt, kind=skind).ap()

    x_d = dram_in("x", [SEQ, D])
    meta_d = dram_in("meta", [NMETA, D])
    win_d = dram_in("w_in", [DEPTH, D, NIN])
    wout_d = dram_in("w_out", [DEPTH, D, D])
    wup_d = dram_in("w_up", [DEPTH, D, 2 * DFF])
    wdn_d = dram_in("w_down", [DEPTH, DFF, D])
    anorm_d = dram_in("anorm_bc", [DEPTH, 128, D])
    fnorm_d = dram_in("fnorm_bc", [DEPTH, 128, D])
    final_d = dram_in("final_bc", [128, D])
    wau_d = dram_in("wau", [DEPTH, 16, 256])
    balpha_d = dram_in("balpha_bc", [DEPTH, 128, 256])
    bforget_d = dram_in("bforget_bc", [DEPTH, 128, 8])
    gnorm_d = dram_in("gnorm_fm", [DEPTH, 128, 4])
    xnorm_d = dram_in("xnorm_fm", [DEPTH, 128, 4])
    convw_d = dram_in("convw_fm", [DEPTH, 128, 2 * NJ, 3])
    convb_d = dram_in("convb_fm", [DEPTH, 128, 2 * NJ])
    cf32_d = dram_in("cf32", [128, 4 * 128 + 8])
    cbf_d = dram_in("cbf", [128, 5 * 128], BF16)
    y_d = nc.dram_tensor("y", [SEQ, D], F32, kind="ExternalOutput").ap()

    H_d = dram_s("H", [L, D], F32)
    QT_d = dram_s("QT", [4, 128, L], BF16)
    KT_d = dram_s("KT", [4, 128, L], BF16)
    V_d = dram_s("V", [L, 768], BF16)
    CSB_d = dram_s("CSB", [128, NKB, 8], F32)
    CREF_d = dram_s("CREF", [128, NT, 8], F32)
    MIXT_d = dram_s("MIXT", [8, 128, L], BF16)

    tk = Tracker(nc, es)
    op, dma = tk.op, tk.dma

    uid = [0]

    def sb(name, shape, dt, stack=es):
        uid[0] += 1
        return stack.enter_context(nc.sbuf_tensor(f"sb{uid[0]}_{name}", list(shape), dt))

    R_up = sb("R_up", [128, 8 * 2 * DFF], BF16)
    B_Rup = Buf("R_up")
    win_v = R_up[:, 0:8 * NIN].rearrange("p (k n) -> p k n", k=8)
    wup_v = R_up[:, :].rearrange("p (k n) -> p k n", k=8)
    cbf = sb("cbf", [128, 5 * 128], BF16)
    B_const = Buf("const")
    IDENT = cbf[:, 0:128]
    TRIB = cbf[:, 128:256]
    ONES128 = cbf[:, 256:384]
    WA = cbf[:, 384:512]
    WB = cbf[:, 512:640]

    PT = es.enter_context(nc.psum_tensor("PT", [128, 1024], BF16))
    PS = [es.enter_context(nc.psum_tensor(f"PS{i}", [128, 512], F32)) for i in range(7)]
    B_PT = Buf("PT", True)
    B_PS = [Buf(f"PS{i}", True) for i in range(7)]

    dma("sp", cbf[:, :], cbf_d, writes=[B_const])

    B_H = [Buf(f"H{g}") for g in range(NKB)]
    B_QT = [Buf(f"QT{t}") for t in range(NT)]
    B_KT = [Buf(f"KT{t}") for t in range(NT)]
    B_V = [Buf(f"V{g}") for g in range(NKB)]
    B_CSB = Buf("CSB")
    B_CREF = Buf("CREF")
    B_MIXG = [Buf(f"MIXG{t}") for t in range(NT)]
    B_MIXF = [[Buf(f"MIXF{p}_{t}") for t in range(NT)] for p in range(4)]

    def h_rows(layer, b0, B, first_src):
        if layer == 0 and first_src:
            if b0 == 0:
                return meta_d[0:B, :]
            return x_d[b0 - NMETA:b0 - NMETA + B, :]
        return H_d[b0:b0 + B, :]

    def act(out, in_, func, bias=None, scale=None, accum_out=None):
        kw = {}
        if bias is not None:
            kw["bias"] = bias
        if scale is not None:
            kw["scale"] = scale
        if accum_out is not None:
            kw["accum_out"] = accum_out
        return lambda e: e.activation(out=out, in_=in_, func=func, **kw)

    def mm(out, lhsT, rhs, start=True, stop=True):
        return lambda e: e.matmul(out, lhsT=lhsT, rhs=rhs, start=start, stop=stop)

    def norm_transpose(ws, hb, B_hb, gbc, B_g, xnT, B_xnT, o, B):
        norm_part1(ws, hb, B_hb, gbc, B_g, B)
        norm_part2(ws, xnT, B_xnT, o, B)

    def norm_part1(ws, hb, B_hb, gbc, B_g, B):
        xn, B_xn, st, B_st = ws["xn"], ws["B_xn"], ws["st"], ws["B_st"]
        op("act", act(xn[0:B, :], hb[0:B, :], AF.Square, accum_out=st[0:B, 0:1]),
           reads=[B_hb], writes=[B_xn, B_st])
        op("act", act(st[0:B, 1:2], st[0:B, 0:1], AF.Ln, bias=EPS, scale=1.0 / D),
           reads=[B_st], writes=[B_st])
        op("act", act(st[0:B, 2:3], st[0:B, 1:2], AF.Exp, scale=-0.5),
           reads=[B_st], writes=[B_st])
        op("dve", lambda e: e.scalar_tensor_tensor(out=xn[0:B, :], in0=hb[0:B, :], scalar=st[0:B, 2:3],
                                                   in1=gbc[0:B, :], op0=ALU.mult, op1=ALU.mult),
           reads=[B_hb, B_st, B_g], writes=[B_xn])

    def norm_part2(ws, xnT, B_xnT, o, B):
        xn, B_xn = ws["xn"], ws["B_xn"]
        for k in range(8):
            op("pe", lambda e, k=k: e.transpose(out=PT[:, k * 128:k * 128 + B], in_=xn[0:B, k * 128:(k + 1) * 128],
                                                identity=IDENT[0:B, 0:B]),
               reads=[B_xn, B_const], writes=[B_PT])
        src = PT[:, :].rearrange("p (k t) -> p k t", k=8)[:, :, 0:B]
        op("act", lambda e: e.activation(out=xnT[:, :, o:o + B], in_=src, func=AF.Copy),
           reads=[B_PT], writes=[B_xnT], partial=True)

    for layer in range(n_layers):
        last = layer == n_layers - 1
        for k in range(8):
            dma("pool", win_v[:, k, :], win_d[layer, k * 128:(k + 1) * 128, :], writes=[B_Rup], weights=True)

        with contextlib.ExitStack() as pa, nc.named_scope(f"A{layer}"):
            def sba(name, shape, dt):
                return sb(name, shape, dt, pa)
            cf32 = sba("cf32", [128, 4 * 128 + 8], F32)
            B_cf = Buf("cf32")
            dma("sp", cf32[:, :], cf32_d, writes=[B_cf])
            TRI = cf32[:, 0:128]
            SU = cf32[:, 128:256]
            E127 = cf32[:, 256:384]
            E15 = cf32[:, 384:512]
            CH = cf32[:, 512:520]
            gbc = sba("gbc", [128, D], F32); B_gbc = Buf("gbc")
            balpha = sba("balpha", [128, 256], F32)
            bforget = sba("bforget", [128, 8], F32)
            wau32 = sba("wau32", [16, 256], F32)
            wau = sba("wau", [16, 256], BF16)
            gnorm = sba("gnorm", [128, 4], F32)
            B_pp = Buf("layer_params")
            dma("sp", gbc[:, :], anorm_d[layer], writes=[B_gbc])
            dma("sp", balpha[:, :], balpha_d[layer], writes=[B_pp])
            dma("sp", bforget[:, :], bforget_d[layer], writes=[B_pp])
            dma("sp", gnorm[:, :], gnorm_d[layer], writes=[B_pp])
            dma("sp", wau32[:, :], wau_d[layer], writes=[B_pp])
            op("dve", lambda e: e.tensor_copy(out=wau[:, :], in_=wau32[:, :]), reads=[B_pp], writes=[B_pp])

            hbs = [sba(f"hb{i}", [128, D], F32) for i in range(2)]
            B_hbs = [Buf(f"hb{i}") for i in range(2)]
            ws = dict(xn=sba("xn", [128, D], BF16), B_xn=Buf("xn"), st=sba("st", [128, 4], F32), B_st=Buf("st"))
            xnTs = [sba(f"xnT{i}", [128, 8, 512], BF16) for i in range(2)]
            B_xnTs = [Buf(f"xnT{i}") for i in range(2)]
            gqT = sba("gqT", [128, 2, 512], BF16); B_gqT = Buf("gqT")
            sgT = sba("sgT", [128, 4, 512], F32); B_sgT = Buf("sgT")
            glrT = sba("glrT", [16, 512], BF16); B_glrT = Buf("glrT")
            qT = sba("qT", [128, 4, 512], BF16); B_qT = Buf("qT")
            kT = sba("kT", [128, 4, 512], BF16); B_kT = Buf("kT")
            vts = [sba(f"vt{i}", [128, 768], BF16) for i in range(2)]
            B_vts = [Buf(f"vt{i}") for i in range(2)]
            gv = sba("gv", [128, 4, 512], BF16); B_gv = Buf("gv")
            kdec = sba("kdec", [128, 4, 256], BF16); B_kdec = Buf("kdec")
            zb = sba("zb", [128, 256], F32); B_zb = Buf("zb")
            lz = sba("lz", [128, 256], F32); B_lz = Buf("lz")
            wdec = sba("wdec", [128, 256], F32); B_wdec = Buf("wdec")
            f8 = sba("f8", [128, 16], F32); B_f8 = Buf("f8")
            cposs = [sba(f"cpos{i}", [128, 8], F32) for i in range(2)]
            B_cposs = [Buf(f"cpos{i}") for i in range(2)]
            crefs = sba("crefs", [128, NT, 8], F32); B_crefs = Buf("crefs")
            dec = sba("dec", [128, 2, 8], F32); B_dec = Buf("dec")
            S = sba("S", [128, 2, 128], F32); B_S = Buf("S")
            Sbs = [sba(f"Sb{i}", [128, 2, 128], BF16) for i in range(2)]
            B_Sbs = [Buf(f"Sb{i}") for i in range(2)]
            oraw = sba("oraw", [128, 4, 512], F32); B_oraw = Buf("oraw")
            sq2 = [sba(f"sq{i}", [128, 512], BF16) for i in range(2)]
            B_sq2 = [Buf(f"sq{i}") for i in range(2)]
            rs2 = [sba(f"rs{i}", [128, 512], F32) for i in range(2)]
            B_rs2 = [Buf(f"rs{i}") for i in range(2)]
            mixg = sba("mixg", [128, 4, 512], BF16); B_mixg = Buf("mixg")

            op("pool", lambda e: e.memset(S[:, :, :], 0.0), writes=[B_S])
            for i in range(2):
                op("pool", lambda e, i=i: e.memset(cposs[i][:, :], 0.0), writes=[B_cposs[i]])
            for i in range(2):
                op("pool", lambda e, i=i: e.memset(vts[i][:, :], 1.0), writes=[B_vts[i]])

            pa_i = [0]

            def next_pa():
                i = pa_i[0] % 2
                pa_i[0] += 1
                return PS[i], B_PS[i]
            PM1, B_PM1a, B_PM1b = PS[3], B_PS[3], B_PS[3]
            PM2 = PS[4]
            B_PM2w = B_PM2ff = B_PM2cum = B_PM2cref = B_PM2dec = B_PS[4]
            PU, B_PU = PS[5], [B_PS[5], B_PS[5]]
            PRs, B_PRs = [PS[6], PS[2]], [B_PS[6], B_PS[2]]
            PUv = PU[:, :].rearrange("p (b q c) -> p b q c", b=2, q=2)
            PRvs = [P_[:, :].rearrange("p (h c) -> p h c", h=8) for P_ in PRs]

            nblk_seen = 0
            chunk_g = 0
            for ti, (t0, T) in enumerate(TILES):
                blks = blocks_of(t0, T)
                C = 16 if ti == 0 else 64
                xnT, B_xnT = xnTs[ti % 2], B_xnTs[ti % 2]

                def norm_block(tj, bj):
                    tt0, TT = TILES[tj]
                    b0_, o_, B_ = blocks_of(tt0, TT)[bj]
                    g_ = gblk(b0_)
                    hb, B_hb = hbs[g_ % 2], B_hbs[g_ % 2]
                    dma("sp", hb[0:B_, :], h_rows(layer, b0_, B_, True), reads=[B_H[g_]], writes=[B_hb])
                    norm_part1(ws, hb, B_hb, gbc, B_gbc, B_)
                    return lambda: norm_part2(ws, xnTs[tj % 2], B_xnTs[tj % 2], o_, B_)

                if ti == 0:
                    norm_block(0, 0)()
                def proj_fm(col0, M, evac):
                    ps, B_ps = next_pa()
                    for k in range(8):
                        op("pe", mm(ps[0:M, 0:T], win_v[:, k, col0:col0 + M], xnT[:, k, 0:T], k == 0, k == 7),
                           reads=[B_Rup, B_xnT], writes=[B_ps])
                    evac(ps, B_ps)
                proj_fm(C_LR, 16, lambda ps, B_ps: op(
                    "dve", lambda e: e.tensor_copy(out=glrT[:, 0:T], in_=ps[0:16, 0:T]), reads=[B_ps], writes=[B_glrT]))
                def blk_gen(bi, b0, o, B):
                    g = gblk(b0)
                    nch = 1 if ti == 0 else 2

                    def proj_tm(ps_ap, B_ps, col0, N):
                        for k in range(8):
                            op("pe", mm(ps_ap, xnT[:, k, o:o + B], win_v[:, k, col0:col0 + N], k == 0, k == 7),
                               reads=[B_Rup, B_xnT], writes=[B_ps])
                    ps, B_ps = next_pa()
                    proj_tm(ps[0:B, 0:512], B_ps, C_FV, 512)
                    vt, B_vt = vts[g % 2], B_vts[g % 2]
                    vt4 = vt[:, :].rearrange("b (p s c) -> b p s c", p=4, s=3)
                    ps4 = ps[:, :].rearrange("b (p e c) -> b p e c", p=4, e=2)
                    for e_ in range(2):
                        op("dve", lambda e, e_=e_: e.tensor_copy(out=vt4[0:B, :, 2 * e_, :], in_=ps4[0:B, :, e_, :]),
                           reads=[B_ps], writes=[B_vt], partial=True)
                    dma("pool", V_d[b0:b0 + B, :], vt[0:B, :], reads=[B_vt], writes=[B_V[g]])
                    ps, B_ps = next_pa()
                    proj_tm(ps[0:B, 0:512], B_ps, C_GV, 512)
                    op("act", act(gv[0:B, bi, :], ps[0:B, 0:512], AF.Copy), reads=[B_ps], writes=[B_gv], partial=True)
                    yield 'heavy'
                    proj_tm(PM1[0:B, 0:256], B_PM1a, C_GK, 256)
                    proj_tm(PM2[0:B, 256:264], B_PM2ff, C_FF, 8)
                    op("dve", lambda e: e.tensor_tensor(out=f8[0:B, 0:8], in0=PM2[0:B, 256:264], in1=bforget[0:B, :], op=ALU.add),
                       reads=[B_PM2ff, B_pp], writes=[B_f8])
                    op("act", act(f8[0:B, 8:16], f8[0:B, 0:8], AF.Exp, scale=-1.0), reads=[B_f8], writes=[B_f8])
                    op("act", act(f8[0:B, 0:8], f8[0:B, 8:16], AF.Ln, bias=1.0), reads=[B_f8], writes=[B_f8])
                    cur, B_cur = cposs[g % 2], B_cposs[g % 2]
                    prv, B_prv = cposs[(g + 1) % 2], B_cposs[(g + 1) % 2]
                    op("pe", mm(PM2[0:B, 264:272], TRI[0:B, 0:B], f8[0:B, 0:8], True, g == 0),
                       reads=[B_cf, B_f8], writes=[B_PM2cum])
                    if g > 0:
                        Bp = 16 if g == 1 else 128
                        Es = E15 if g == 1 else E127
                        op("pe", mm(PM2[0:B, 264:272], Es[0:Bp, 0:B], prv[0:Bp, :], False, True),
                           reads=[B_cf, B_prv], writes=[B_PM2cum])
                    op("dve", lambda e: e.tensor_copy(out=cur[0:B, :], in_=PM2[0:B, 264:272]), reads=[B_PM2cum], writes=[B_cur])
                    Bw = 128 if g == 0 else B
                    dma("pool", CSB_d[0:Bw, g, :], cur[0:Bw, :], reads=[B_cur], writes=[B_CSB])
                    if (ti == 0 and bi == 0) or (ti > 0 and bi == 1):
                        Es = E15 if ti == 0 else E127
                        op("pe", mm(PM2[:, 272:280], Es[0:B, :], cur[0:B, :]), reads=[B_cf, B_cur], writes=[B_PM2cref])
                        op("dve", lambda e: e.tensor_copy(out=crefs[:, ti, :], in_=PM2[:, 272:280]),
                           reads=[B_PM2cref], writes=[B_crefs], partial=True)
                    op("pe", mm(PM1[0:B, 256:512], glrT[0:16, o:o + B], wau[:, :]), reads=[B_glrT, B_pp], writes=[B_PM1b])
                    op("dve", lambda e: e.tensor_tensor(out=zb[0:B, :], in0=PM1[0:B, 256:512], in1=balpha[0:B, :], op=ALU.add),
                       reads=[B_PM1b, B_pp], writes=[B_zb])
                    op("act", act(zb[0:B, :], zb[0:B, :], AF.Exp, scale=-1.0), reads=[B_zb], writes=[B_zb])
                    op("act", act(lz[0:B, :], zb[0:B, :], AF.Ln, bias=1.0), reads=[B_zb], writes=[B_lz])
                    yield 'chainA'
                    op("pe", mm(PM2[0:B, 0:256], SU[0:B, 0:B], lz[0:B, :]), reads=[B_cf, B_lz], writes=[B_PM2w])
                    op("act", act(wdec[0:B, :], PM2[0:B, 0:256], AF.Exp, scale=-1.0 / 16), reads=[B_PM2w], writes=[B_wdec])
                    op("dve", lambda e: e.tensor_tensor(out=kdec[0:B, bi, :], in0=PM1[0:B, 0:256], in1=wdec[0:B, :], op=ALU.mult),
                       reads=[B_PM1a, B_wdec], writes=[B_kdec], partial=True)
                    for p in range(2):
                        op("pe", mm(PM2[:, 280 + 8 * p:280 + 8 * p + 8], lz[0:B, p * 128:(p + 1) * 128], CH[0:B, 0:8]),
                           reads=[B_cf, B_lz], writes=[B_PM2dec])
                    PMd = PM2[:, 280:296].rearrange("p (q c) -> p q c", q=2)
                    op("act", act(dec[:, :, 2 * bi:2 * bi + nch], PMd[:, :, 0:nch], AF.Exp, scale=-1.0 / 16),
                       reads=[B_PM2dec], writes=[B_dec], partial=True)
                gens = [blk_gen(bi_, *blk_) for bi_, blk_ in enumerate(blks)]
                next(gens[0])
                for bi_ in range(len(gens)):
                    next(gens[bi_])
                    if bi_ + 1 < len(gens):
                        next(gens[bi_ + 1])
                    for _ in gens[bi_]:
                        pass
                for j in range(2):
                    proj_fm(C_GQ + 128 * j, 128, lambda ps, B_ps, j=j: op(
                        "act", act(gqT[:, j, 0:T], ps[:, 0:T], AF.Copy, scale=0.125), reads=[B_ps], writes=[B_gqT], partial=True))
                pq = []
                for j in range(4):
                    pq.append(lambda j=j: proj_fm(C_GR + 128 * j, 128, lambda ps, B_ps, j=j: op(
                        "act", act(sgT[:, j, 0:T], ps[:, 0:T], AF.Silu), reads=[B_ps], writes=[B_sgT], partial=True)))
                for j in range(4):
                    pq.append(lambda j=j: proj_fm(C_FQ + 128 * j, 128, lambda ps, B_ps, j=j: op(
                        "dve", lambda e: e.tensor_copy(out=qT[:, j, 0:T], in_=ps[:, 0:T]), reads=[B_ps], writes=[B_qT], partial=True)))
                pq.append(lambda: dma("pool", QT_d[:, :, t0:t0 + T].rearrange("c p t -> p c t"), qT[:, :, 0:T], reads=[B_qT], writes=[B_QT[ti]]))
                for j in range(4):
                    pq.append(lambda j=j: proj_fm(C_FK + 128 * j, 128, lambda ps, B_ps, j=j: op(
                        "dve", lambda e: e.tensor_copy(out=kT[:, j, 0:T], in_=ps[:, 0:T]), reads=[B_ps], writes=[B_kT], partial=True)))
                pq.append(lambda: dma("pool", KT_d[:, :, t0:t0 + T].rearrange("c p t -> p c t"), kT[:, :, 0:T], reads=[B_kT], writes=[B_KT[ti]]))
                nq = []
                part2 = []
                if ti + 1 < NT:
                    for bj in range(len(blocks_of(*TILES[ti + 1]))):
                        nq.append(lambda bj=bj: norm_block(ti + 1, bj))
                nchunks = T // C
                for c in range(nchunks):
                    bi = (c * C) // 128
                    r0 = (c * C) % 128
                    cb = 0
                    for h in range(4):
                        p, e_ = h // 2, h % 2
                        op("pe", mm(PUv[e_ * 64:(e_ + 1) * 64, cb, p, :], kdec[r0:r0 + C, bi, h * 64:(h + 1) * 64],
                                    gv[r0:r0 + C, bi, h * 128:(h + 1) * 128]),
                           reads=[B_kdec, B_gv], writes=[B_PU[cb]])
                    for p in range(2):
                        op("dve", lambda e, p=p: e.scalar_tensor_tensor(out=S[:, p, :], in0=S[:, p, :], scalar=dec[:, p, c:c + 1],
                                                                        in1=PUv[:, cb, p, :], op0=ALU.mult, op1=ALU.add),
                           reads=[B_S, B_dec, B_PU[cb]], writes=[B_S])
                    Sb, B_Sb = Sbs[cb], B_Sbs[cb]
                    op("act", act(Sb[:, :, :], S[:, :, :], AF.Copy), reads=[B_S], writes=[B_Sb])
                    for h in range(4):
                        p, e_ = h // 2, h % 2
                        op("pe", mm(PRvs[e_][:, p, 0:C], Sb[e_ * 64:(e_ + 1) * 64, p, :], gqT[e_ * 64:(e_ + 1) * 64, p, c * C:(c + 1) * C]),
                           reads=[B_Sb, B_gqT], writes=[B_PRs[e_]])
                    orv = oraw[:, :, :].rearrange("p (q e) t -> p q e t", e=2)
                    for e_ in range(2):
                        op("act" if e_ == 0 else "dve",
                           (lambda e, e_=e_: e.activation(out=orv[:, :, e_, c * C:(c + 1) * C], in_=PRvs[e_][:, 0:2, 0:C], func=AF.Copy)) if e_ == 0 else
                           (lambda e, e_=e_: e.tensor_copy(out=orv[:, :, e_, c * C:(c + 1) * C], in_=PRvs[e_][:, 0:2, 0:C])),
                           reads=[B_PRs[e_]], writes=[B_oraw], partial=True)
                    for _ in range(2):
                        if pq:
                            pq.pop(0)()
                    if part2:
                        part2.pop(0)()
                    if nq and (c % 2 == 1 or nchunks == 1):
                        part2.append(nq.pop(0)())
                while pq:
                    pq.pop(0)()
                while nq or part2:
                    if part2:
                        part2.pop(0)()
                    if nq:
                        part2.append(nq.pop(0)())
                stat = {}

                def fin_a(h):
                    op("act", act(sq2[h % 2][:, 0:T], oraw[:, h, 0:T], AF.Square), reads=[B_oraw], writes=[B_sq2[h % 2]])
                    ps, B_ps = next_pa()
                    op("pe", mm(ps[:, 0:T], ONES128, sq2[h % 2][:, 0:T]), reads=[B_const, B_sq2[h % 2]], writes=[B_ps])
                    stat[h] = (ps, B_ps)

                def fin_b(h):
                    ps, B_ps = stat[h]
                    r_, B_r = rs2[h % 2], B_rs2[h % 2]
                    op("act", act(r_[:, 0:T], ps[:, 0:T], AF.Ln, bias=EPS), reads=[B_ps], writes=[B_r])
                    op("act", act(r_[:, 0:T], r_[:, 0:T], AF.Exp, scale=-0.5), reads=[B_r], writes=[B_r])
                    op("dve", lambda e: e.scalar_tensor_tensor(out=r_[:, 0:T], in0=oraw[:, h, 0:T], scalar=gnorm[:, h:h + 1],
                                                               in1=r_[:, 0:T], op0=ALU.mult, op1=ALU.mult),
                       reads=[B_oraw, B_pp, B_r], writes=[B_r])
                    op("dve", lambda e: e.tensor_tensor(out=mixg[:, h, 0:T], in0=r_[:, 0:T], in1=sgT[:, h, 0:T], op=ALU.mult),
                       reads=[B_r, B_sgT], writes=[B_mixg], partial=True)

                fin_a(0)
                for h in range(4):
                    if h + 1 < 4:
                        fin_a(h + 1)
                    fin_b(h)
                dma("pool", MIXT_d[0:4, :, t0:t0 + T].rearrange("c p t -> p c t"), mixg[:, :, 0:T], reads=[B_mixg], writes=[B_MIXG[ti]])
            dma("pool", CREF_d, crefs[:, :, :], reads=[B_crefs], writes=[B_CREF])
            tk.barrier()
        if stop_after == "A":
            break

        lw = contextlib.ExitStack()
        R_dn = sb("R_dn", [128, NJ, D], BF16, lw); B_Rdn = Buf("R_dn")
        lo = contextlib.ExitStack()
        R_out = sb("R_out", [128, 8, D], BF16, lo); B_Rout = Buf("R_out")
        for k in range(8):
            dma("pool", R_out[:, k, :], wout_d[layer, k * 128:(k + 1) * 128, :], writes=[B_Rout], weights=True)
        for k in range(8):
            dma("pool", wup_v[:, k, :], wup_d[layer, k * 128:(k + 1) * 128, :], writes=[B_Rup], weights=True)
        for j in range(NJ):
            dma("pool", R_dn[:, j, :], wdn_d[layer, j * 128:(j + 1) * 128, :], writes=[B_Rdn], weights=True)

        with contextlib.ExitStack() as pb, nc.named_scope(f"B{layer}"):
            def sbb(name, shape, dt):
                return sb(name, shape, dt, pb)
            KTs = [sbb("KTs0", [128, L], BF16)] * 2
            B_KTs = [Buf("KTs0")] * 2
            Vps = [sbb(f"Vp{i}", [128, NKB, 192], BF16) for i in range(2)]
            B_Vps = [Buf(f"Vp{i}") for i in range(2)]
            csb = sbb("csb", [128, NKB, 8], F32); B_csb = Buf("csb")
            cref = sbb("cref", [128, NT, 8], F32); B_cref = Buf("cref")
            xnorm = sbb("xnorm", [128, 4], F32); B_xnorm = Buf("xnorm")
            biases = [sbb(f"bias{i}", [128, NKB, 2], F32) for i in range(2)]
            B_biases = [Buf(f"bias{i}") for i in range(2)]
            qts = [sbb(f"qt{i}", [128, 512], BF16) for i in range(2)]
            B_qts = [Buf(f"qt{i}") for i in range(2)]
            pts = [sbb(f"pt{i}", [128, 512], BF16) for i in range(4)]
            B_pts = [Buf(f"pt{i}") for i in range(4)]
            pcs = [sbb(f"pc{i}", [128, 512], F32) for i in range(2)]
            B_pcs = [Buf(f"pc{i}") for i in range(2)]
            sqs = [sbb("sqb0", [128, 512], BF16)]
            B_sqs = [Buf("sqb0")]
            rd = sbb("rd", [128, 512], F32); B_rd = Buf("rd")
            onorm = sbb("onorm", [128, 512], F32); B_on = Buf("onorm")
            rsb = sbb("rsb", [128, 512], F32); B_rsb = Buf("rsb")
            ots = [sbb(f"ot{i}", [128, 512], BF16) for i in range(2)]
            B_ots = [Buf(f"ot{i}") for i in range(2)]

            dma("sp", csb[:, :, :], CSB_d, reads=[B_CSB], writes=[B_csb])
            dma("sp", cref[:, :, :], CREF_d, reads=[B_CREF], writes=[B_cref])
            dma("sp", xnorm[:, :], xnorm_d[layer], writes=[B_xnorm])
            SBK = [[(PS[0], B_PS[0]), (PS[1], B_PS[1])], [(PS[2], B_PS[2]), (PS[6], B_PS[6])]]
            iters = [(p, ti) for p in range(4) for ti in range(NT)]

            def kbs_of(ti):
                if ti == 0:
                    return [(0, 0, 16)]
                return [(0, 0, 16)] + [(1 + j, NMETA + 128 * j, 128) for j in range(4 * ti)]

            def load_kt(p):
                dma("sp", KTs[p % 2][:, :], KT_d[p], reads=B_KT, writes=[B_KTs[p % 2]])

            def load_pair(p):
                Vp, B_Vp = Vps[p % 2], B_Vps[p % 2]
                dma("sp", Vp[0:16, 0, :], V_d[0:16, 192 * p:192 * p + 192], reads=B_V, writes=[B_Vp])
                dma("sp", Vp[:, 1:NKB, :], V_d[16:L, 192 * p:192 * p + 192].rearrange("(n q) c -> q n c", q=128),
                    reads=B_V, writes=[B_Vp])

            def load_q(n):
                p, ti = iters[n]
                t0, T = TILES[ti]
                nkb = len(kbs_of(ti))
                dma("sp", qts[n % 2][:, 0:T], QT_d[p, :, t0:t0 + T], reads=[B_QT[ti]], writes=[B_qts[n % 2]])
                op("dve", lambda e: e.tensor_tensor(
                    out=biases[n % 2][:, 0:nkb, :], in0=csb[:, 0:nkb, 2 * p:2 * p + 2],
                    in1=cref[:, ti:ti + 1, 2 * p:2 * p + 2].to_broadcast([128, nkb, 2]), op=ALU.subtract),
                   reads=[B_csb, B_cref], writes=[B_biases[n % 2]])

            def epilogue_dve(n):
                p, ti = iters[n]
                t0, T = TILES[ti]
                for e_ in range(2):
                    orow = slice(e_ * 64, (e_ + 1) * 64)
                    drow = slice((1 - e_) * 64, (2 - e_) * 64)
                    op("dve", lambda e: e.reciprocal(out=rd[orow, 0:T], in_=pcs[e_][drow, 0:T]), reads=[B_pcs[e_]], writes=[B_rd], partial=True)
                    op("dve", lambda e: e.tensor_tensor(out=onorm[orow, 0:T], in0=pcs[e_][orow, 0:T], in1=rd[orow, 0:T], op=ALU.mult),
                       reads=[B_pcs[e_], B_rd], writes=[B_on], partial=True)

            def epilogue(n):
                p, ti = iters[n]
                t0, T = TILES[ti]
                ot, B_ot = ots[n % 2], B_ots[n % 2]
                op("act", act(sqs[0][:, 0:T], onorm[:, 0:T], AF.Square), reads=[B_on], writes=[B_sqs[0]])
                pst, B_pst = PS[5], B_PS[5]
                op("pe", mm(pst[:, 0:T], WA, sqs[0][:, 0:T]), reads=[B_const, B_sqs[0]], writes=[B_pst])
                op("act", act(rsb[:, 0:T], pst[:, 0:T], AF.Ln, bias=EPS), reads=[B_pst], writes=[B_rsb])
                op("act", act(rsb[:, 0:T], rsb[:, 0:T], AF.Exp, scale=-0.5), reads=[B_rsb], writes=[B_rsb])
                op("dve", lambda e: e.scalar_tensor_tensor(
                    out=ot[:, 0:T], in0=onorm[:, 0:T], scalar=xnorm[:, p:p + 1],
                    in1=rsb[:, 0:T], op0=ALU.mult, op1=ALU.mult),
                   reads=[B_on, B_xnorm, B_rsb], writes=[B_ot])
                dma("sp", MIXT_d[4 + p, :, t0:t0 + T], ot[:, 0:T], reads=[B_ot], writes=[B_MIXF[p][ti]])

            load_kt(0)
            load_pair(0)
            load_q(0)
            deferred = []
            for n, (p, ti) in enumerate(iters):
                t0, T = TILES[ti]
                KTp, B_KTp = KTs[p % 2], B_KTs[p % 2]
                Vp, B_Vp = Vps[p % 2], B_Vps[p % 2]
                qt, B_qt = qts[n % 2], B_qts[n % 2]
                bias, B_bias = biases[n % 2], B_biases[n % 2]
                kbs = kbs_of(ti)
                nkb = len(kbs)

                def emit_s(idx):
                    g, k0, KB = kbs[idx]
                    qa = max(0, k0 - t0)
                    for e_ in range(2):
                        rows = slice(e_ * 64, (e_ + 1) * 64)
                        ps, B_ps = SBK[e_][idx % 2]
                        op("pe", mm(ps[0:KB, qa:T], KTp[rows, k0:k0 + KB], qt[rows, qa:T]),
                           reads=[B_KTp, B_qt], writes=[B_ps])

                emit_s(0)
                if n + 1 < len(iters):
                    if iters[n + 1][0] != p:
                        load_pair(iters[n + 1][0])
                    load_q(n + 1)
                for idx in range(nkb):
                    g, k0, KB = kbs[idx]
                    qa = max(0, k0 - t0)
                    diag = k0 + KB > t0
                    if idx + 1 < nkb:
                        emit_s(idx + 1)
                    for e_ in range(2):
                        ps, B_ps = SBK[e_][idx % 2]
                        pt, B_pt = pts[2 * e_ + idx % 2], B_pts[2 * e_ + idx % 2]
                        po, B_po = PS[3 + e_], B_PS[3 + e_]
                        op("act", act(pt[0:KB, qa:T], ps[0:KB, qa:T], AF.Exp, bias=bias[0:KB, g, e_:e_ + 1], scale=0.125),
                           reads=[B_ps, B_bias], writes=[B_pt])
                        if diag:
                            op("dve", lambda e: e.tensor_tensor(out=pt[0:KB, qa:qa + KB], in0=pt[0:KB, qa:qa + KB],
                                                                in1=TRIB[0:KB, 0:KB], op=ALU.mult),
                               reads=[B_pt, B_const], writes=[B_pt])
                        op("pe", mm(po[:, qa:T], Vp[0:KB, g, e_ * 64:e_ * 64 + 128], pt[0:KB, qa:T], idx == 0, idx == nkb - 1),
                           reads=[B_Vp, B_pt], writes=[B_po])
                    if deferred and idx == min(7, nkb - 1):
                        epilogue(deferred.pop(0))
                for e_ in range(2):
                    op("dve", lambda e, e_=e_: e.tensor_copy(out=pcs[e_][:, 0:T], in_=PS[3 + e_][:, 0:T]),
                       reads=[B_PS[3 + e_]], writes=[B_pcs[e_]])
                epilogue_dve(n)
                deferred.append(n)
                if n + 1 < len(iters) and iters[n + 1][0] != p:
                    load_kt(iters[n + 1][0])
            while deferred:
                epilogue(deferred.pop(0))
            tk.barrier()
        if stop_after == "B":
            lo.close()
            lw.close()
            break

        with contextlib.ExitStack() as pc, nc.named_scope(f"C{layer}"):
            def sbc(name, shape, dt):
                return sb(name, shape, dt, pc)
            mixts = [sbc(f"mixt{i}", [128, 8, 512], BF16) for i in range(2)]
            B_mixts = [Buf(f"mixt{i}") for i in range(2)]
            hbs = [sbc(f"hbc{i}", [128, D], F32) for i in range(2)]
            B_hbs = [Buf(f"hbc{i}") for i in range(2)]
            for ti, (t0, T) in enumerate(TILES):
                mt, B_mt = mixts[ti % 2], B_mixts[ti % 2]
                dma("sp", mt[:, :, 0:T], MIXT_d[:, :, t0:t0 + T].rearrange("c p t -> p c t"),
                    reads=[B_MIXG[ti]] + [B_MIXF[p][ti] for p in range(4)], writes=[B_mt])
                for bi, (b0, o, B) in enumerate(blocks_of(t0, T)):
                    g = gblk(b0)
                    hb, B_hb = hbs[g % 2], B_hbs[g % 2]
                    dma("sp", hb[0:B, :], h_rows(layer, b0, B, True), reads=[B_H[g]], writes=[B_hb])
                    for half in range(2):
                        ps, B_ps = PS[half], B_PS[half]
                        for c in range(8):
                            op("pe", mm(ps[0:B, :], mt[:, c, o:o + B], R_out[:, c, half * 512:(half + 1) * 512], c == 0, c == 7),
                               reads=[B_mt, B_Rout], writes=[B_ps])
                        op("dve", lambda e, half=half, ps=ps: e.tensor_tensor(
                            out=hb[0:B, half * 512:(half + 1) * 512], in0=ps[0:B, :], in1=hb[0:B, half * 512:(half + 1) * 512], op=ALU.add),
                           reads=[B_ps, B_hb], writes=[B_hb], partial=True)
                    dma("pool", H_d[b0:b0 + B, :], hb[0:B, :], reads=[B_hb], writes=[B_H[g]])
            tk.barrier()
        lo.close()
        if stop_after == "C":
            lw.close()
            break

        with contextlib.ExitStack() as pd, nc.named_scope(f"D{layer}"):
            def sbd(name, shape, dt):
                return sb(name, shape, dt, pd)
            g2bc = sbd("g2bc", [128, D], F32); B_g2 = Buf("g2bc")
            convw = sbd("convw", [128, 2 * NJ, 3], F32)
            convb = sbd("convb", [128, 2 * NJ], F32)
            B_cv = Buf("conv")
            dma("sp", g2bc[:, :], fnorm_d[layer], writes=[B_g2])
            dma("sp", convw[:, :, :], convw_d[layer], writes=[B_cv])
            dma("sp", convb[:, :], convb_d[layer], writes=[B_cv])
            if last:
                fbc = sbd("fbc", [128, D], F32); B_fbc = Buf("fbc")
                dma("sp", fbc[:, :], final_d, writes=[B_fbc])
            halo = sbd("halo", [128, 2 * NJ, 2], F32); B_halo = Buf("halo")
            op("pool", lambda e: e.memset(halo[:, :, :], 0.0), writes=[B_halo])
            hbs = [sbd(f"hbd{i}", [128, D], F32) for i in range(2)]
            B_hbs = [Buf(f"hbd{i}") for i in range(2)]
            if last:
                hbx = sbd("hbx", [128, D], F32)
                B_hbx = Buf("hbx")
                hbn, B_hbn = [hbx, hbx], [B_hbx, B_hbx]
            else:
                hbn = [sbd(f"hbn{i}", [128, D], F32) for i in range(2)]
                B_hbn = [Buf(f"hbn{i}") for i in range(2)]
            ws = dict(xn=sbd("xnd", [128, D], BF16), B_xn=Buf("xnd"), st=sbd("std", [128, 4], F32), B_st=Buf("std"))
            xnT = sbd("xnTd", [128, 8, 512], BF16); B_xnT = Buf("xnTd")
            hbufs = [sbd(f"hbuf{i}", [128, 514], F32) for i in range(4)]
            B_hbufs = [Buf(f"hbuf{i}") for i in range(4)]
            B_hhalo = [Buf(f"hhalo{i}") for i in range(4)]
            t1s = [sbd(f"t1_{i}", [128, 512], F32) for i in range(6)]
            B_t1s = [Buf(f"t1_{i}") for i in range(6)]
            actT = sbd("actT", [128, NJ, 512], BF16); B_actT = Buf("actT")
            hcnt = 0
            for ti, (t0, T) in enumerate(TILES):
                blks = blocks_of(t0, T)

                def norm_block_d(tj, bj):
                    tt0, TT = TILES[tj]
                    b0_, o_, B_ = blocks_of(tt0, TT)[bj]
                    g_ = gblk(b0_)
                    hb, B_hb = hbn[g_ % 2], B_hbn[g_ % 2]
                    dma("sp", hb[0:B_, :], H_d[b0_:b0_ + B_, :], reads=[B_H[g_]], writes=[B_hb])
                    norm_part1(ws, hb, B_hb, g2bc, B_g2, B_)
                    return lambda: norm_part2(ws, xnT, B_xnT, o_, B_)

                if ti == 0:
                    norm_block_d(0, 0)()
                nq = []
                part2 = []
                if ti + 1 < NT:
                    for bj in range(len(blocks_of(*TILES[ti + 1]))):
                        nq.append(lambda bj=bj: norm_block_d(ti + 1, bj))
                pend = []
                silu_done = set()

                def fin(item):
                    pj, ((tu, B_tu), (tg, B_tg)) = item
                    if pj not in silu_done:
                        op("act", act(tg[:, 0:T], tg[:, 0:T], AF.Silu), reads=[B_tg], writes=[B_tg])
                    op("dve", lambda e: e.tensor_tensor(out=actT[:, pj, 0:T], in0=tu[:, 0:T], in1=tg[:, 0:T], op=ALU.mult),
                       reads=[B_tu, B_tg], writes=[B_actT], partial=True)

                for j in range(NJ):
                    pss = []
                    for br in range(2):
                        ps, B_ps = PS[2 + 2 * (j % 2) + br], B_PS[2 + 2 * (j % 2) + br]
                        col0 = br * DFF + j * 128
                        for k in range(8):
                            op("pe", mm(ps[:, 0:T], wup_v[:, k, col0:col0 + 128], xnT[:, k, 0:T], k == 0, k == 7),
                               reads=[B_Rup, B_xnT], writes=[B_ps])
                        pss.append((ps, B_ps))
                    t3 = []
                    for br in range(2):
                        ps, B_ps = pss[br]
                        ci = br * NJ + j
                        bi_ = 2 * (j % 2) + br
                        hbuf, B_hbuf = hbufs[bi_], B_hbufs[bi_]
                        t1, B_t1 = t1s[2 * (j % 3) + br], B_t1s[2 * (j % 3) + br]
                        op("pool", lambda e, ci=ci, hbuf=hbuf: e.tensor_copy(out=hbuf[:, 0:2], in_=halo[:, ci, :]),
                           reads=[B_halo], writes=[B_hhalo[bi_]])
                        op("act", act(hbuf[:, 2:2 + T], ps[:, 0:T], AF.Copy), reads=[B_ps], writes=[B_hbuf])
                        op("act", act(t1[:, 0:T], ps[:, 0:T], AF.Identity, bias=convb[:, ci:ci + 1], scale=convw[:, ci, 2:3]),
                           reads=[B_ps, B_cv], writes=[B_t1])
                        op("pool", lambda e, ci=ci, hbuf=hbuf: e.tensor_copy(out=halo[:, ci, :], in_=hbuf[:, T:T + 2]),
                           reads=[B_hbuf], writes=[B_halo], partial=True)
                        t3.append((t1, B_t1))
                    if pend:
                        pj, ((ptu, B_ptu), (ptg, B_ptg)) = pend[0]
                        op("act", act(ptg[:, 0:T], ptg[:, 0:T], AF.Silu), reads=[B_ptg], writes=[B_ptg])
                        silu_done.add(pj)
                    for br in range(2):
                        ci = br * NJ + j
                        bi_ = 2 * (j % 2) + br
                        hbuf, B_hbuf = hbufs[bi_], B_hbufs[bi_]
                        t1, B_t1 = t1s[2 * (j % 3) + br], B_t1s[2 * (j % 3) + br]
                        op("dve", lambda e, ci=ci, hbuf=hbuf, t1=t1: e.scalar_tensor_tensor(
                            out=t1[:, 0:T], in0=hbuf[:, 1:1 + T], scalar=convw[:, ci, 1:2], in1=t1[:, 0:T], op0=ALU.mult, op1=ALU.add),
                           reads=[B_hbuf, B_hhalo[bi_], B_cv, B_t1], writes=[B_t1])
                        op("dve", lambda e, ci=ci, hbuf=hbuf, t1=t1: e.scalar_tensor_tensor(
                            out=t1[:, 0:T], in0=hbuf[:, 0:T], scalar=convw[:, ci, 0:1], in1=t1[:, 0:T], op0=ALU.mult, op1=ALU.add),
                           reads=[B_hbuf, B_hhalo[bi_], B_cv, B_t1], writes=[B_t1])
                    pend.append((j, t3))
                    if len(pend) > 1:
                        fin(pend.pop(0))
                while pend:
                    fin(pend.pop(0))
                for bi, (b0, o, B) in enumerate(blks):
                    g = gblk(b0)
                    hb, B_hb = hbs[hcnt % 2], B_hbs[hcnt % 2]
                    hcnt += 1
                    dma("sp", hb[0:B, :], H_d[b0:b0 + B, :], reads=[B_H[g]], writes=[B_hb])
                    if nq:
                        part2.append(nq.pop(0)())
                    for half in range(2):
                        ps, B_ps = PS[half], B_PS[half]
                        for j in range(NJ):
                            op("pe", mm(ps[0:B, :], actT[:, j, o:o + B], R_dn[:, j, half * 512:(half + 1) * 512], j == 0, j == NJ - 1),
                               reads=[B_actT, B_Rdn], writes=[B_ps])
                        op("dve", lambda e, half=half, ps=ps, hb=hb: e.tensor_tensor(
                            out=hb[0:B, half * 512:(half + 1) * 512], in0=ps[0:B, :], in1=hb[0:B, half * 512:(half + 1) * 512], op=ALU.add),
                           reads=[B_ps, B_hb], writes=[B_hb], partial=True)
                    if not last:
                        dma("pool", H_d[b0:b0 + B, :], hb[0:B, :], reads=[B_hb], writes=[B_H[g]])
                    elif ti > 0:
                        st, B_st = ws["st"], ws["B_st"]
                        xn, B_xn = t1s[0][:, :].bitcast(BF16), B_t1s[0]
                        op("act", act(xn[0:B, :], hb[0:B, :], AF.Square, accum_out=st[0:B, 0:1]), reads=[B_hb], writes=[B_xn, B_st])
                        op("act", act(st[0:B, 1:2], st[0:B, 0:1], AF.Ln, bias=EPS, scale=1.0 / D), reads=[B_st], writes=[B_st])
                        op("act", act(st[0:B, 2:3], st[0:B, 1:2], AF.Exp, scale=-0.5), reads=[B_st], writes=[B_st])
                        op("dve", lambda e, hb=hb: e.scalar_tensor_tensor(out=hb[0:B, :], in0=hb[0:B, :], scalar=st[0:B, 2:3],
                                                                        in1=fbc[0:B, :], op0=ALU.mult, op1=ALU.mult),
                           reads=[B_hb, B_st, B_fbc], writes=[B_hb])
                        dma("pool", y_d[b0 - NMETA:b0 - NMETA + B, :], hb[0:B, :], reads=[B_hb], writes=[B_H[g]])
                    if part2:
                        part2.pop(0)()
                while nq or part2:
                    if part2:
                        part2.pop(0)()
                    if nq:
                        part2.append(nq.pop(0)())
            tk.barrier()
        lw.close()
    tk.final_wait()
    if tk.limit is not None:
        print("K_LIMIT", tk.limit, "total ops", tk.nops, "last emitted:", [x for x in tk.log if x[0] in (tk.limit - 1, tk.limit, tk.limit + 1)])


def _consts():
    s = np.arange(128)[:, None]
    t = np.arange(128)[None, :]
    tri = (s <= t).astype(np.float32)
    su = ((s > t) & (s // 64 == t // 64)).astype(np.float32)
    e127 = np.zeros((128, 128), np.float32); e127[127, :] = 1.0
    e15 = np.zeros((128, 128), np.float32); e15[15, :] = 1.0
    ch = (s // 64 == np.arange(8)[None, :]).astype(np.float32)
    cf32 = np.concatenate([tri, su, e127, e15, ch], axis=1)
    ident = np.eye(128, dtype=np.float32)
    ones128 = np.full((128, 128), 1.0 / 128, np.float32)
    wa = np.zeros((128, 128), np.float32); wa[0:64, 0:64] = 1.0 / 64; wa[64:128, 64:128] = 1.0 / 64
    wb = np.zeros((128, 128), np.float32); wb[0:64, 64:128] = EPS / 64; wb[64:128, 64:128] = 1.0 / 64
    cbf = np.concatenate([ident, tri, ones128, wa, wb], axis=1).astype(ml_dtypes.bfloat16)
    return np.ascontiguousarray(cf32), np.ascontiguousarray(cbf)


def make_in_maps(inputs, cores=range(8)):
    f = lambda a: np.ascontiguousarray(np.asarray(a, dtype=np.float32))
    x = f(inputs["x"])
    bc = lambda a: np.ascontiguousarray(np.broadcast_to(f(a)[:, None, :], (a.shape[0], 128, a.shape[1])))
    cf32, cbf = _consts()
    conv_w = f(inputs["conv_w"])
    shared = {
        "meta": f(inputs["meta_tokens"]),
        "w_in": f(inputs["w_in"]), "w_out": f(inputs["w_out"]), "w_up": f(inputs["w_up"]), "w_down": f(inputs["w_down"]),
        "anorm_bc": bc(np.asarray(inputs["attn_norm"])), "fnorm_bc": bc(np.asarray(inputs["ffn_norm"])),
        "final_bc": np.ascontiguousarray(np.broadcast_to(f(inputs["final_norm"])[None, :], (128, D))),
        "wau": f(inputs["w_alpha_up"]),
        "balpha_bc": bc(np.asarray(inputs["b_alpha"])), "bforget_bc": bc(np.asarray(inputs["b_forget"])),
        "gnorm_fm": np.ascontiguousarray(f(inputs["gla_norm"]).reshape(DEPTH, 4, 128).transpose(0, 2, 1)),
        "xnorm_fm": np.ascontiguousarray(f(inputs["fox_norm"]).reshape(DEPTH, 4, 128).transpose(0, 2, 1)),
        "convw_fm": np.ascontiguousarray(conv_w.reshape(DEPTH, 3, 2 * NJ, 128).transpose(0, 3, 2, 1)),
        "convb_fm": np.ascontiguousarray(f(inputs["conv_b"]).reshape(DEPTH, 2 * NJ, 128).transpose(0, 2, 1)),
        "cf32": cf32, "cbf": cbf,
    }
    return [dict(shared, x=np.ascontiguousarray(x[c])) for c in cores]


_NC_CACHE = {}


def kernel(**inputs):
    if "nc" not in _NC_CACHE:
        _NC_CACHE["nc"] = build_nc()
    nc = _NC_CACHE["nc"]
    in_maps = make_in_maps(inputs)
    res = run_bass_kernel_spmd(nc, in_maps, core_ids=list(range(8)))
    return np.stack([np.asarray(r["y"], dtype=np.float32) for r in res.results], axis=0)
```

```python
import contextlib
import numpy as np
import ml_dtypes
import concourse.bass as bass
import concourse.mybir as mybir
from concourse.bass_utils import run_bass_kernel_spmd

F32 = mybir.dt.float32
BF16 = mybir.dt.bfloat16
ALU = mybir.AluOpType
AF = mybir.ActivationFunctionType

D = 1024
SEQ = 4096
NMETA = 16
L = SEQ + NMETA
DEPTH = 4
NIN = 3096
DFF = 2816
NJ = DFF // 128
EPS = 1e-6
C_GQ, C_GK, C_GV, C_GR, C_LR, C_FQ, C_FK, C_FV, C_FF = 0, 256, 512, 1024, 1536, 1552, 2064, 2576, 3088

TILES = [(0, NMETA)] + [(NMETA + 512 * i, 512) for i in range(8)]
NT = len(TILES)
NKB = 33


def blocks_of(t0, T):
    return [(t0 + o, o, min(128, T - o)) for o in range(0, T, 128)]


def gblk(b0):
    return 0 if b0 == 0 else 1 + (b0 - NMETA) // 128


class Buf:
    __slots__ = ("name", "w", "r", "excl")

    def __init__(self, name, excl=False):
        self.name = name
        self.w = {}
        self.r = {}
        self.excl = excl


class _Eng:
    def __init__(self, name, h, sem):
        self.name, self.h, self.sem = name, h, sem
        self.count = 0
        self.seen = {}


class _Slot:
    def __init__(self, sem):
        self.sem = sem
        self.cnt = 0


class Tracker:
    def __init__(self, nc, es, n_work=24, n_wq=46):
        self.nc = nc
        self.e = {}
        for name, h in (("pe", nc.tensor), ("act", nc.scalar), ("dve", nc.vector),
                        ("pool", nc.gpsimd), ("sp", nc.sync)):
            self.e[name] = _Eng(name, h, es.enter_context(nc.semaphore("s_" + name)))
        self.work = [_Slot(es.enter_context(nc.semaphore(f"dw{i}"))) for i in range(n_work)]
        self.wq = [_Slot(es.enter_context(nc.semaphore(f"dq{i}"))) for i in range(n_wq)]
        self.pwork = [_Slot(es.enter_context(nc.semaphore(f"dp{i}"))) for i in range(10)]
        self.wi = 0
        self.qi = 0
        self.pi = 0
        import os, sys
        self.limit = int(os.environ.get("K_LIMIT", "0")) or None
        self.nops = 0
        self.log = []

    def _skip(self, engname):
        self.nops += 1
        if self.limit is not None:
            import sys
            self.log.append((self.nops, engname, sys._getframe(2).f_lineno))
            return self.nops > self.limit
        return False

    def _wait(self, eng, sem, val):
        k = id(sem)
        if eng.seen.get(k, 0) >= val:
            return
        eng.h.wait_ge(sem, val)
        eng.seen[k] = val

    def _deps(self, eng, reads, writes):
        need = {}

        def add(ev, raw):
            sem, val = ev
            if sem is eng.sem:
                if eng.name == "pe":
                    return
            k = id(sem)
            if k not in need or need[k][1] < val:
                need[k] = ev

        for b in reads:
            for ev in b.w.values():
                add(ev, True)
        for b in writes:
            for ev in b.w.values():
                add(ev, False)
            for ev in b.r.values():
                add(ev, False)
        for sem, val in need.values():
            self._wait(eng, sem, val)

    def op(self, engname, fn, reads=(), writes=(), partial=False):
        if self._skip(engname):
            return None
        eng = self.e[engname]
        xs = [b for b in reads if b.excl]
        if xs:
            writes = list(writes) + xs
        self._deps(eng, reads, writes)
        ins = fn(eng.h)
        eng.count += 1
        ins.then_inc(eng.sem, 1)
        ev = (eng.sem, eng.count)
        k = id(eng.sem)
        for b in reads:
            b.r[k] = ev
        for b in writes:
            if partial:
                b.w[k] = ev
            else:
                b.w = {k: ev}
                b.r = {}
        return ins

    def dma(self, qname, out, in_, reads=(), writes=(), weights=False):
        if self._skip("dma_" + qname):
            return None
        eng = self.e[qname]
        if weights:
            slot = self.wq[self.qi % len(self.wq)]
            self.qi += 1
        elif qname == "pool":
            slot = self.pwork[self.pi % len(self.pwork)]
            self.pi += 1
        else:
            slot = self.work[self.wi % len(self.work)]
            self.wi += 1
        self._deps(eng, reads, writes)
        if slot.cnt:
            self._wait(eng, slot.sem, slot.cnt)
        ins = eng.h.dma_start(out=out, in_=in_)
        slot.cnt += 16
        ins.then_inc(slot.sem, 16)
        ev = (slot.sem, slot.cnt)
        k = id(slot.sem)
        for b in reads:
            b.r[k] = ev
        for b in writes:
            b.w = {k: ev}
            b.r = {}

    def barrier(self):
        engs = list(self.e.values())
        for e in engs:
            for o in engs:
                if o is not e and o.count:
                    self._wait(e, o.sem, o.count)
            for s in self.work + self.pwork:
                if s.cnt:
                    self._wait(e, s.sem, s.cnt)

    def final_wait(self):
        sp = self.e["sp"]
        for s in self.work + self.wq + self.pwork:
            if s.cnt:
                self._wait(sp, s.sem, s.cnt)
        for o in self.e.values():
            if o is not sp and o.count:
                self._wait(sp, o.sem, o.count)


def build_nc(n_layers=DEPTH, dbg=False, stop_after=None):
    nc = bass.Bass("TRN2", target_bir_lowering=False)
    es = contextlib.ExitStack()
    with es:
        _build(nc, es, n_layers, dbg, stop_after)
    return nc


def _build(nc, es, n_layers, dbg, stop_after):
    def dram_in(name, shape, dt=F32):
        return nc.dram_tensor(name, list(shape), dt, kind="ExternalInput").ap()

    skind = "ExternalOutput" if dbg else "Internal"

    def dram_s(name, shape, dt):
        return nc.dram_tensor(name, list(shape), dt, kind=skind).ap()

    x_d = dram_in("x", [SEQ, D])
    meta_d = dram_in("meta", [NMETA, D])
    win_d = dram_in("w_in", [DEPTH, D, NIN])
    wout_d = dram_in("w_out", [DEPTH, D, D])
    wup_d = dram_in("w_up", [DEPTH, D, 2 * DFF])
    wdn_d = dram_in("w_down", [DEPTH, DFF, D])
    anorm_d = dram_in("anorm_bc", [DEPTH, 128, D])
    fnorm_d = dram_in("fnorm_bc", [DEPTH, 128, D])
    final_d = dram_in("final_bc", [128, D])
    wau_d = dram_in("wau", [DEPTH, 16, 256])
    balpha_d = dram_in("balpha_bc", [DEPTH, 128, 256])
    bforget_d = dram_in("bforget_bc", [DEPTH, 128, 8])
    gnorm_d = dram_in("gnorm_fm", [DEPTH, 128, 4])
    xnorm_d = dram_in("xnorm_fm", [DEPTH, 128, 4])
    convw_d = dram_in("convw_fm", [DEPTH, 128, 2 * NJ, 3])
    convb_d = dram_in("convb_fm", [DEPTH, 128, 2 * NJ])
    cf32_d = dram_in("cf32", [128, 4 * 128 + 8])
    cbf_d = dram_in("cbf", [128, 5 * 128], BF16)
    y_d = nc.dram_tensor("y", [SEQ, D], F32, kind="ExternalOutput").ap()

    H_d = dram_s("H", [L, D], F32)
    QT_d = dram_s("QT", [4, 128, L], BF16)
    KT_d = dram_s("KT", [4, 128, L], BF16)
    V_d = dram_s("V", [L, 768], BF16)
    CSB_d = dram_s("CSB", [128, NKB, 8], F32)
    CREF_d = dram_s("CREF", [128, NT, 8], F32)
    MIXT_d = dram_s("MIXT", [8, 128, L], BF16)

    tk = Tracker(nc, es)
    op, dma = tk.op, tk.dma

    uid = [0]

    def sb(name, shape, dt, stack=es):
        uid[0] += 1
        return stack.enter_context(nc.sbuf_tensor(f"sb{uid[0]}_{name}", list(shape), dt))

    R_up = sb("R_up", [128, 8 * 2 * DFF], BF16)
    B_Rup = Buf("R_up")
    win_v = R_up[:, 0:8 * NIN].rearrange("p (k n) -> p k n", k=8)
    wup_v = R_up[:, :].rearrange("p (k n) -> p k n", k=8)
    cbf = sb("cbf", [128, 5 * 128], BF16)
    B_const = Buf("const")
    IDENT = cbf[:, 0:128]
    TRIB = cbf[:, 128:256]
    ONES128 = cbf[:, 256:384]
    WA = cbf[:, 384:512]
    WB = cbf[:, 512:640]

    PT = es.enter_context(nc.psum_tensor("PT", [128, 1024], BF16))
    PS = [es.enter_context(nc.psum_tensor(f"PS{i}", [128, 512], F32)) for i in range(7)]
    B_PT = Buf("PT", True)
    B_PS = [Buf(f"PS{i}", True) for i in range(7)]

    dma("sp", cbf[:, :], cbf_d, writes=[B_const])

    B_H = [Buf(f"H{g}") for g in range(NKB)]
    B_QT = [Buf(f"QT{t}") for t in range(NT)]
    B_KT = [Buf(f"KT{t}") for t in range(NT)]
    B_V = [Buf(f"V{g}") for g in range(NKB)]
    B_CSB = Buf("CSB")
    B_CREF = Buf("CREF")
    B_MIXG = [Buf(f"MIXG{t}") for t in range(NT)]
    B_MIXF = [[Buf(f"MIXF{p}_{t}") for t in range(NT)] for p in range(4)]

    def h_rows(layer, b0, B, first_src):
        if layer == 0 and first_src:
            if b0 == 0:
                return meta_d[0:B, :]
            return x_d[b0 - NMETA:b0 - NMETA + B, :]
        return H_d[b0:b0 + B, :]

    def act(out, in_, func, bias=None, scale=None, accum_out=None):
        kw = {}
        if bias is not None:
            kw["bias"] = bias
        if scale is not None:
            kw["scale"] = scale
        if accum_out is not None:
            kw["accum_out"] = accum_out
        return lambda e: e.activation(out=out, in_=in_, func=func, **kw)

    def mm(out, lhsT, rhs, start=True, stop=True):
        return lambda e: e.matmul(out, lhsT=lhsT, rhs=rhs, start=start, stop=stop)

    def norm_transpose(ws, hb, B_hb, gbc, B_g, xnT, B_xnT, o, B):
        norm_part1(ws, hb, B_hb, gbc, B_g, B)
        norm_part2(ws, xnT, B_xnT, o, B)

    def norm_part1(ws, hb, B_hb, gbc, B_g, B):
        xn, B_xn, st, B_st = ws["xn"], ws["B_xn"], ws["st"], ws["B_st"]
        op("act", act(xn[0:B, :], hb[0:B, :], AF.Square, accum_out=st[0:B, 0:1]),
           reads=[B_hb], writes=[B_xn, B_st])
        op("act", act(st[0:B, 1:2], st[0:B, 0:1], AF.Ln, bias=EPS, scale=1.0 / D),
           reads=[B_st], writes=[B_st])
        op("act", act(st[0:B, 2:3], st[0:B, 1:2], AF.Exp, scale=-0.5),
           reads=[B_st], writes=[B_st])
        op("dve", lambda e: e.scalar_tensor_tensor(out=xn[0:B, :], in0=hb[0:B, :], scalar=st[0:B, 2:3],
                                                   in1=gbc[0:B, :], op0=ALU.mult, op1=ALU.mult),
           reads=[B_hb, B_st, B_g], writes=[B_xn])

    def norm_part2(ws, xnT, B_xnT, o, B):
        xn, B_xn = ws["xn"], ws["B_xn"]
        for k in range(8):
            op("pe", lambda e, k=k: e.transpose(out=PT[:, k * 128:k * 128 + B], in_=xn[0:B, k * 128:(k + 1) * 128],
                                                identity=IDENT[0:B, 0:B]),
               reads=[B_xn, B_const], writes=[B_PT])
        src = PT[:, :].rearrange("p (k t) -> p k t", k=8)[:, :, 0:B]
        op("act", lambda e: e.activation(out=xnT[:, :, o:o + B], in_=src, func=AF.Copy),
           reads=[B_PT], writes=[B_xnT], partial=True)

    for layer in range(n_layers):
        last = layer == n_layers - 1
        for k in range(8):
            dma("pool", win_v[:, k, :], win_d[layer, k * 128:(k + 1) * 128, :], writes=[B_Rup], weights=True)

        with contextlib.ExitStack() as pa, nc.named_scope(f"A{layer}"):
            def sba(name, shape, dt):
                return sb(name, shape, dt, pa)
            cf32 = sba("cf32", [128, 4 * 128 + 8], F32)
            B_cf = Buf("cf32")
            dma("sp", cf32[:, :], cf32_d, writes=[B_cf])
            TRI = cf32[:, 0:128]
            SU = cf32[:, 128:256]
            E127 = cf32[:, 256:384]
            E15 = cf32[:, 384:512]
            CH = cf32[:, 512:520]
            gbc = sba("gbc", [128, D], F32); B_gbc = Buf("gbc")
            balpha = sba("balpha", [128, 256], F32)
            bforget = sba("bforget", [128, 8], F32)
            wau32 = sba("wau32", [16, 256], F32)
            wau = sba("wau", [16, 256], BF16)
            gnorm = sba("gnorm", [128, 4], F32)
            B_pp = Buf("layer_params")
            dma("sp", gbc[:, :], anorm_d[layer], writes=[B_gbc])
            dma("sp", balpha[:, :], balpha_d[layer], writes=[B_pp])
            dma("sp", bforget[:, :], bforget_d[layer], writes=[B_pp])
            dma("sp", gnorm[:, :], gnorm_d[layer], writes=[B_pp])
            dma("sp", wau32[:, :], wau_d[layer], writes=[B_pp])
            op("dve", lambda e: e.tensor_copy(out=wau[:, :], in_=wau32[:, :]), reads=[B_pp], writes=[B_pp])

            hbs = [sba(f"hb{i}", [128, D], F32) for i in range(2)]
            B_hbs = [Buf(f"hb{i}") for i in range(2)]
            ws = dict(xn=sba("xn", [128, D], BF16), B_xn=Buf("xn"), st=sba("st", [128, 4], F32), B_st=Buf("st"))
            xnTs = [sba(f"xnT{i}", [128, 8, 512], BF16) for i in range(2)]
            B_xnTs = [Buf(f"xnT{i}") for i in range(2)]
            gqT = sba("gqT", [128, 2, 512], BF16); B_gqT = Buf("gqT")
            sgT = sba("sgT", [128, 4, 512], F32); B_sgT = Buf("sgT")
            glrT = sba("glrT", [16, 512], BF16); B_glrT = Buf("glrT")
            qT = sba("qT", [128, 4, 512], BF16); B_qT = Buf("qT")
            kT = sba("kT", [128, 4, 512], BF16); B_kT = Buf("kT")
            vts = [sba(f"vt{i}", [128, 768], BF16) for i in range(2)]
            B_vts = [Buf(f"vt{i}") for i in range(2)]
            gv = sba("gv", [128, 4, 512], BF16); B_gv = Buf("gv")
            kdec = sba("kdec", [128, 4, 256], BF16); B_kdec = Buf("kdec")
            zb = sba("zb", [128, 256], F32); B_zb = Buf("zb")
            lz = sba("lz", [128, 256], F32); B_lz = Buf("lz")
            wdec = sba("wdec", [128, 256], F32); B_wdec = Buf("wdec")
            f8 = sba("f8", [128, 16], F32); B_f8 = Buf("f8")
            cposs = [sba(f"cpos{i}", [128, 8], F32) for i in range(2)]
            B_cposs = [Buf(f"cpos{i}") for i in range(2)]
            crefs = sba("crefs", [128, NT, 8], F32); B_crefs = Buf("crefs")
            dec = sba("dec", [128, 2, 8], F32); B_dec = Buf("dec")
            S = sba("S", [128, 2, 128], F32); B_S = Buf("S")
            Sbs = [sba(f"Sb{i}", [128, 2, 128], BF16) for i in range(2)]
            B_Sbs = [Buf(f"Sb{i}") for i in range(2)]
            oraw = sba("oraw", [128, 4, 512], F32); B_oraw = Buf("oraw")
            sq2 = [sba(f"sq{i}", [128, 512], BF16) for i in range(2)]
            B_sq2 = [Buf(f"sq{i}") for i in range(2)]
            rs2 = [sba(f"rs{i}", [128, 512], F32) for i in range(2)]
            B_rs2 = [Buf(f"rs{i}") for i in range(2)]
            mixg = sba("mixg", [128, 4, 512], BF16); B_mixg = Buf("mixg")

            op("pool", lambda e: e.memset(S[:, :, :], 0.0), writes=[B_S])
            for i in range(2):
                op("pool", lambda e, i=i: e.memset(cposs[i][:, :], 0.0), writes=[B_cposs[i]])
            for i in range(2):
                op("pool", lambda e, i=i: e.memset(vts[i][:, :], 1.0), writes=[B_vts[i]])

            pa_i = [0]

            def next_pa():
                i = pa_i[0] % 2
                pa_i[0] += 1
                return PS[i], B_PS[i]
            PM1, B_PM1a, B_PM1b = PS[3], B_PS[3], B_PS[3]
            PM2 = PS[4]
            B_PM2w = B_PM2ff = B_PM2cum = B_PM2cref = B_PM2dec = B_PS[4]
            PU, B_PU = PS[5], [B_PS[5], B_PS[5]]
            PRs, B_PRs = [PS[6], PS[2]], [B_PS[6], B_PS[2]]
            PUv = PU[:, :].rearrange("p (b q c) -> p b q c", b=2, q=2)
            PRvs = [P_[:, :].rearrange("p (h c) -> p h c", h=8) for P_ in PRs]

            nblk_seen = 0
            chunk_g = 0
            for ti, (t0, T) in enumerate(TILES):
                blks = blocks_of(t0, T)
                C = 16 if ti == 0 else 64
                xnT, B_xnT = xnTs[ti % 2], B_xnTs[ti % 2]

                def norm_block(tj, bj):
                    tt0, TT = TILES[tj]
                    b0_, o_, B_ = blocks_of(tt0, TT)[bj]
                    g_ = gblk(b0_)
                    hb, B_hb = hbs[g_ % 2], B_hbs[g_ % 2]
                    dma("sp", hb[0:B_, :], h_rows(layer, b0_, B_, True), reads=[B_H[g_]], writes=[B_hb])
                    norm_part1(ws, hb, B_hb, gbc, B_gbc, B_)
                    return lambda: norm_part2(ws, xnTs[tj % 2], B_xnTs[tj % 2], o_, B_)

                if ti == 0:
                    norm_block(0, 0)()
                def proj_fm(col0, M, evac):
                    ps, B_ps = next_pa()
                    for k in range(8):
                        op("pe", mm(ps[0:M, 0:T], win_v[:, k, col0:col0 + M], xnT[:, k, 0:T], k == 0, k == 7),
                           reads=[B_Rup, B_xnT], writes=[B_ps])
                    evac(ps, B_ps)
                proj_fm(C_LR, 16, lambda ps, B_ps: op(
                    "dve", lambda e: e.tensor_copy(out=glrT[:, 0:T], in_=ps[0:16, 0:T]), reads=[B_ps], writes=[B_glrT]))
                def blk_gen(bi, b0, o, B):
                    g = gblk(b0)
                    nch = 1 if ti == 0 else 2

                    def proj_tm(ps_ap, B_ps, col0, N):
                        for k in range(8):
                            op("pe", mm(ps_ap, xnT[:, k, o:o + B], win_v[:, k, col0:col0 + N], k == 0, k == 7),
                               reads=[B_Rup, B_xnT], writes=[B_ps])
                    ps, B_ps = next_pa()
                    proj_tm(ps[0:B, 0:512], B_ps, C_FV, 512)
                    vt, B_vt = vts[g % 2], B_vts[g % 2]
                    vt4 = vt[:, :].rearrange("b (p s c) -> b p s c", p=4, s=3)
                    ps4 = ps[:, :].rearrange("b (p e c) -> b p e c", p=4, e=2)
                    for e_ in range(2):
                        op("dve", lambda e, e_=e_: e.tensor_copy(out=vt4[0:B, :, 2 * e_, :], in_=ps4[0:B, :, e_, :]),
                           reads=[B_ps], writes=[B_vt], partial=True)
                    dma("pool", V_d[b0:b0 + B, :], vt[0:B, :], reads=[B_vt], writes=[B_V[g]])
                    ps, B_ps = next_pa()
                    proj_tm(ps[0:B, 0:512], B_ps, C_GV, 512)
                    op("act", act(gv[0:B, bi, :], ps[0:B, 0:512], AF.Copy), reads=[B_ps], writes=[B_gv], partial=True)
                    yield 'heavy'
                    proj_tm(PM1[0:B, 0:256], B_PM1a, C_GK, 256)
                    proj_tm(PM2[0:B, 256:264], B_PM2ff, C_FF, 8)
                    op("dve", lambda e: e.tensor_tensor(out=f8[0:B, 0:8], in0=PM2[0:B, 256:264], in1=bforget[0:B, :], op=ALU.add),
                       reads=[B_PM2ff, B_pp], writes=[B_f8])
                    op("act", act(f8[0:B, 8:16], f8[0:B, 0:8], AF.Exp, scale=-1.0), reads=[B_f8], writes=[B_f8])
                    op("act", act(f8[0:B, 0:8], f8[0:B, 8:16], AF.Ln, bias=1.0), reads=[B_f8], writes=[B_f8])
                    cur, B_cur = cposs[g % 2], B_cposs[g % 2]
                    prv, B_prv = cposs[(g + 1) % 2], B_cposs[(g + 1) % 2]
                    op("pe", mm(PM2[0:B, 264:272], TRI[0:B, 0:B], f8[0:B, 0:8], True, g == 0),
                       reads=[B_cf, B_f8], writes=[B_PM2cum])
                    if g > 0:
                        Bp = 16 if g == 1 else 128
                        Es = E15 if g == 1 else E127
                        op("pe", mm(PM2[0:B, 264:272], Es[0:Bp, 0:B], prv[0:Bp, :], False, True),
                           reads=[B_cf, B_prv], writes=[B_PM2cum])
                    op("dve", lambda e: e.tensor_copy(out=cur[0:B, :], in_=PM2[0:B, 264:272]), reads=[B_PM2cum], writes=[B_cur])
                    Bw = 128 if g == 0 else B
                    dma("pool", CSB_d[0:Bw, g, :], cur[0:Bw, :], reads=[B_cur], writes=[B_CSB])
                    if (ti == 0 and bi == 0) or (ti > 0 and bi == 1):
                        Es = E15 if ti == 0 else E127
                        op("pe", mm(PM2[:, 272:280], Es[0:B, :], cur[0:B, :]), reads=[B_cf, B_cur], writes=[B_PM2cref])
                        op("dve", lambda e: e.tensor_copy(out=crefs[:, ti, :], in_=PM2[:, 272:280]),
                           reads=[B_PM2cref], writes=[B_crefs], partial=True)
                    op("pe", mm(PM1[0:B, 256:512], glrT[0:16, o:o + B], wau[:, :]), reads=[B_glrT, B_pp], writes=[B_PM1b])
                    op("dve", lambda e: e.tensor_tensor(out=zb[0:B, :], in0=PM1[0:B, 256:512], in1=balpha[0:B, :], op=ALU.add),
                       reads=[B_PM1b, B_pp], writes=[B_zb])
                    op("act", act(zb[0:B, :], zb[0:B, :], AF.Exp, scale=-1.0), reads=[B_zb], writes=[B_zb])
                    op("act", act(lz[0:B, :], zb[0:B, :], AF.Ln, bias=1.0), reads=[B_zb], writes=[B_lz])
                    yield 'chainA'
                    op("pe", mm(PM2[0:B, 0:256], SU[0:B, 0:B], lz[0:B, :]), reads=[B_cf, B_lz], writes=[B_PM2w])
                    op("act", act(wdec[0:B, :], PM2[0:B, 0:256], AF.Exp, scale=-1.0 / 16), reads=[B_PM2w], writes=[B_wdec])
                    op("dve", lambda e: e.tensor_tensor(out=kdec[0:B, bi, :], in0=PM1[0:B, 0:256], in1=wdec[0:B, :], op=ALU.mult),
                       reads=[B_PM1a, B_wdec], writes=[B_kdec], partial=True)
                    for p in range(2):
                        op("pe", mm(PM2[:, 280 + 8 * p:280 + 8 * p + 8], lz[0:B, p * 128:(p + 1) * 128], CH[0:B, 0:8]),
                           reads=[B_cf, B_lz], writes=[B_PM2dec])
                    PMd = PM2[:, 280:296].rearrange("p (q c) -> p q c", q=2)
                    op("act", act(dec[:, :, 2 * bi:2 * bi + nch], PMd[:, :, 0:nch], AF.Exp, scale=-1.0 / 16),
                       reads=[B_PM2dec], writes=[B_dec], partial=True)
                gens = [blk_gen(bi_, *blk_) for bi_, blk_ in enumerate(blks)]
                next(gens[0])
                for bi_ in range(len(gens)):
                    next(gens[bi_])
                    if bi_ + 1 < len(gens):
                        next(gens[bi_ + 1])
                    for _ in gens[bi_]:
                        pass
                for j in range(2):
                    proj_fm(C_GQ + 128 * j, 128, lambda ps, B_ps, j=j: op(
                        "act", act(gqT[:, j, 0:T], ps[:, 0:T], AF.Copy, scale=0.125), reads=[B_ps], writes=[B_gqT], partial=True))
                pq = []
                for j in range(4):
                    pq.append(lambda j=j: proj_fm(C_GR + 128 * j, 128, lambda ps, B_ps, j=j: op(
                        "act", act(sgT[:, j, 0:T], ps[:, 0:T], AF.Silu), reads=[B_ps], writes=[B_sgT], partial=True)))
                for j in range(4):
                    pq.append(lambda j=j: proj_fm(C_FQ + 128 * j, 128, lambda ps, B_ps, j=j: op(
                        "dve", lambda e: e.tensor_copy(out=qT[:, j, 0:T], in_=ps[:, 0:T]), reads=[B_ps], writes=[B_qT], partial=True)))
                pq.append(lambda: dma("pool", QT_d[:, :, t0:t0 + T].rearrange("c p t -> p c t"), qT[:, :, 0:T], reads=[B_qT], writes=[B_QT[ti]]))
                for j in range(4):
                    pq.append(lambda j=j: proj_fm(C_FK + 128 * j, 128, lambda ps, B_ps, j=j: op(
                        "dve", lambda e: e.tensor_copy(out=kT[:, j, 0:T], in_=ps[:, 0:T]), reads=[B_ps], writes=[B_kT], partial=True)))
                pq.append(lambda: dma("pool", KT_d[:, :, t0:t0 + T].rearrange("c p t -> p c t"), kT[:, :, 0:T], reads=[B_kT], writes=[B_KT[ti]]))
                nq = []
                part2 = []
                if ti + 1 < NT:
                    for bj in range(len(blocks_of(*TILES[ti + 1]))):
                        nq.append(lambda bj=bj: norm_block(ti + 1, bj))
                nchunks = T // C
                for c in range(nchunks):
                    bi = (c * C) // 128
                    r0 = (c * C) % 128
                    cb = 0
                    for h in range(4):
                        p, e_ = h // 2, h % 2
                        op("pe", mm(PUv[e_ * 64:(e_ + 1) * 64, cb, p, :], kdec[r0:r0 + C, bi, h * 64:(h + 1) * 64],
                                    gv[r0:r0 + C, bi, h * 128:(h + 1) * 128]),
                           reads=[B_kdec, B_gv], writes=[B_PU[cb]])
                    for p in range(2):
                        op("dve", lambda e, p=p: e.scalar_tensor_tensor(out=S[:, p, :], in0=S[:, p, :], scalar=dec[:, p, c:c + 1],
                                                                        in1=PUv[:, cb, p, :], op0=ALU.mult, op1=ALU.add),
                           reads=[B_S, B_dec, B_PU[cb]], writes=[B_S])
                    Sb, B_Sb = Sbs[cb], B_Sbs[cb]
                    op("act", act(Sb[:, :, :], S[:, :, :], AF.Copy), reads=[B_S], writes=[B_Sb])
                    for h in range(4):
                        p, e_ = h // 2, h % 2
                        op("pe", mm(PRvs[e_][:, p, 0:C], Sb[e_ * 64:(e_ + 1) * 64, p, :], gqT[e_ * 64:(e_ + 1) * 64, p, c * C:(c + 1) * C]),
                           reads=[B_Sb, B_gqT], writes=[B_PRs[e_]])
                    orv = oraw[:, :, :].rearrange("p (q e) t -> p q e t", e=2)
                    for e_ in range(2):
                        op("act" if e_ == 0 else "dve",
                           (lambda e, e_=e_: e.activation(out=orv[:, :, e_, c * C:(c + 1) * C], in_=PRvs[e_][:, 0:2, 0:C], func=AF.Copy)) if e_ == 0 else
                           (lambda e, e_=e_: e.tensor_copy(out=orv[:, :, e_, c * C:(c + 1) * C], in_=PRvs[e_][:, 0:2, 0:C])),
                           reads=[B_PRs[e_]], writes=[B_oraw], partial=True)
                    for _ in range(2):
                        if pq:
                            pq.pop(0)()
                    if part2:
                        part2.pop(0)()
                    if nq and (c % 2 == 1 or nchunks == 1):
                        part2.append(nq.pop(0)())
                while pq:
                    pq.pop(0)()
                while nq or part2:
                    if part2:
                        part2.pop(0)()
                    if nq:
                        part2.append(nq.pop(0)())
                stat = {}

                def fin_a(h):
                    op("act", act(sq2[h % 2][:, 0:T], oraw[:, h, 0:T], AF.Square), reads=[B_oraw], writes=[B_sq2[h % 2]])
                    ps, B_ps = next_pa()
                    op("pe", mm(ps[:, 0:T], ONES128, sq2[h % 2][:, 0:T]), reads=[B_const, B_sq2[h % 2]], writes=[B_ps])
                    stat[h] = (ps, B_ps)

                def fin_b(h):
                    ps, B_ps = stat[h]
                    r_, B_r = rs2[h % 2], B_rs2[h % 2]
                    op("act", act(r_[:, 0:T], ps[:, 0:T], AF.Ln, bias=EPS), reads=[B_ps], writes=[B_r])
                    op("act", act(r_[:, 0:T], r_[:, 0:T], AF.Exp, scale=-0.5), reads=[B_r], writes=[B_r])
                    op("dve", lambda e: e.scalar_tensor_tensor(out=r_[:, 0:T], in0=oraw[:, h, 0:T], scalar=gnorm[:, h:h + 1],
                                                               in1=r_[:, 0:T], op0=ALU.mult, op1=ALU.mult),
                       reads=[B_oraw, B_pp, B_r], writes=[B_r])
                    op("dve", lambda e: e.tensor_tensor(out=mixg[:, h, 0:T], in0=r_[:, 0:T], in1=sgT[:, h, 0:T], op=ALU.mult),
                       reads=[B_r, B_sgT], writes=[B_mixg], partial=True)

                fin_a(0)
                for h in range(4):
                    if h + 1 < 4:
                        fin_a(h + 1)
                    fin_b(h)
                dma("pool", MIXT_d[0:4, :, t0:t0 + T].rearrange("c p t -> p c t"), mixg[:, :, 0:T], reads=[B_mixg], writes=[B_MIXG[ti]])
            dma("pool", CREF_d, crefs[:, :, :], reads=[B_crefs], writes=[B_CREF])
            tk.barrier()
        if stop_after == "A":
            break

        lw = contextlib.ExitStack()
        R_dn = sb("R_dn", [128, NJ, D], BF16, lw); B_Rdn = Buf("R_dn")
        lo = contextlib.ExitStack()
        R_out = sb("R_out", [128, 8, D], BF16, lo); B_Rout = Buf("R_out")
        for k in range(8):
            dma("pool", R_out[:, k, :], wout_d[layer, k * 128:(k + 1) * 128, :], writes=[B_Rout], weights=True)
        for k in range(8):
            dma("pool", wup_v[:, k, :], wup_d[layer, k * 128:(k + 1) * 128, :], writes=[B_Rup], weights=True)
        for j in range(NJ):
            dma("pool", R_dn[:, j, :], wdn_d[layer, j * 128:(j + 1) * 128, :], writes=[B_Rdn], weights=True)

        with contextlib.ExitStack() as pb, nc.named_scope(f"B{layer}"):
            def sbb(name, shape, dt):
                return sb(name, shape, dt, pb)
            KTs = [sbb("KTs0", [128, L], BF16)] * 2
            B_KTs = [Buf("KTs0")] * 2
            Vps = [sbb(f"Vp{i}", [128, NKB, 192], BF16) for i in range(2)]
            B_Vps = [Buf(f"Vp{i}") for i in range(2)]
            csb = sbb("csb", [128, NKB, 8], F32); B_csb = Buf("csb")
            cref = sbb("cref", [128, NT, 8], F32); B_cref = Buf("cref")
            xnorm = sbb("xnorm", [128, 4], F32); B_xnorm = Buf("xnorm")
            biases = [sbb(f"bias{i}", [128, NKB, 2], F32) for i in range(2)]
            B_biases = [Buf(f"bias{i}") for i in range(2)]
            qts = [sbb(f"qt{i}", [128, 512], BF16) for i in range(2)]
            B_qts = [Buf(f"qt{i}") for i in range(2)]
            pts = [sbb(f"pt{i}", [128, 512], BF16) for i in range(4)]
            B_pts = [Buf(f"pt{i}") for i in range(4)]
            pcs = [sbb(f"pc{i}", [128, 512], F32) for i in range(2)]
            B_pcs = [Buf(f"pc{i}") for i in range(2)]
            sqs = [sbb("sqb0", [128, 512], BF16)]
            B_sqs = [Buf("sqb0")]
            rd = sbb("rd", [128, 512], F32); B_rd = Buf("rd")
            onorm = sbb("onorm", [128, 512], F32); B_on = Buf("onorm")
            rsb = sbb("rsb", [128, 512], F32); B_rsb = Buf("rsb")
            ots = [sbb(f"ot{i}", [128, 512], BF16) for i in range(2)]
            B_ots = [Buf(f"ot{i}") for i in range(2)]

            dma("sp", csb[:, :, :], CSB_d, reads=[B_CSB], writes=[B_csb])
            dma("sp", cref[:, :, :], CREF_d, reads=[B_CREF], writes=[B_cref])
            dma("sp", xnorm[:, :], xnorm_d[layer], writes=[B_xnorm])
            SBK = [[(PS[0], B_PS[0]), (PS[1], B_PS[1])], [(PS[2], B_PS[2]), (PS[6], B_PS[6])]]
            iters = [(p, ti) for p in range(4) for ti in range(NT)]

            def kbs_of(ti):
                if ti == 0:
                    return [(0, 0, 16)]
                return [(0, 0, 16)] + [(1 + j, NMETA + 128 * j, 128) for j in range(4 * ti)]

            def load_kt(p):
                dma("sp", KTs[p % 2][:, :], KT_d[p], reads=B_KT, writes=[B_KTs[p % 2]])

            def load_pair(p):
                Vp, B_Vp = Vps[p % 2], B_Vps[p % 2]
                dma("sp", Vp[0:16, 0, :], V_d[0:16, 192 * p:192 * p + 192], reads=B_V, writes=[B_Vp])
                dma("sp", Vp[:, 1:NKB, :], V_d[16:L, 192 * p:192 * p + 192].rearrange("(n q) c -> q n c", q=128),
                    reads=B_V, writes=[B_Vp])

            def load_q(n):
                p, ti = iters[n]
                t0, T = TILES[ti]
                nkb = len(kbs_of(ti))
                dma("sp", qts[n % 2][:, 0:T], QT_d[p, :, t0:t0 + T], reads=[B_QT[ti]], writes=[B_qts[n % 2]])
                op("dve", lambda e: e.tensor_tensor(
                    out=biases[n % 2][:, 0:nkb, :], in0=csb[:, 0:nkb, 2 * p:2 * p + 2],
                    in1=cref[:, ti:ti + 1, 2 * p:2 * p + 2].to_broadcast([128, nkb, 2]), op=ALU.subtract),
                   reads=[B_csb, B_cref], writes=[B_biases[n % 2]])

            def epilogue_dve(n):
                p, ti = iters[n]
                t0, T = TILES[ti]
                for e_ in range(2):
                    orow = slice(e_ * 64, (e_ + 1) * 64)
                    drow = slice((1 - e_) * 64, (2 - e_) * 64)
                    op("dve", lambda e: e.reciprocal(out=rd[orow, 0:T], in_=pcs[e_][drow, 0:T]), reads=[B_pcs[e_]], writes=[B_rd], partial=True)
                    op("dve", lambda e: e.tensor_tensor(out=onorm[orow, 0:T], in0=pcs[e_][orow, 0:T], in1=rd[orow, 0:T], op=ALU.mult),
                       reads=[B_pcs[e_], B_rd], writes=[B_on], partial=True)

            def epilogue(n):
                p, ti = iters[n]
                t0, T = TILES[ti]
                ot, B_ot = ots[n % 2], B_ots[n % 2]
                op("act", act(sqs[0][:, 0:T], onorm[:, 0:T], AF.Square), reads=[B_on], writes=[B_sqs[0]])
                pst, B_pst = PS[5], B_PS[5]
                op("pe", mm(pst[:, 0:T], WA, sqs[0][:, 0:T]), reads=[B_const, B_sqs[0]], writes=[B_pst])
                op("act", act(rsb[:, 0:T], pst[:, 0:T], AF.Ln, bias=EPS), reads=[B_pst], writes=[B_rsb])
                op("act", act(rsb[:, 0:T], rsb[:, 0:T], AF.Exp, scale=-0.5), reads=[B_rsb], writes=[B_rsb])
                op("dve", lambda e: e.scalar_tensor_tensor(
                    out=ot[:, 0:T], in0=onorm[:, 0:T], scalar=xnorm[:, p:p + 1],
                    in1=rsb[:, 0:T], op0=ALU.mult, op1=ALU.mult),
                   reads=[B_on, B_xnorm, B_rsb], writes=[B_ot])
                dma("sp", MIXT_d[4 + p, :, t0:t0 + T], ot[:, 0:T], reads=[B_ot], writes=[B_MIXF[p][ti]])

            load_kt(0)
            load_pair(0)
            load_q(0)
            deferred = []
            for n, (p, ti) in enumerate(iters):
                t0, T = TILES[ti]
                KTp, B_KTp = KTs[p % 2], B_KTs[p % 2]
                Vp, B_Vp = Vps[p % 2], B_Vps[p % 2]
                qt, B_qt = qts[n % 2], B_qts[n % 2]
                bias, B_bias = biases[n % 2], B_biases[n % 2]
                kbs = kbs_of(ti)
                nkb = len(kbs)

                def emit_s(idx):
                    g, k0, KB = kbs[idx]
                    qa = max(0, k0 - t0)
                    for e_ in range(2):
                        rows = slice(e_ * 64, (e_ + 1) * 64)
                        ps, B_ps = SBK[e_][idx % 2]
                        op("pe", mm(ps[0:KB, qa:T], KTp[rows, k0:k0 + KB], qt[rows, qa:T]),
                           reads=[B_KTp, B_qt], writes=[B_ps])

                emit_s(0)
                if n + 1 < len(iters):
                    if iters[n + 1][0] != p:
                        load_pair(iters[n + 1][0])
                    load_q(n + 1)
                for idx in range(nkb):
                    g, k0, KB = kbs[idx]
                    qa = max(0, k0 - t0)
                    diag = k0 + KB > t0
                    if idx + 1 < nkb:
                        emit_s(idx + 1)
                    for e_ in range(2):
                        ps, B_ps = SBK[e_][idx % 2]
                        pt, B_pt = pts[2 * e_ + idx % 2], B_pts[2 * e_ + idx % 2]
                        po, B_po = PS[3 + e_], B_PS[3 + e_]
                        op("act", act(pt[0:KB, qa:T], ps[0:KB, qa:T], AF.Exp, bias=bias[0:KB, g, e_:e_ + 1], scale=0.125),
                           reads=[B_ps, B_bias], writes=[B_pt])
                        if diag:
                            op("dve", lambda e: e.tensor_tensor(out=pt[0:KB, qa:qa + KB], in0=pt[0:KB, qa:qa + KB],
                                                                in1=TRIB[0:KB, 0:KB], op=ALU.mult),
                               reads=[B_pt, B_const], writes=[B_pt])
                        op("pe", mm(po[:, qa:T], Vp[0:KB, g, e_ * 64:e_ * 64 + 128], pt[0:KB, qa:T], idx == 0, idx == nkb - 1),
                           reads=[B_Vp, B_pt], writes=[B_po])
                    if deferred and idx == min(7, nkb - 1):
                        epilogue(deferred.pop(0))
                for e_ in range(2):
                    op("dve", lambda e, e_=e_: e.tensor_copy(out=pcs[e_][:, 0:T], in_=PS[3 + e_][:, 0:T]),
                       reads=[B_PS[3 + e_]], writes=[B_pcs[e_]])
                epilogue_dve(n)
                deferred.append(n)
                if n + 1 < len(iters) and iters[n + 1][0] != p:
                    load_kt(iters[n + 1][0])
            while deferred:
                epilogue(deferred.pop(0))
            tk.barrier()
        if stop_after == "B":
            lo.close()
            lw.close()
            break

        with contextlib.ExitStack() as pc, nc.named_scope(f"C{layer}"):
            def sbc(name, shape, dt):
                return sb(name, shape, dt, pc)
            mixts = [sbc(f"mixt{i}", [128, 8, 512], BF16) for i in range(2)]
            B_mixts = [Buf(f"mixt{i}") for i in range(2)]
            hbs = [sbc(f"hbc{i}", [128, D], F32) for i in range(2)]
            B_hbs = [Buf(f"hbc{i}") for i in range(2)]
            for ti, (t0, T) in enumerate(TILES):
                mt, B_mt = mixts[ti % 2], B_mixts[ti % 2]
                dma("sp", mt[:, :, 0:T], MIXT_d[:, :, t0:t0 + T].rearrange("c p t -> p c t"),
                    reads=[B_MIXG[ti]] + [B_MIXF[p][ti] for p in range(4)], writes=[B_mt])
                for bi, (b0, o, B) in enumerate(blocks_of(t0, T)):
                    g = gblk(b0)
                    hb, B_hb = hbs[g % 2], B_hbs[g % 2]
                    dma("sp", hb[0:B, :], h_rows(layer, b0, B, True), reads=[B_H[g]], writes=[B_hb])
                    for half in range(2):
                        ps, B_ps = PS[half + 2 * (g % 2)], B_PS[half + 2 * (g % 2)]
                        for c in range(8):
                            op("pe", mm(ps[0:B, :], mt[:, c, o:o + B], R_out[:, c, half * 512:(half + 1) * 512], c == 0, c == 7),
                               reads=[B_mt, B_Rout], writes=[B_ps])
                        op("dve", lambda e, half=half, ps=ps: e.tensor_tensor(
                            out=hb[0:B, half * 512:(half + 1) * 512], in0=ps[0:B, :], in1=hb[0:B, half * 512:(half + 1) * 512], op=ALU.add),
                           reads=[B_ps, B_hb], writes=[B_hb], partial=True)
                    dma("pool", H_d[b0:b0 + B, :], hb[0:B, :], reads=[B_hb], writes=[B_H[g]])
            tk.barrier()
        lo.close()
        if stop_after == "C":
            lw.close()
            break

        with contextlib.ExitStack() as pd, nc.named_scope(f"D{layer}"):
            def sbd(name, shape, dt):
                return sb(name, shape, dt, pd)
            g2bc = sbd("g2bc", [128, D], F32); B_g2 = Buf("g2bc")
            convw = sbd("convw", [128, 2 * NJ, 3], F32)
            convb = sbd("convb", [128, 2 * NJ], F32)
            B_cv = Buf("conv")
            dma("sp", g2bc[:, :], fnorm_d[layer], writes=[B_g2])
            dma("sp", convw[:, :, :], convw_d[layer], writes=[B_cv])
            dma("sp", convb[:, :], convb_d[layer], writes=[B_cv])
            if last:
                fbc = sbd("fbc", [128, D], F32); B_fbc = Buf("fbc")
                dma("sp", fbc[:, :], final_d, writes=[B_fbc])
            halo = sbd("halo", [128, 2 * NJ, 2], F32); B_halo = Buf("halo")
            op("pool", lambda e: e.memset(halo[:, :, :], 0.0), writes=[B_halo])
            hbs = [sbd(f"hbd{i}", [128, D], F32) for i in range(2)]
            B_hbs = [Buf(f"hbd{i}") for i in range(2)]
            if last:
                hbx = sbd("hbx", [128, D], F32)
                B_hbx = Buf("hbx")
                hbn, B_hbn = [hbx, hbx], [B_hbx, B_hbx]
            else:
                hbn = [sbd(f"hbn{i}", [128, D], F32) for i in range(2)]
                B_hbn = [Buf(f"hbn{i}") for i in range(2)]
            ws = dict(xn=sbd("xnd", [128, D], BF16), B_xn=Buf("xnd"), st=sbd("std", [128, 4], F32), B_st=Buf("std"))
            xnT = sbd("xnTd", [128, 8, 512], BF16); B_xnT = Buf("xnTd")
            hbufs = [sbd(f"hbuf{i}", [128, 514], F32) for i in range(4)]
            B_hbufs = [Buf(f"hbuf{i}") for i in range(4)]
            B_hhalo = [Buf(f"hhalo{i}") for i in range(4)]
            t1s = [sbd(f"t1_{i}", [128, 512], F32) for i in range(6)]
            B_t1s = [Buf(f"t1_{i}") for i in range(6)]
            actT = sbd("actT", [128, NJ, 512], BF16); B_actT = Buf("actT")
            hcnt = 0
            for ti, (t0, T) in enumerate(TILES):
                blks = blocks_of(t0, T)

                def norm_block_d(tj, bj):
                    tt0, TT = TILES[tj]
                    b0_, o_, B_ = blocks_of(tt0, TT)[bj]
                    g_ = gblk(b0_)
                    hb, B_hb = hbn[g_ % 2], B_hbn[g_ % 2]
                    dma("sp", hb[0:B_, :], H_d[b0_:b0_ + B_, :], reads=[B_H[g_]], writes=[B_hb])
                    norm_part1(ws, hb, B_hb, g2bc, B_g2, B_)
                    return lambda: norm_part2(ws, xnT, B_xnT, o_, B_)

                if ti == 0:
                    norm_block_d(0, 0)()
                nq = []
                part2 = []
                if ti + 1 < NT:
                    for bj in range(len(blocks_of(*TILES[ti + 1]))):
                        nq.append(lambda bj=bj: norm_block_d(ti + 1, bj))
                pend = []
                silu_done = set()

                def fin(item):
                    pj, ((tu, B_tu), (tg, B_tg)) = item
                    if pj not in silu_done:
                        op("act", act(tg[:, 0:T], tg[:, 0:T], AF.Silu), reads=[B_tg], writes=[B_tg])
                    op("dve", lambda e: e.tensor_tensor(out=actT[:, pj, 0:T], in0=tu[:, 0:T], in1=tg[:, 0:T], op=ALU.mult),
                       reads=[B_tu, B_tg], writes=[B_actT], partial=True)

                for j in range(NJ):
                    pss = []
                    for br in range(2):
                        ps, B_ps = PS[2 + 2 * (j % 2) + br], B_PS[2 + 2 * (j % 2) + br]
                        col0 = br * DFF + j * 128
                        for k in range(8):
                            op("pe", mm(ps[:, 0:T], wup_v[:, k, col0:col0 + 128], xnT[:, k, 0:T], k == 0, k == 7),
                               reads=[B_Rup, B_xnT], writes=[B_ps])
                        pss.append((ps, B_ps))
                    t3 = []
                    for br in range(2):
                        ps, B_ps = pss[br]
                        ci = br * NJ + j
                        bi_ = 2 * (j % 2) + br
                        hbuf, B_hbuf = hbufs[bi_], B_hbufs[bi_]
                        t1, B_t1 = t1s[2 * (j % 3) + br], B_t1s[2 * (j % 3) + br]
                        op("pool", lambda e, ci=ci, hbuf=hbuf: e.tensor_copy(out=hbuf[:, 0:2], in_=halo[:, ci, :]),
                           reads=[B_halo], writes=[B_hhalo[bi_]])
                        op("act", act(hbuf[:, 2:2 + T], ps[:, 0:T], AF.Copy), reads=[B_ps], writes=[B_hbuf])
                        op("act", act(t1[:, 0:T], ps[:, 0:T], AF.Identity, bias=convb[:, ci:ci + 1], scale=convw[:, ci, 2:3]),
                           reads=[B_ps, B_cv], writes=[B_t1])
                        op("pool", lambda e, ci=ci, hbuf=hbuf: e.tensor_copy(out=halo[:, ci, :], in_=hbuf[:, T:T + 2]),
                           reads=[B_hbuf], writes=[B_halo], partial=True)
                        t3.append((t1, B_t1))
                    if pend:
                        pj, ((ptu, B_ptu), (ptg, B_ptg)) = pend[0]
                        op("act", act(ptg[:, 0:T], ptg[:, 0:T], AF.Silu), reads=[B_ptg], writes=[B_ptg])
                        silu_done.add(pj)
                    for br in range(2):
                        ci = br * NJ + j
                        bi_ = 2 * (j % 2) + br
                        hbuf, B_hbuf = hbufs[bi_], B_hbufs[bi_]
                        t1, B_t1 = t1s[2 * (j % 3) + br], B_t1s[2 * (j % 3) + br]
                        op("dve", lambda e, ci=ci, hbuf=hbuf, t1=t1: e.scalar_tensor_tensor(
                            out=t1[:, 0:T], in0=hbuf[:, 1:1 + T], scalar=convw[:, ci, 1:2], in1=t1[:, 0:T], op0=ALU.mult, op1=ALU.add),
                           reads=[B_hbuf, B_hhalo[bi_], B_cv, B_t1], writes=[B_t1])
                        op("dve", lambda e, ci=ci, hbuf=hbuf, t1=t1: e.scalar_tensor_tensor(
                            out=t1[:, 0:T], in0=hbuf[:, 0:T], scalar=convw[:, ci, 0:1], in1=t1[:, 0:T], op0=ALU.mult, op1=ALU.add),
                           reads=[B_hbuf, B_hhalo[bi_], B_cv, B_t1], writes=[B_t1])
                    pend.append((j, t3))
                    if len(pend) > 1:
                        fin(pend.pop(0))
                while pend:
                    fin(pend.pop(0))
                for bi, (b0, o, B) in enumerate(blks):
                    g = gblk(b0)
                    hb, B_hb = hbs[hcnt % 2], B_hbs[hcnt % 2]
                    hcnt += 1
                    dma("sp", hb[0:B, :], H_d[b0:b0 + B, :], reads=[B_H[g]], writes=[B_hb])
                    if nq:
                        part2.append(nq.pop(0)())
                    for half in range(2):
                        ps, B_ps = PS[half], B_PS[half]
                        for j in range(NJ):
                            op("pe", mm(ps[0:B, :], actT[:, j, o:o + B], R_dn[:, j, half * 512:(half + 1) * 512], j == 0, j == NJ - 1),
                               reads=[B_actT, B_Rdn], writes=[B_ps])
                        op("dve", lambda e, half=half, ps=ps, hb=hb: e.tensor_tensor(
                            out=hb[0:B, half * 512:(half + 1) * 512], in0=ps[0:B, :], in1=hb[0:B, half * 512:(half + 1) * 512], op=ALU.add),
                           reads=[B_ps, B_hb], writes=[B_hb], partial=True)
                    if not last:
                        dma("pool", H_d[b0:b0 + B, :], hb[0:B, :], reads=[B_hb], writes=[B_H[g]])
                    elif ti > 0:
                        st, B_st = ws["st"], ws["B_st"]
                        xn, B_xn = t1s[0][:, :].bitcast(BF16), B_t1s[0]
                        op("act", act(xn[0:B, :], hb[0:B, :], AF.Square, accum_out=st[0:B, 0:1]), reads=[B_hb], writes=[B_xn, B_st])
                        op("act", act(st[0:B, 1:2], st[0:B, 0:1], AF.Ln, bias=EPS, scale=1.0 / D), reads=[B_st], writes=[B_st])
                        op("act", act(st[0:B, 2:3], st[0:B, 1:2], AF.Exp, scale=-0.5), reads=[B_st], writes=[B_st])
                        op("dve", lambda e, hb=hb: e.scalar_tensor_tensor(out=hb[0:B, :], in0=hb[0:B, :], scalar=st[0:B, 2:3],
                                                                        in1=fbc[0:B, :], op0=ALU.mult, op1=ALU.mult),
                           reads=[B_hb, B_st, B_fbc], writes=[B_hb])
                        dma("pool", y_d[b0 - NMETA:b0 - NMETA + B, :], hb[0:B, :], reads=[B_hb], writes=[B_H[g]])
                    if part2:
                        part2.pop(0)()
                while nq or part2:
                    if part2:
                        part2.pop(0)()
                    if nq:
                        part2.append(nq.pop(0)())
            tk.barrier()
        lw.close()
    tk.final_wait()
    if tk.limit is not None:
        print("K_LIMIT", tk.limit, "total ops", tk.nops, "last emitted:", [x for x in tk.log if x[0] in (tk.limit - 1, tk.limit, tk.limit + 1)])


def _consts():
    s = np.arange(128)[:, None]
    t = np.arange(128)[None, :]
    tri = (s <= t).astype(np.float32)
    su = ((s > t) & (s // 64 == t // 64)).astype(np.float32)
    e127 = np.zeros((128, 128), np.float32); e127[127, :] = 1.0
    e15 = np.zeros((128, 128), np.float32); e15[15, :] = 1.0
    ch = (s // 64 == np.arange(8)[None, :]).astype(np.float32)
    cf32 = np.concatenate([tri, su, e127, e15, ch], axis=1)
    ident = np.eye(128, dtype=np.float32)
    ones128 = np.full((128, 128), 1.0 / 128, np.float32)
    wa = np.zeros((128, 128), np.float32); wa[0:64, 0:64] = 1.0 / 64; wa[64:128, 64:128] = 1.0 / 64
    wb = np.zeros((128, 128), np.float32); wb[0:64, 64:128] = EPS / 64; wb[64:128, 64:128] = 1.0 / 64
    cbf = np.concatenate([ident, tri, ones128, wa, wb], axis=1).astype(ml_dtypes.bfloat16)
    return np.ascontiguousarray(cf32), np.ascontiguousarray(cbf)


def make_in_maps(inputs, cores=range(8)):
    f = lambda a: np.ascontiguousarray(np.asarray(a, dtype=np.float32))
    x = f(inputs["x"])
    bc = lambda a: np.ascontiguousarray(np.broadcast_to(f(a)[:, None, :], (a.shape[0], 128, a.shape[1])))
    cf32, cbf = _consts()
    conv_w = f(inputs["conv_w"])
    shared = {
        "meta": f(inputs["meta_tokens"]),
        "w_in": f(inputs["w_in"]), "w_out": f(inputs["w_out"]), "w_up": f(inputs["w_up"]), "w_down": f(inputs["w_down"]),
        "anorm_bc": bc(np.asarray(inputs["attn_norm"])), "fnorm_bc": bc(np.asarray(inputs["ffn_norm"])),
        "final_bc": np.ascontiguousarray(np.broadcast_to(f(inputs["final_norm"])[None, :], (128, D))),
        "wau": f(inputs["w_alpha_up"]),
        "balpha_bc": bc(np.asarray(inputs["b_alpha"])), "bforget_bc": bc(np.asarray(inputs["b_forget"])),
        "gnorm_fm": np.ascontiguousarray(f(inputs["gla_norm"]).reshape(DEPTH, 4, 128).transpose(0, 2, 1)),
        "xnorm_fm": np.ascontiguousarray(f(inputs["fox_norm"]).reshape(DEPTH, 4, 128).transpose(0, 2, 1)),
        "convw_fm": np.ascontiguousarray(conv_w.reshape(DEPTH, 3, 2 * NJ, 128).transpose(0, 3, 2, 1)),
        "convb_fm": np.ascontiguousarray(f(inputs["conv_b"]).reshape(DEPTH, 2 * NJ, 128).transpose(0, 2, 1)),
        "cf32": cf32, "cbf": cbf,
    }
    return [dict(shared, x=np.ascontiguousarray(x[c])) for c in cores]


_NC_CACHE = {}


def kernel(**inputs):
    if "nc" not in _NC_CACHE:
        _NC_CACHE["nc"] = build_nc()
    nc = _NC_CACHE["nc"]
    in_maps = make_in_maps(inputs)
    res = run_bass_kernel_spmd(nc, in_maps, core_ids=list(range(8)))
    return np.stack([np.asarray(r["y"], dtype=np.float32) for r in res.results], axis=0)
```
